# Optimizing a Trainium2 kernel written in Bass

```python
import math
import jax, jax.numpy as jnp
from jax import lax
import numpy as np

D_MODEL = 1024
BATCH = 16
SEQ = 256
DEPTH = 2
DEC_BATCH = 4
DEC_SEQ = 1024
PAST_LEN = 512

GRID_W = 64
MIX_W = D_MODEL // 2
N_RET_HEADS = 4
RET_DK = MIX_W // N_RET_HEADS
RET_DV = MIX_W // N_RET_HEADS
RET_CHUNK = 128
SSM_CH = MIX_W
SSM_GROUP = 16
SSM_GROUPS = SSM_CH // SSM_GROUP
SSM_STATE = 64
NA_HEADS = 8
NA_HEAD_DIM = MIX_W // NA_HEADS
NA_KR = 8
NA_KW = 16
N_BRANCH = 3
D_FF = ((8 * D_MODEL // 3 + 127) // 128) * 128
ROPE_BASE = 10000.0
LN_EPS = 1e-5
NEG_INF = -1e30
DEEPNORM_ALPHA = (2 * DEPTH) ** 0.25
DEEPNORM_BETA = (8 * DEPTH) ** -0.25
IN_SPLITS = (MIX_W, MIX_W, MIX_W, MIX_W, SSM_CH, MIX_W, MIX_W, MIX_W, N_BRANCH * D_MODEL)
IN_COLS = sum(IN_SPLITS)

kernel_name = 'hybrid_diffusion_retention_s5_natten_step'


def _layer_norm(x, g, b):
    xf = x.astype(jnp.float32)
    mu = jnp.mean(xf, -1, keepdims=True)
    var = jnp.mean(jnp.square(xf - mu), -1, keepdims=True)
    return ((xf - mu) * lax.rsqrt(var + LN_EPS) * g.astype(jnp.float32) + b.astype(jnp.float32)).astype(x.dtype)


def _head_norm(o):
    mu = jnp.mean(o, -1, keepdims=True)
    var = jnp.mean(jnp.square(o - mu), -1, keepdims=True)
    return (o - mu) * lax.rsqrt(var + LN_EPS)


def _ada(cond, w_ada, b_ada):
    p = jax.nn.silu(cond) @ w_ada + b_ada
    return jnp.split(p[:, None, :], 6, axis=-1)


def _split_in(h, w_in):
    z = h @ w_in
    cuts = [int(o) for o in np.cumsum(IN_SPLITS)[:-1]]
    return jnp.split(z, cuts, axis=-1)


def _axial_rope(x):
    B, L, H, Dh = x.shape
    pos = jnp.arange(L)
    row = (pos // GRID_W).astype(jnp.float32)
    col = (pos % GRID_W).astype(jnp.float32)
    half = Dh // 2
    quarter = half // 2
    inv_freq = ROPE_BASE ** (-jnp.arange(quarter, dtype=jnp.float32) / quarter)

    def rot(xa, p):
        ang = p[:, None] * inv_freq[None, :]
        cos = jnp.cos(ang)[None, :, None, :]
        sin = jnp.sin(ang)[None, :, None, :]
        x1, x2 = xa[..., :quarter], xa[..., quarter:]
        return jnp.concatenate([x1 * cos - x2 * sin, x1 * sin + x2 * cos], -1)

    xf = x.astype(jnp.float32)
    return jnp.concatenate([rot(xf[..., :half], row), rot(xf[..., half:], col)], -1)


def _retention_scan(q, k, v, s0, log_gamma):
    B, L, H, DK = q.shape
    DV = v.shape[-1]
    C = RET_CHUNK
    n = L // C
    idx = jnp.arange(C, dtype=jnp.float32)
    diff = idx[:, None] - idx[None, :]
    inner_decay = jnp.where(diff >= 0, jnp.exp(log_gamma[:, None, None] * jnp.maximum(diff, 0.0)), 0.0)
    q_decay = jnp.exp(log_gamma[:, None] * (idx + 1.0)).T[None, :, :, None]
    k_decay = jnp.exp(log_gamma[:, None] * (C - 1.0 - idx)).T[None, :, :, None]
    chunk_decay = jnp.exp(log_gamma * C)[None, :, None, None]
    qc = q.reshape(B, n, C, H, DK).swapaxes(0, 1)
    kc = k.reshape(B, n, C, H, DK).swapaxes(0, 1)
    vc = v.reshape(B, n, C, H, DV).swapaxes(0, 1)

    def step(s, inp):
        qi, ki, vi = inp
        att = jnp.einsum('bqhd,bkhd->bhqk', qi, ki) * inner_decay[None]
        o = jnp.einsum('bhqk,bkhe->bqhe', att, vi)
        o = o + jnp.einsum('bqhd,bhde->bqhe', qi, s) * q_decay
        s = s * chunk_decay + jnp.einsum('bkhd,bkhe->bhde', ki * k_decay, vi)
        return s, o

    s, o = lax.scan(step, s0, (qc, kc, vc))
    return o.swapaxes(0, 1).reshape(B, L, H, DV), s


def _retention(q, k, v, g, s0, decay_logit):
    B, L = q.shape[:2]
    log_gamma = jax.nn.log_sigmoid(decay_logit.astype(jnp.float32))
    k = k * RET_DK ** -0.5
    s0 = s0.astype(jnp.float32)
    o_f, s_f = _retention_scan(q, k, v, s0[:, 0], log_gamma[0])
    o_b, s_b = _retention_scan(q[:, ::-1], k[:, ::-1], v[:, ::-1], s0[:, 1], log_gamma[1])
    o = _head_norm(o_f + o_b[:, ::-1]).reshape(B, L, MIX_W)
    out = o * jax.nn.silu(g.astype(jnp.float32))
    return out.astype(g.dtype), jnp.stack([s_f, s_b], axis=1)


def _complex_scan(bu, a_bar, h0):
    bu = bu.at[:, 0].add(a_bar * h0)
    a = jnp.broadcast_to(a_bar, bu.shape)

    def combine(e1, e2):
        a1, b1 = e1
        a2, b2 = e2
        return a1 * a2, a2 * b1 + b2

    _, xs = lax.associative_scan(combine, (a, bu), axis=1)
    return xs


def _s5(u, h0, a_re, a_im, log_dt, b_re, b_im, c_re, c_im, d_skip, w_glu):
    f32 = jnp.float32
    B, L, _ = u.shape
    uf = u.astype(f32)
    ug = uf.reshape(B, L, SSM_GROUPS, SSM_GROUP).astype(jnp.complex64)
    lam = lax.complex(jnp.minimum(a_re.astype(f32), -1e-4), a_im.astype(f32))
    dt = jnp.exp(log_dt.astype(f32))[..., None]
    a_bar = jnp.exp(lam * dt)
    b = lax.complex(b_re.astype(f32), b_im.astype(f32))
    b_bar = ((a_bar - 1.0) / lam)[..., None] * b[None]
    c = lax.complex(c_re.astype(f32), c_im.astype(f32))
    h0c = lax.complex(h0[..., 0].astype(f32), h0[..., 1].astype(f32))
    y = d_skip.astype(f32) * uf
    finals = []
    for di in range(2):
        ud = ug if di == 0 else ug[:, ::-1]
        bu = jnp.einsum('blgc,gpc->blgp', ud, b_bar[di])
        xs = _complex_scan(bu, a_bar[di], h0c[:, di])
        yd = jnp.einsum('blgp,gcp->blgc', xs, c[di]).real
        if di == 1:
            yd = yd[:, ::-1]
        y = y + yd.reshape(B, L, SSM_CH)
        finals.append(xs[:, -1])
    y = jax.nn.gelu(y)
    y = y * jax.nn.sigmoid(y @ w_glu.astype(f32))
    fin = jnp.stack(finals, axis=1)
    return y.astype(u.dtype), jnp.stack([fin.real, fin.imag], axis=-1)


def _context_attention(q, k, v):
    s = jnp.einsum('bqhd,bkhd->bhqk', q, k).astype(jnp.float32) * NA_HEAD_DIM ** -0.5
    p = jax.nn.softmax(s, axis=-1).astype(v.dtype)
    return jnp.einsum('bhqk,bkhd->bqhd', p, v)


def _neighbourhood_attention(q, k, v, k_ctx, v_ctx, rpb):
    B, L, H, Dh = q.shape
    rows = L // GRID_W
    kr = min(NA_KR, rows)
    ncb = GRID_W // NA_KW
    span = 2 * NA_KW
    r = jnp.arange(rows)
    key_rows = jnp.clip(r - kr // 2, 0, rows - kr)[:, None] + jnp.arange(kr)[None, :]
    qcol = jnp.arange(GRID_W).reshape(ncb, NA_KW)
    win_start = jnp.clip(qcol - NA_KW // 2, 0, GRID_W - NA_KW)
    blk_start = jnp.clip(jnp.arange(ncb) * NA_KW - NA_KW // 2, 0, GRID_W - span)
    key_cols = blk_start[:, None] + jnp.arange(span)[None, :]
    ridx = key_rows[:, None, :, None]
    cidx = key_cols[None, :, None, :]
    kb = k.reshape(B, rows, GRID_W, H, Dh)[:, ridx, cidx]
    vb = v.reshape(B, rows, GRID_W, H, Dh)[:, ridx, cidx]
    qb = q.reshape(B, rows, ncb, NA_KW, H, Dh)
    scale = Dh ** -0.5
    s_loc = jnp.einsum('brjqhd,brjkshd->bhrjqks', qb, kb).astype(jnp.float32) * scale
    kc = key_cols[:, None, :]
    valid = (kc >= win_start[:, :, None]) & (kc < win_start[:, :, None] + NA_KW)
    roff = key_rows - r[:, None] + NA_KR - 1
    coff = jnp.clip(kc - qcol[:, :, None] + NA_KW - 1, 0, 2 * NA_KW - 2)
    bias = rpb.astype(jnp.float32)[:, roff[:, None, None, :, None], coff[None, :, :, None, :]]
    s_loc = jnp.where(valid[None, None, None, :, :, None, :], s_loc + bias[None], NEG_INF)
    n_loc = kr * span
    s_loc = s_loc.reshape(B, H, rows, ncb, NA_KW, n_loc)
    s_ctx = jnp.einsum('brjqhd,bchd->bhrjqc', qb, k_ctx).astype(jnp.float32) * scale
    p = jax.nn.softmax(jnp.concatenate([s_loc, s_ctx], axis=-1), axis=-1).astype(v.dtype)
    p_loc = p[..., :n_loc].reshape(B, H, rows, ncb, NA_KW, kr, span)
    out = (jnp.einsum('bhrjqks,brjkshd->brjqhd', p_loc, vb)
           + jnp.einsum('bhrjqc,bchd->brjqhd', p[..., n_loc:], v_ctx))
    return out.reshape(B, L, H * Dh)


def _merge_branches(r_out, s_out, n_out, gates, w_branch, w_o):
    g = jax.nn.sigmoid(gates.astype(jnp.float32)).astype(r_out.dtype)
    ga, gb, gc = jnp.split(g, 3, axis=-1)
    merged = ga * (r_out @ w_branch[0]) + gb * (s_out @ w_branch[1]) + gc * (n_out @ w_branch[2])
    return merged @ w_o


def _conv_ffn(h, w_up, conv_w, conv_b, w_down):
    z = h @ w_up
    zp = jnp.pad(z, ((0, 0), (1, 1), (0, 0)))
    z = zp[:, :-2] * conv_w[0] + zp[:, 1:-1] * conv_w[1] + zp[:, 2:] * conv_w[2] + conv_b
    a, b = jnp.split(z, 2, axis=-1)
    return (jax.nn.gelu(a) * b) @ w_down


def _s5_call(su, h0, lw):
    return _s5(su, h0, lw['ssm_a_re'], lw['ssm_a_im'], lw['ssm_log_dt'], lw['ssm_b_re'], lw['ssm_b_im'],
               lw['ssm_c_re'], lw['ssm_c_im'], lw['ssm_d'], lw['ssm_w_glu'])


def _context_mixer(h, lw):
    B, L, _ = h.shape
    f32 = jnp.float32
    rq, rk, rv, rg, su, nq, nk, nv, gates = _split_in(h, lw['w_in'])
    rq = rq.reshape(B, L, N_RET_HEADS, RET_DK).astype(f32)
    rk = rk.reshape(B, L, N_RET_HEADS, RET_DK).astype(f32)
    rv = rv.reshape(B, L, N_RET_HEADS, RET_DV).astype(f32)
    zero_ret = jnp.zeros((B, 2, N_RET_HEADS, RET_DK, RET_DV), f32)
    r_out, ret_state = _retention(rq, rk, rv, rg, zero_ret, lw['ret_decay'])
    zero_ssm = jnp.zeros((B, 2, SSM_GROUPS, SSM_STATE, 2), f32)
    s_out, ssm_state = _s5_call(su, zero_ssm, lw)
    nq = nq.reshape(B, L, NA_HEADS, NA_HEAD_DIM)
    nk = nk.reshape(B, L, NA_HEADS, NA_HEAD_DIM)
    nv = nv.reshape(B, L, NA_HEADS, NA_HEAD_DIM)
    n_out = _context_attention(nq, nk, nv).reshape(B, L, MIX_W)
    m = _merge_branches(r_out, s_out, n_out, gates, lw['w_branch'], lw['w_o'])
    return m, (ret_state, ssm_state, nk, nv)


def _latent_mixer(h, lw, s_ret, s_ssm, k_ctx, v_ctx):
    B, L, _ = h.shape
    rq, rk, rv, rg, su, nq, nk, nv, gates = _split_in(h, lw['w_in'])
    rq = _axial_rope(rq.reshape(B, L, N_RET_HEADS, RET_DK))
    rk = _axial_rope(rk.reshape(B, L, N_RET_HEADS, RET_DK))
    rv = rv.reshape(B, L, N_RET_HEADS, RET_DV).astype(jnp.float32)
    r_out, _ = _retention(rq, rk, rv, rg, s_ret, lw['ret_decay'])
    s_out, _ = _s5_call(su, s_ssm, lw)
    nq = nq.reshape(B, L, NA_HEADS, NA_HEAD_DIM)
    nk = nk.reshape(B, L, NA_HEADS, NA_HEAD_DIM)
    nv = nv.reshape(B, L, NA_HEADS, NA_HEAD_DIM)
    n_out = _neighbourhood_attention(nq, nk, nv, k_ctx.astype(nq.dtype), v_ctx.astype(nv.dtype), lw['na_rpb'])
    m = _merge_branches(r_out, s_out, n_out, gates, lw['w_branch'], lw['w_o'])
    return m, None


def _trunk_layer(x, cond, lw, mix):
    sh1, sc1, g1, sh2, sc2, g2 = _ada(cond, lw['w_ada'], lw['b_ada'])
    m, extras = mix(x * (1.0 + sc1) + sh1)
    x = _layer_norm(DEEPNORM_ALPHA * x + g1 * m, lw['ln1_g'], lw['ln1_b'])
    f = _conv_ffn(x * (1.0 + sc2) + sh2, lw['w_up'], lw['conv_w'], lw['conv_b'], lw['w_down'])
    x = _layer_norm(DEEPNORM_ALPHA * x + g2 * f, lw['ln2_g'], lw['ln2_b'])
    return x, extras


def setup_inputs(seed: int = 0) -> dict:
    key = jax.random.key(seed)
    ks = jax.random.split(key, 32)
    f32 = jnp.float32

    def nrm(k, shape, s):
        return jax.random.normal(k, shape, f32) * s

    gamma0 = 1.0 - 2.0 ** (-5.0 - jnp.arange(N_RET_HEADS, dtype=f32))
    ret_logit = jnp.log(gamma0) - jnp.log1p(-gamma0)
    return {
        'x_prompt': nrm(ks[0], (BATCH, SEQ, D_MODEL), 1.0),
        'x_sample': nrm(ks[1], (DEC_BATCH, DEC_SEQ, D_MODEL), 1.0),
        'state_ret': nrm(ks[2], (DEC_BATCH, DEPTH, 2, N_RET_HEADS, RET_DK, RET_DV), 0.5),
        'state_ssm': nrm(ks[3], (DEC_BATCH, DEPTH, 2, SSM_GROUPS, SSM_STATE, 2), 0.5),
        'cache_na_k': nrm(ks[4], (DEC_BATCH, DEPTH, PAST_LEN, NA_HEADS, NA_HEAD_DIM), 1.0),
        'cache_na_v': nrm(ks[5], (DEC_BATCH, DEPTH, PAST_LEN, NA_HEADS, NA_HEAD_DIM), 1.0),
        'c': nrm(ks[6], (DEC_BATCH, D_MODEL), 1.0),
        'c_ctx': nrm(ks[7], (D_MODEL,), 1.0),
        'w_ada': nrm(ks[8], (DEPTH, D_MODEL, 6 * D_MODEL), 0.5 * D_MODEL ** -0.5),
        'b_ada': nrm(ks[9], (DEPTH, 6 * D_MODEL), 0.02),
        'w_in': nrm(ks[10], (DEPTH, D_MODEL, IN_COLS), D_MODEL ** -0.5),
        'ret_decay': ret_logit + nrm(ks[11], (DEPTH, 2, N_RET_HEADS), 0.05),
        'ssm_a_re': -0.5 + nrm(ks[12], (DEPTH, 2, SSM_GROUPS, SSM_STATE), 0.01),
        'ssm_a_im': jnp.pi * jnp.arange(SSM_STATE, dtype=f32) + nrm(ks[13], (DEPTH, 2, SSM_GROUPS, SSM_STATE), 0.01),
        'ssm_log_dt': jax.random.uniform(ks[14], (DEPTH, 2, SSM_GROUPS), f32, math.log(1e-3), math.log(1e-1)),
        'ssm_b_re': nrm(ks[15], (DEPTH, SSM_GROUPS, SSM_STATE, SSM_GROUP), (2 * SSM_GROUP) ** -0.5),
        'ssm_b_im': nrm(ks[16], (DEPTH, SSM_GROUPS, SSM_STATE, SSM_GROUP), (2 * SSM_GROUP) ** -0.5),
        'ssm_c_re': nrm(ks[17], (DEPTH, 2, SSM_GROUPS, SSM_GROUP, SSM_STATE), (2 * SSM_STATE) ** -0.5),
        'ssm_c_im': nrm(ks[18], (DEPTH, 2, SSM_GROUPS, SSM_GROUP, SSM_STATE), (2 * SSM_STATE) ** -0.5),
        'ssm_d': nrm(ks[19], (DEPTH, SSM_CH), 1.0),
        'ssm_w_glu': nrm(ks[20], (DEPTH, SSM_CH, SSM_CH), SSM_CH ** -0.5),
        'na_rpb': nrm(ks[21], (DEPTH, NA_HEADS, 2 * NA_KR - 1, 2 * NA_KW - 1), 0.1),
        'w_branch': nrm(ks[22], (DEPTH, N_BRANCH, MIX_W, D_MODEL), DEEPNORM_BETA * MIX_W ** -0.5),
        'w_o': nrm(ks[23], (DEPTH, D_MODEL, D_MODEL), DEEPNORM_BETA * D_MODEL ** -0.5),
        'ln1_g': 1.0 + nrm(ks[24], (DEPTH, D_MODEL), 0.02),
        'ln1_b': nrm(ks[25], (DEPTH, D_MODEL), 0.02),
        'w_up': nrm(ks[26], (DEPTH, D_MODEL, 2 * D_FF), D_MODEL ** -0.5),
        'conv_w': nrm(ks[27], (DEPTH, 3, 2 * D_FF), 3 ** -0.5),
        'conv_b': nrm(ks[28], (DEPTH, 2 * D_FF), 0.02),
        'w_down': nrm(ks[29], (DEPTH, D_FF, D_MODEL), DEEPNORM_BETA * D_FF ** -0.5),
        'ln2_g': 1.0 + nrm(ks[30], (DEPTH, D_MODEL), 0.02),
        'ln2_b': nrm(ks[31], (DEPTH, D_MODEL), 0.02),
    }


def reference(x_prompt, x_sample, state_ret, state_ssm, cache_na_k, cache_na_v, c, c_ctx,
              w_ada, b_ada, w_in, ret_decay, ssm_a_re, ssm_a_im, ssm_log_dt, ssm_b_re, ssm_b_im,
              ssm_c_re, ssm_c_im, ssm_d, ssm_w_glu, na_rpb, w_branch, w_o, ln1_g, ln1_b,
              w_up, conv_w, conv_b, w_down, ln2_g, ln2_b):
    cond_ctx = jnp.broadcast_to(c_ctx[None, :], (x_prompt.shape[0], D_MODEL))
    xp = x_prompt
    xs = x_sample
    ret_states, ssm_states, na_ks, na_vs = [], [], [], []
    for l in range(DEPTH):
        lw = dict(w_ada=w_ada[l], b_ada=b_ada[l], w_in=w_in[l], ret_decay=ret_decay[l],
                  ssm_a_re=ssm_a_re[l], ssm_a_im=ssm_a_im[l], ssm_log_dt=ssm_log_dt[l],
                  ssm_b_re=ssm_b_re[l], ssm_b_im=ssm_b_im[l], ssm_c_re=ssm_c_re[l], ssm_c_im=ssm_c_im[l],
                  ssm_d=ssm_d[l], ssm_w_glu=ssm_w_glu[l], na_rpb=na_rpb[l], w_branch=w_branch[l],
                  w_o=w_o[l], ln1_g=ln1_g[l], ln1_b=ln1_b[l], w_up=w_up[l], conv_w=conv_w[l],
                  conv_b=conv_b[l], w_down=w_down[l], ln2_g=ln2_g[l], ln2_b=ln2_b[l])
        xp, (s_ret, s_ssm, k_ctx, v_ctx) = _trunk_layer(xp, cond_ctx, lw, lambda h: _context_mixer(h, lw))
        ret_states.append(s_ret)
        ssm_states.append(s_ssm)
        na_ks.append(k_ctx)
        na_vs.append(v_ctx)
        xs, _ = _trunk_layer(xs, c, lw, lambda h: _latent_mixer(h, lw, state_ret[:, l], state_ssm[:, l],
                                                                cache_na_k[:, l], cache_na_v[:, l]))
    new_state_ret = jnp.stack(ret_states, axis=1).astype(x_prompt.dtype)
    new_state_ssm = jnp.stack(ssm_states, axis=1).astype(x_prompt.dtype)
    new_cache_na_k = jnp.stack(na_ks, axis=1).astype(x_prompt.dtype)
    new_cache_na_v = jnp.stack(na_vs, axis=1).astype(x_prompt.dtype)
    return (xp, xs, new_state_ret, new_state_ssm, new_cache_na_k, new_cache_na_v)
```

```python
from contextlib import ExitStack
import math
import numpy as np
import concourse.bass as bass
import concourse.mybir as mybir
from concourse.bass_utils import run_bass_kernel_spmd

F32 = mybir.dt.float32
BF16 = mybir.dt.bfloat16
AF = mybir.ActivationFunctionType
ALU = mybir.AluOpType
AX = mybir.AxisListType

ENGS = ("pe", "act", "dve", "pool", "sp")
N_DMA_SEM = 20
INLINE_WAIT = True

D = 1024
DEPTH = 2
MIX = 512
DFF = 2816
ALPHA = (2 * DEPTH) ** 0.25
LN_EPS = 1e-5
N_CORES = 8


class Op:
    __slots__ = ("eng", "fn", "deps", "dma", "idx", "sig", "sem", "val")

    def __init__(self, eng, fn, deps, dma, idx):
        self.eng, self.fn, self.deps, self.dma, self.idx = eng, fn, deps, dma, idx
        self.sig = False
        self.sem = None
        self.val = 0


class _Rec:
    def __init__(self):
        self.name = None

    def __getattr__(self, name):
        def call(*args, **kwargs):
            assert self.name is None, "one engine call per op"
            self.name, self.args, self.kwargs = name, args, kwargs
            return None
        return call


class Prog:
    def __init__(self, nc):
        self.nc = nc
        self.ops = []
        self.last_w = {}
        self.readers = {}
        self.stack = ExitStack()
        self.n_alloc = 0
        self.last_eng = {}

    def sb(self, shape, dtype, name=None):
        self.n_alloc += 1
        name = name or f"sb{self.n_alloc}"
        return self.stack.enter_context(self.nc.sbuf_tensor(name, list(shape), dtype))

    def ps(self, shape, dtype=F32, name=None):
        self.n_alloc += 1
        name = name or f"ps{self.n_alloc}"
        return self.stack.enter_context(self.nc.psum_tensor(name, list(shape), dtype))

    def add(self, eng, fn, reads=(), writes=(), dma=False, extra_deps=()):
        idx = len(self.ops)
        deps = set(extra_deps)
        for t in reads:
            w = self.last_w.get(t)
            if w is not None:
                deps.add(w)
            if t.startswith("ps"):
                for r in self.readers.get(t, ()):
                    if self.ops[r].eng != eng:
                        deps.add(r)
        for t in writes:
            w = self.last_w.get(t)
            if w is not None:
                deps.add(w)
            for r in self.readers.get(t, ()):
                deps.add(r)
        deps.discard(idx)
        if eng == "pe" and not dma:
            deps = {d for d in deps if not (self.ops[d].eng == "pe" and not self.ops[d].dma)}
        rec = _Rec()
        fn(rec)
        assert rec.name is not None
        op = Op(eng, (lambda e, rec=rec: getattr(e, rec.name)(*rec.args, **rec.kwargs)), sorted(deps), dma, idx)
        self.ops.append(op)
        for t in reads:
            self.readers.setdefault(t, []).append(idx)
        for t in writes:
            self.last_w[t] = idx
            self.readers[t] = []
        if not dma:
            self.last_eng[eng] = idx
        return idx

    def pe(self, fn, r=(), w=()):
        return self.add("pe", fn, r, w)

    def act(self, fn, r=(), w=()):
        return self.add("act", fn, r, w)

    def dve(self, fn, r=(), w=()):
        return self.add("dve", fn, r, w)

    def pool(self, fn, r=(), w=()):
        return self.add("pool", fn, r, w)

    def dma(self, q, fn, r=(), w=()):
        return self.add(q, fn, r, w, dma=True)

    def barrier(self):
        deps = list(self.last_eng.values())
        dmas = [o.idx for o in self.ops if o.dma]
        for e in ("pe", "act", "dve", "pool", "sp"):
            self.add(e, (lambda eng: eng.nop()), extra_deps=deps + dmas[-3 * N_DMA_SEM:])

    def emit(self):
        nc = self.nc
        ops = self.ops
        for op in ops:
            for d in op.deps:
                ops[d].sig = True
        st = self.stack
        esem = {e: st.enter_context(nc.semaphore(f"s_{e}")) for e in ENGS}
        dsem = {e: [st.enter_context(nc.semaphore(f"d_{e}{i}")) for i in range(N_DMA_SEM)]
                for e in ("sp", "pool", "act")}
        ecount = {e: 0 for e in ENGS}
        dcount = {e: [0] * N_DMA_SEM for e in dsem}
        dnum = {e: 0 for e in dsem}
        per_eng = {e: [] for e in ENGS}
        for op in ops:
            if op.dma:
                k = dnum[op.eng] % N_DMA_SEM
                dnum[op.eng] += 1
                dcount[op.eng][k] += 16
                op.sem, op.val = dsem[op.eng][k], dcount[op.eng][k]
            elif op.sig:
                ecount[op.eng] += 1
                op.sem, op.val = esem[op.eng], ecount[op.eng]
            per_eng[op.eng].append(op)
        final = {}
        for e in dsem:
            for k in range(N_DMA_SEM):
                if dcount[e][k]:
                    final[id(dsem[e][k])] = (dsem[e][k], dcount[e][k])
        for e in ENGS:
            if ecount[e]:
                final[id(esem[e])] = (esem[e], ecount[e])

        def run(eng_name, e):
            waited = {}
            for op in per_eng[eng_name]:
                need = {}
                for d in op.deps:
                    p = ops[d]
                    key = id(p.sem)
                    if waited.get(key, 0) < p.val and need.get(key, (None, 0))[1] < p.val:
                        need[key] = (p.sem, p.val)
                if op.dma and op.val > 16:
                    key = id(op.sem)
                    if waited.get(key, 0) < op.val - 16 and need.get(key, (None, 0))[1] < op.val - 16:
                        need[key] = (op.sem, op.val - 16)
                need = list(need.items())
                inline = None
                if INLINE_WAIT and need and not op.dma:
                    inline = need.pop()
                for key, (sm, vl) in need:
                    e.wait_ge(sm, vl)
                    waited[key] = vl
                ins = op.fn(e)
                if inline is not None:
                    key, (sm, vl) = inline
                    ins._wait_ge(sm, vl)
                    waited[key] = vl
                if op.dma:
                    ins.then_inc(op.sem, 16)
                elif op.sig:
                    ins.then_inc(op.sem, 1)
            if eng_name == "sp":
                for key, (s, v) in final.items():
                    if waited.get(key, 0) < v:
                        e.wait_ge(s, v)

        with nc.Block() as block:
            @block.tensor
            def _(e):
                run("pe", e)

            @block.scalar
            def _(e):
                run("act", e)

            @block.vector
            def _(e):
                run("dve", e)

            @block.gpsimd
            def _(e):
                run("pool", e)

            @block.sync
            def _(e):
                run("sp", e)
        self.stats = {e: len(per_eng[e]) for e in ENGS}
        self.stack.close()


def _consts():
    c = {}
    c["ident"] = np.eye(128, dtype=np.float32)
    k = np.arange(128, dtype=np.float32)
    dq = k[None, :] - k[:, None]
    c["e1"] = np.maximum(dq, 0.0).astype(np.float32)
    c["e2"] = np.maximum(-dq, 0.0).astype(np.float32)
    c["m1"] = (dq >= 0).astype(np.float32)
    c["m2"] = (dq <= 0).astype(np.float32)
    pidx = np.zeros((128, 4), np.float32)
    pidx[:, 0] = k + 1.0
    pidx[:, 1] = 128.0 - k
    pidx[:, 2] = 127.0 - k
    pidx[:, 3] = k
    c["pidx"] = pidx
    pos = np.arange(1024)
    row = (pos // 64).astype(np.float32)
    col = (pos % 64).astype(np.float32)
    inv_freq = (10000.0 ** (-np.arange(32, dtype=np.float32) / 32)).astype(np.float32)
    cosT = np.zeros((128, 1024), np.float32)
    sinT = np.zeros((128, 1024), np.float32)
    for d in range(128):
        p = row if d < 64 else col
        f = inv_freq[d % 32]
        ang = (p * f).astype(np.float32)
        cosT[d] = np.cos(ang)
        sinT[d] = np.sin(ang)
    c["cosT"] = cosT
    c["sinT"] = sinT
    rm = np.zeros((128, 128), np.float32)
    for dp in range(128):
        if (dp % 64) < 32:
            rm[dp + 32, dp] = -1.0
        else:
            rm[dp - 32, dp] = 1.0
    c["rotm"] = rm
    ev = np.zeros((2, 4, 8), np.float32)
    i8 = np.arange(8, dtype=np.float32)
    ev[0, 0] = i8; ev[0, 1] = -i8; ev[0, 2] = i8 + 1; ev[0, 3] = 7 - i8
    ev[1, 0] = -i8; ev[1, 1] = i8; ev[1, 2] = 8 - i8; ev[1, 3] = i8
    c["ev"] = np.broadcast_to(ev.reshape(1, 64), (128, 64)).copy()
    s0 = np.arange(128) // 16
    ev8 = np.zeros((3, 17), np.float32)
    ev8[0] = 8.0 * np.arange(17)
    ev8[1, :16] = 8.0 * (15 - np.arange(16)); ev8[1, 16] = 128.0
    ev8[2, :8] = 8.0 * (7 - np.arange(8)); ev8[2, 8] = 64.0
    c["ev8"] = np.broadcast_to(ev8.reshape(1, 51), (128, 51)).copy()
    c["mf"] = (s0[None, :] >= s0[:, None]).astype(np.float32)
    c["mb"] = (s0[:, None] >= s0[None, :]).astype(np.float32)
    sg = np.ones((128, 2), np.float32)
    sg[:64, 0] = -1.0
    sg[64:, 1] = -1.0
    c["sgn"] = sg
    ust = [0, 0, 0, 2, 4, 6, 6, 6]
    rowm = np.zeros((8, 2, 10, 64), np.float32)
    for i in range(8):
        for rl in range(2):
            r = 2 * i + rl
            st = min(max(r - 4, 0), 8)
            for kr in range(10):
                ka = ust[i] + kr
                if not (st <= ka < st + 8):
                    rowm[i, rl, kr, :] = -1e30
    c["rowm"] = rowm.reshape(8, 2, 640)
    ind = np.zeros((2, 128), np.float32)
    ind[0, :64] = 1.0
    ind[1, 64:] = 1.0
    c["ind"] = ind
    return c


NA_UST = [0, 0, 0, 2, 4, 6, 6, 6]


def _rpb_table(rpb):
    cq = np.arange(64)
    ck = np.arange(64)
    ws = np.clip(cq - 8, 0, 48)
    colok = (ck[None, :] >= ws[:, None]) & (ck[None, :] < ws[:, None] + 16)
    coff = np.clip(ck[None, :] - cq[:, None] + 15, 0, 30)
    out = np.full((2, 64, 8, 19, 64), -1e30, np.float32)
    for rl in range(2):
        for e in range(19):
            dr = e - 2 - rl
            if 0 <= dr <= 14:
                g = rpb[:, dr, :][:, coff]
                out[rl, :, :, e, :] = np.where(colok[None], g, np.float32(-1e30)).transpose(1, 0, 2)
            else:
                out[rl, :, :, e, :] = np.where(colok[:, None, :], np.float32(0.0), np.float32(-1e30))
    return out.reshape(128, 8, 19, 64)


CONST_SHAPES = {"ident": [128, 128], "e1": [128, 128], "e2": [128, 128], "m1": [128, 128], "m2": [128, 128],
                "pidx": [128, 4], "cosT": [128, 1024], "sinT": [128, 1024], "rotm": [128, 128],
                "ev": [128, 64], "mf": [128, 128], "mb": [128, 128], "sgn": [128, 2],
                "rowm": [8, 2, 640], "ind": [2, 128], "ev8": [128, 51]}

W_SHAPES = {
    'w_ada': [2, 1024, 6144], 'b_ada': [2, 6144], 'w_in': [2, 1024, 7168], 'ret_decay': [2, 2, 4],
    'ssm_a_re': [2, 2, 32, 64], 'ssm_a_im': [2, 2, 32, 64], 'ssm_log_dt': [2, 2, 32],
    'ssm_b_re': [2, 32, 64, 16], 'ssm_b_im': [2, 32, 64, 16], 'ssm_c_re': [2, 2, 32, 16, 64],
    'ssm_c_im': [2, 2, 32, 16, 64], 'ssm_d': [2, 512], 'ssm_w_glu': [2, 512, 512], 'na_rpb': [2, 8, 15, 31],
    'w_branch': [2, 3, 512, 1024], 'w_o': [2, 1024, 1024], 'ln1_g': [2, 1024], 'ln1_b': [2, 1024],
    'w_up': [2, 1024, 5632], 'conv_w': [2, 3, 5632], 'conv_b': [2, 5632], 'w_down': [2, 2816, 1024],
    'ln2_g': [2, 1024], 'ln2_b': [2, 1024],
}


class Path:
    def __init__(self, kind):
        self.kind = kind
        if kind == "p":
            self.n_seq, self.L = 2, 256
        else:
            self.n_seq, self.L = 1, 1024
        self.T = self.n_seq * self.L
        self.NT = self.T // 128
        self.CPS = self.L // 128
        self.NH = max(1, self.T // 512)


class Builder:
    def __init__(self, debug=None, stop_after=None):
        self.debug = debug or []
        self.stop_after = stop_after
        nc = self.nc = bass.Bass("TRN2", target_bir_lowering=False)
        self.P = Prog(nc)
        di = lambda n, s: nc.dram_tensor(n, list(s), F32, kind="ExternalInput").ap()
        do = lambda n, s: nc.dram_tensor(n, list(s), F32, kind="ExternalOutput").ap()
        self.xin = {"p": di("xin_p", [512, 1024]), "s": di("xin_s", [1024, 1024])}
        self.cond = {"p": di("cond_p", [1024]), "s": di("cond_s", [1024])}
        self.sret = di("sret", [2, 2, 4, 128, 128])
        self.sssm = di("sssm", [2, 2, 32, 64, 2])
        self.ck = di("ck", [2, 512, 512])
        self.cv = di("cv", [2, 512, 512])
        self.rpbt = di("rpbt", [2, 128, 8, 19, 64])
        self.adas = nc.dram_tensor("adas", [2, 6144], F32, kind="Internal").ap()
        self.s5c = nc.dram_tensor("s5c", [2, 2, 4, 4, 128, 1024], BF16, kind="Internal").ap()
        self.s5a = nc.dram_tensor("s5a", [2, 2, 128, 2, 32], F32, kind="Internal").ap()
        self.W = {k: di(k, s) for k, s in W_SHAPES.items()}
        self.C = {k: di("c_" + k, s) for k, s in CONST_SHAPES.items()}
        self.yout = {"p": do("yp", [512, 1024]), "s": do("ys", [1024, 1024])}
        self.nsr = do("nsr", [2, 2, 2, 4, 128, 128])
        self.nss = do("nss", [2, 2, 2, 32, 64, 2])
        self.nck = do("nck", [2, 2, 256, 512])
        self.ncv = do("ncv", [2, 2, 256, 512])
        self.dbg = {}
        self.uid = 0
        self.alloc()

    def dbg_out(self, name, shape):
        if name not in self.dbg:
            self.dbg[name] = self.nc.dram_tensor("dbg_" + name, list(shape), F32, kind="ExternalOutput").ap()
        return self.dbg[name]

    def tok(self, base):
        self.uid += 1
        return f"{base}#{self.uid}"

    def alloc(self):
        P = self.P
        sb = P.sb
        self.ident_f = sb([128, 128], F32)
        self.ident = sb([128, 128], BF16)
        self.ones = sb([128, 128], BF16)
        self.epsc = sb([128, 1], F32)
        self.cE1 = sb([128, 128], F32)
        self.cE2 = sb([128, 128], F32)
        self.cM1 = sb([128, 128], F32)
        self.cM2 = sb([128, 128], F32)
        self.pidx = sb([128, 4], F32)
        self.rotm = sb([128, 128], BF16)
        self.cosT = sb([128, 1024], F32)
        self.sinT = sb([128, 1024], F32)
        self.ev = sb([128, 2, 32], F32)
        self.mf = sb([128, 128], F32)
        self.mb = sb([128, 128], F32)
        self.sgn = sb([128, 2], F32)
        self.ind = sb([2, 128], BF16)
        self.ev8 = sb([128, 3, 17], F32)
        self.xres = [sb([128, 1024], F32) for _ in range(8)]
        self.mod = sb([128, 3072], F32)
        self.hT = sb([128, 8, 1024], BF16)
        self.NS = 3
        self.wring = [sb([128, 8, 512], BF16) for _ in range(self.NS)]
        self.wcnt = 0
        self.pref = []
        self.condc = sb([128, 8], F32)
        self.condb = sb([128, 8], BF16)
        self.scondT = sb([128, 8, 128], BF16)
        self.scondT2 = sb([128, 8, 128], BF16)
        self.rdec = sb([128, 8], F32)
        self.rtab = sb([128, 5, 8], F32)
        self.Dc = sb([128, 4, 128], F32)
        _t = sb([128, 1024], F32)
        self.t1024 = [_t, _t]
        self.tmpA = _t[:, 0:128]
        self.tmpB = _t[:, 128:256]
        self.hb = [sb([128, 1024], BF16) for _ in range(2)]
        self.bnst = sb([128, 4, 8], F32)
        self.bnag = sb([128, 4, 8], F32)
        self.ARENA = 96 * 1024
        self.arena = sb([128, self.ARENA // 2], BF16)
        self.aoff = 0
        self.routT = self.av([128, 4, 1024], BF16)
        self.soutT = self.av([128, 4, 1024], BF16)
        self.noutT = self.av([128, 4, 1024], BF16)
        self.amark = self.aoff
        self.psf = [P.ps([128, 512], F32) for _ in range(5)]
        self.psb = [P.ps([128, 8, 128], BF16) for _ in range(3)]
        self.nf = 0
        self.nb = 0

    def av(self, shape, dtype):
        n = 1
        for d in shape[1:]:
            n *= d
        nbytes = n * (4 if dtype == F32 else 2)
        off = (self.aoff + 63) // 64 * 64
        assert off + nbytes <= self.ARENA, ("arena overflow", off, nbytes)
        self.aoff = off + nbytes
        v = self.arena[:, off // 2:(off + nbytes) // 2]
        if dtype == F32:
            v = v.bitcast(F32)
        if len(shape) == 3:
            v = v.rearrange("p (a b) -> p a b", b=shape[2])
        elif len(shape) == 4:
            v = v.rearrange("p (a b c) -> p a b c", b=shape[2], c=shape[3])
        elif len(shape) == 5:
            v = v.rearrange("p (a b c d) -> p a b c d", b=shape[2], c=shape[3], d=shape[4])
        return v

    def phase(self, at=None):
        self.P.barrier()
        self.aoff = self.amark if at is None else at

    def alloc_ret(self):
        av = self.av
        self.rqT = av([128, 4, 1024], BF16)
        self.rkT = av([128, 4, 1024], BF16)
        self.rv = av([128, 8, 512], BF16)
        self.rg = av([128, 8, 512], BF16)
        self.Sm = av([128, 2, 4, 128], F32)
        self.Sin = av([128, 8, 2, 4, 128], BF16)
        self.PT4 = [av([128, 128], BF16) for _ in range(4)]
        self.osb = [av([128, 512], F32) for _ in range(2)]
        self.qtmp = [av([128, 512], BF16) for _ in range(2)]
        self.kdall = [av([128, 4, 128], BF16) for _ in range(2)]

    def next_f(self):
        i = self.nf % len(self.psf)
        self.nf += 1
        return self.psf[i], f"psf{i}"

    def next_b(self):
        i = self.nb % len(self.psb)
        self.nb += 1
        return self.psb[i], f"psb{i}"

    def wload(self, src, nk, ncols):
        i = self.wcnt % self.NS
        self.wcnt += 1
        slot = self.wring[i]
        t = f"wslot{i}"
        self.P.dma("pool", lambda e: e.dma_start(out=slot[:, 0:nk, 0:ncols],
                                                  in_=src.rearrange("(c p) n -> p c n", p=128)), w=[t])
        return slot, t

    @staticmethod
    def _wkey(src):
        return (src.name, src.offset, tuple(src.shape))

    def prefetch(self, specs):
        if "nopf" in self.debug:
            return
        for sp in specs:
            self.pref.append((self._wkey(sp[0]), self.wload(*sp)))

    def wget(self, *spec):
        if self.pref and self.pref[0][0] == self._wkey(spec[0]):
            return self.pref.pop(0)[1]
        assert not self.pref, ("prefetch mismatch", self.pref[0][0], self._wkey(spec[0]))
        return self.wload(*spec)

    def wstream(self, specs, depth=2):
        specs = list(specs)
        n = len(specs)
        loaded = []
        k = 0
        while k < min(depth, n):
            loaded.append(self.wget(*specs[k]))
            k += 1
        for i in range(n):
            yield loaded[i]
            if k < n:
                loaded.append(self.wget(*specs[k]))
                k += 1

    def load_consts(self):
        P = self.P
        C = self.C
        P.dma("sp", lambda e: e.dma_start(out=self.ident_f[:], in_=C["ident"]), w=["ident_f"])
        P.dve(lambda e: e.tensor_copy(out=self.ident[:], in_=self.ident_f[:]), r=["ident_f"], w=["ident"])
        P.pool(lambda e: e.memset(self.ones[:], 1.0), w=["ones"])
        P.pool(lambda e: e.memset(self.epsc[:], LN_EPS), w=["epsc"])
        for nm, t in (("e1", self.cE1), ("e2", self.cE2), ("m1", self.cM1), ("m2", self.cM2), ("pidx", self.pidx),
                      ("cosT", self.cosT), ("sinT", self.sinT), ("mf", self.mf), ("mb", self.mb), ("sgn", self.sgn)):
            P.dma("sp", lambda e, nm=nm, t=t: e.dma_start(out=t[:], in_=C[nm]), w=["c_" + nm])
        P.dma("pool", lambda e: e.dma_start(out=self.rotm[:], in_=C["rotm"]), w=["rotm"])
        P.dma("sp", lambda e: e.dma_start(out=self.ev[:], in_=C["ev"].rearrange("p (a b) -> p a b", a=2)), w=["c_ev"])
        P.dma("pool", lambda e: e.dma_start(out=self.ind[:], in_=C["ind"]), w=["c_ind"])
        P.dma("sp", lambda e: e.dma_start(out=self.ev8[:], in_=C["ev8"].rearrange("p (a b) -> p a b", a=3)), w=["c_ev8"])

    def ada_half(self, pa, l, hh):
        P = self.P
        W = self.W
        c0 = hh * 3072
        share = "noadashare" not in self.debug
        if pa.kind == "s" and share:
            P.dma("sp", lambda e: e.dma_start(out=self.mod[:], in_=self.adas[l, c0:c0 + 3072].partition_broadcast(128)),
                  r=[f"adas{l}_{hh}_{b}" for b in range(6)], w=["mod"])
            P.dve(lambda e: e.tensor_scalar(out=self.mod[:, 1024:2048], in0=self.mod[:, 1024:2048], scalar1=1.0, scalar2=None,
                                            op0=ALU.add), r=["mod"], w=["mod"])
            return
        if hh == 0 and l == 0:
            kinds = ("p", "s") if (pa.kind == "p" and share) else (pa.kind,)
            for kd in kinds:
                dst = self.scondT if kd == pa.kind else self.scondT2
                dtok = "scondT" if kd == pa.kind else "scondT2"
                P.dma("sp", lambda e, kd=kd: e.dma_start(out=self.condc[:], in_=self.cond[kd].rearrange("(c p) -> p c", p=128),
                                                         allow_slow_non_contiguous=True), w=["condc"])
                P.act(lambda e: e.activation(out=self.condb[:], in_=self.condc[:], func=AF.Silu), r=["condc"], w=["condb"])
                for c in range(8):
                    P.dve(lambda e, c=c, dst=dst: e.tensor_scalar(out=dst[:, c, :], in0=self.ones[:], scalar1=self.condb[:, c:c + 1],
                                                                  scalar2=None, op0=ALU.mult), r=["ones", "condb"], w=[dtok])
        P.dma("sp", lambda e: e.dma_start(out=self.mod[:], in_=W["b_ada"][l, c0:c0 + 3072].partition_broadcast(128)),
              w=["mod"])
        specs = [(W["w_ada"][l, :, c0 + b * 512:c0 + (b + 1) * 512], 8, 512) for b in range(6)]
        for b, (slot, wt) in enumerate(self.wstream(specs)):
            blk = slice(b * 512, (b + 1) * 512)
            if pa.kind == "p" and share:
                ps2, pt2 = self.next_f()
                for c in range(8):
                    P.pe(lambda e, c=c: e.matmul(ps2[:], lhsT=self.scondT2[:, c, :], rhs=slot[:, c, :], start=(c == 0), stop=(c == 7)),
                         r=["scondT2", wt], w=[pt2])
                t = self.t1024[0]
                P.dve(lambda e: e.tensor_tensor(out=t[:, 0:512], in0=self.mod[:, blk], in1=ps2[:], op=ALU.add), r=[pt2, "mod"], w=["t1024_0"])
                P.dma("sp", lambda e, b=b: e.dma_start(out=self.adas[l:l + 1, c0 + b * 512:c0 + (b + 1) * 512], in_=t[0:1, 0:512]),
                      r=["t1024_0"], w=[f"adas{l}_{hh}_{b}"])
            ps, pt = self.next_f()
            for c in range(8):
                P.pe(lambda e, c=c: e.matmul(ps[:], lhsT=self.scondT[:, c, :], rhs=slot[:, c, :], start=(c == 0), stop=(c == 7)),
                     r=["scondT", wt], w=[pt])
            P.dve(lambda e: e.tensor_tensor(out=self.mod[:, blk], in0=self.mod[:, blk], in1=ps[:], op=ALU.add), r=[pt, "mod"], w=["mod"])
        P.dve(lambda e: e.tensor_scalar(out=self.mod[:, 1024:2048], in0=self.mod[:, 1024:2048], scalar1=1.0, scalar2=None,
                                        op0=ALU.add), r=["mod"], w=["mod"])

    def modulate_T(self, pa):
        P = self.P
        for i in range(pa.NT):
            t = self.t1024[i % 2]
            hb = self.hb[i % 2]
            tt, ht = "t1024_0", f"hb{i % 2}"
            P.dve(lambda e, i=i, t=t: e.tensor_tensor(out=t[:], in0=self.xres[i][:], in1=self.mod[:, 1024:2048], op=ALU.mult),
                  r=[f"x{i}", "mod"], w=[tt])
            P.dve(lambda e, t=t, hb=hb: e.tensor_tensor(out=hb[:], in0=t[:], in1=self.mod[:, 0:1024], op=ALU.add),
                  r=[tt, "mod"], w=[ht])
            pb, pbt = self.next_b()
            for c in range(8):
                P.pe(lambda e, c=c, hb=hb, pb=pb: e.transpose(out=pb[:, c, :], in_=hb[:, c * 128:(c + 1) * 128],
                                                              identity=self.ident[:]), r=[ht, "ident"], w=[pbt])
            P.act(lambda e, i=i, pb=pb: e.copy(out=self.hT[:, :, i * 128:(i + 1) * 128], in_=pb[:]), r=[pbt], w=[f"hT{i}"])

    def hT_tokens(self, pa, lo, hi):
        return [f"hT{i}" for i in range(lo // 128, (hi + 127) // 128)]

    def proj_feat(self, pa, slot, wt, j, consume):
        P = self.P
        for th in range(pa.NH):
            ps, pt = self.next_f()
            for c in range(8):
                P.pe(lambda e, c=c, ps=ps, th=th: e.matmul(ps[:], lhsT=slot[:, c, j * 128:(j + 1) * 128],
                                                           rhs=self.hT[:, c, th * 512:(th + 1) * 512],
                                                           start=(c == 0), stop=(c == 7)),
                     r=[wt] + self.hT_tokens(pa, th * 512, (th + 1) * 512), w=[pt])
            consume(ps, pt, th)

    def proj_tok(self, pa, slot, wt, consume, ncols=512):
        P = self.P
        for i in range(pa.NT):
            ps, pt = self.next_f()
            for c in range(8):
                P.pe(lambda e, c=c, ps=ps, i=i: e.matmul(ps[:, 0:ncols], lhsT=self.hT[:, c, i * 128:(i + 1) * 128],
                                                         rhs=slot[:, c, 0:ncols], start=(c == 0), stop=(c == 7)),
                     r=[wt, f"hT{i}"], w=[pt])
            consume(ps, pt, i)

    def ret_tables(self, l):
        P = self.P
        P.dma("sp", lambda e: e.dma_start(out=self.rdec[:], in_=self.W["ret_decay"][l].rearrange("a b -> (a b)")
                                          .partition_broadcast(128)), w=["rdec"])
        P.act(lambda e: e.activation(out=self.rdec[:], in_=self.rdec[:], func=AF.Exp, scale=-1.0), r=["rdec"], w=["rdec"])
        P.act(lambda e: e.activation(out=self.rdec[:], in_=self.rdec[:], func=AF.Ln, bias=1.0, scale=1.0), r=["rdec"], w=["rdec"])
        P.dve(lambda e: e.tensor_scalar(out=self.rdec[:], in0=self.rdec[:], scalar1=-1.0, scalar2=None, op0=ALU.mult),
              r=["rdec"], w=["rdec"])
        for di in range(2):
            for h in range(4):
                col = di * 4 + h
                P.act(lambda e, col=col, di=di: e.activation(out=self.rtab[:, 0, col:col + 1], in_=self.pidx[:, di:di + 1],
                                                             func=AF.Exp, scale=self.rdec[:, col:col + 1]),
                      r=["rdec", "c_pidx"], w=["rtab"])
                P.act(lambda e, col=col, di=di: e.activation(out=self.rtab[:, 2, col:col + 1], in_=self.pidx[:, 2 + di:3 + di],
                                                             func=AF.Exp, scale=self.rdec[:, col:col + 1]),
                      r=["rdec", "c_pidx"], w=["rtab"])
        P.act(lambda e: e.activation(out=self.rtab[:, 4, :], in_=self.rdec[:], func=AF.Exp, scale=128.0), r=["rdec"], w=["rtab"])
        sc = 128.0 ** -0.5
        P.dve(lambda e: e.tensor_scalar(out=self.rtab[:, 2, :], in0=self.rtab[:, 2, :], scalar1=sc, scalar2=None, op0=ALU.mult),
              r=["rtab"], w=["rtab"])
        for h in range(4):
            P.act(lambda e, h=h: e.activation(out=self.tmpA[:], in_=self.cE1[:], func=AF.Exp, scale=self.rdec[:, h:h + 1]),
                  r=["rdec", "c_e1"], w=["t1024_0"])
            P.act(lambda e, h=h: e.activation(out=self.tmpB[:], in_=self.cE2[:], func=AF.Exp, scale=self.rdec[:, 4 + h:5 + h]),
                  r=["rdec", "c_e2"], w=["t1024_0"])
            P.dve(lambda e: e.tensor_tensor(out=self.tmpA[:], in0=self.tmpA[:], in1=self.cM1[:], op=ALU.mult),
                  r=["t1024_0", "c_m1"], w=["t1024_0"])
            P.dve(lambda e: e.tensor_tensor(out=self.tmpB[:], in0=self.tmpB[:], in1=self.cM2[:], op=ALU.mult),
                  r=["t1024_0", "c_m2"], w=["t1024_0"])
            P.dve(lambda e, h=h: e.scalar_tensor_tensor(out=self.Dc[:, h, :], in0=self.tmpA[:], scalar=sc, in1=self.tmpB[:],
                                                        op0=ALU.mult, op1=ALU.add), r=["t1024_0", "t1024_0"], w=["Dc"])
            P.dve(lambda e, h=h: e.scalar_tensor_tensor(out=self.Dc[:, h, :], in0=self.tmpB[:], scalar=sc - 1.0,
                                                        in1=self.Dc[:, h, :], op0=ALU.mult, op1=ALU.add),
                  r=["t1024_0", "Dc"], w=["Dc"])

    def retention(self, pa, l):
        P = self.P
        W = self.W
        rope = pa.kind == "s" and "norope" not in self.debug
        w_in = W["w_in"][l]
        specs = [(w_in[:, b * 512:(b + 1) * 512], 8, 512) for b in range(4)]
        ws = self.wstream(specs)
        for which, dst, dname in ((0, self.rqT, "rqT"), (1, self.rkT, "rkT")):
            slot, wt = next(ws)
            for hd in range(4):
                def consume(ps, pt, th, hd=hd, dst=dst, dname=dname):
                    cols = slice(th * 512, (th + 1) * 512)
                    otok = f"{dname}{hd}_{th}"
                    if not rope:
                        P.act(lambda e: e.copy(out=dst[:, hd, cols], in_=ps[:]), r=[pt], w=[otok])
                        return
                    qt = self.qtmp[(hd + th) % 2]
                    qtt = f"qtmp{(hd + th) % 2}"
                    P.act(lambda e: e.copy(out=qt[:], in_=ps[:]), r=[pt], w=[qtt])
                    ps2, pt2 = self.next_f()
                    P.pe(lambda e: e.matmul(ps2[:], lhsT=(self.ident[:] if "norot" in self.debug else self.rotm[:]), rhs=qt[:], start=True, stop=True),
                         r=[qtt, "rotm", "ident"], w=[pt2])
                    t1 = self.t1024[0]
                    if "v1" in self.debug:
                        P.dve(lambda e: e.tensor_copy(out=dst[:, hd, cols], in_=ps2[:]), r=[pt2], w=[otok])
                        return
                    if "v2" in self.debug:
                        src = self.mod[:, 0:512] if "v3" in self.debug else self.cosT[:, cols]
                        P.dve(lambda e: e.tensor_tensor(out=t1[:, 0:512], in0=ps[:], in1=src, op=ALU.mult),
                              r=[pt, "c_cosT", "mod"], w=["t1024_0"])
                        P.dve(lambda e: e.tensor_copy(out=dst[:, hd, cols], in_=t1[:, 0:512]), r=["t1024_0"], w=[otok])
                        return
                    P.dve(lambda e: e.tensor_tensor(out=t1[:, 0:512], in0=ps[:], in1=self.cosT[:, cols], op=ALU.mult),
                          r=[pt, "c_cosT"], w=["t1024_0"])
                    P.dve(lambda e: e.tensor_tensor(out=t1[:, 512:1024], in0=ps2[:], in1=self.sinT[:, cols], op=ALU.mult),
                          r=[pt2, "c_sinT"], w=["t1024_0"])
                    P.dve(lambda e: e.tensor_tensor(out=dst[:, hd, cols], in0=t1[:, 0:512], in1=t1[:, 512:1024], op=ALU.add),
                          r=["t1024_0"], w=[otok])
                self.proj_feat(pa, slot, wt, hd, consume)
        slot, wt = next(ws)
        self.proj_tok(pa, slot, wt, lambda ps, pt, i: P.act(lambda e: e.copy(out=self.rv[:, i, :], in_=ps[:]),
                                                            r=[pt], w=[f"rv{i}"]))
        slot, wt = next(ws)
        self.proj_tok(pa, slot, wt, lambda ps, pt, i: P.act(lambda e: e.activation(out=self.rg[:, i, :], in_=ps[:], func=AF.Silu),
                                                            r=[pt], w=[f"rg{i}"]))
        for _ in ws:
            pass
        NHD = 4
        if self.stop_after == "retproj":
            return
        for s in range(pa.n_seq):
            if pa.kind == "s":
                P.dma("sp", lambda e: e.dma_start(out=self.Sm[:], in_=self.sret[l].rearrange("a h d e -> d a h e")), w=["Sm0", "Sm1"])
            else:
                P.pool(lambda e: e.memset(self.Sm[:], 0.0), w=["Sm0", "Sm1"])
            for ii in range(pa.CPS):
                for di in range(2):
                    i = ii if di == 0 else pa.CPS - 1 - ii
                    ci = s * pa.CPS + i
                    tks = slice(ci * 128, (ci + 1) * 128)
                    th = (ci * 128) // 512
                    P.act(lambda e, ci=ci, di=di: e.copy(out=self.Sin[:, ci, di, :, :], in_=self.Sm[:, di, :, :]),
                          r=[f"Sm{di}"], w=[f"Sin{ci}_{di}"])
                    if ii == pa.CPS - 1 and pa.kind == "s":
                        continue
                    pb, pbt = self.next_b()
                    for hd in range(NHD):
                        P.pe(lambda e, hd=hd, pb=pb, tks=tks: e.transpose(out=pb[:, hd, :], in_=self.rkT[:, hd, tks],
                                                                          identity=self.ident[:]),
                             r=[f"rkT{hd}_{th}", "ident"], w=[pbt])
                    kdall = self.kdall[di]
                    kdat = f"kdall{di}"
                    for hd in range(NHD):
                        col = di * 4 + hd
                        if hd % 2 == 0:
                            P.act(lambda e, hd=hd, col=col, pb=pb, kdall=kdall: e.activation(
                                out=kdall[:, hd, :], in_=pb[:, hd, :], func=AF.Copy, scale=self.rtab[:, 2, col:col + 1]),
                                r=[pbt, "rtab"], w=[kdat])
                        else:
                            P.dve(lambda e, hd=hd, col=col, pb=pb, kdall=kdall: e.tensor_scalar(
                                out=kdall[:, hd, :], in0=pb[:, hd, :], scalar1=self.rtab[:, 2, col:col + 1], scalar2=None,
                                op0=ALU.mult), r=[pbt, "rtab"], w=[kdat])
                    ps, pt = self.next_f()
                    for hd in range(NHD):
                        P.pe(lambda e, hd=hd, ps=ps, kdall=kdall, ci=ci: e.matmul(
                            ps[:, hd * 128:(hd + 1) * 128], lhsT=kdall[:, hd, :], rhs=self.rv[:, ci, hd * 128:(hd + 1) * 128],
                            start=True, stop=True), r=[kdat, f"rv{ci}"], w=[pt])
                    for hd in range(NHD):
                        col = di * 4 + hd
                        P.dve(lambda e, hd=hd, col=col, ps=ps, di=di: e.scalar_tensor_tensor(
                            out=self.Sm[:, di, hd, :], in0=self.Sm[:, di, hd, :], scalar=self.rtab[:, 4, col:col + 1],
                            in1=ps[:, hd * 128:(hd + 1) * 128], op0=ALU.mult, op1=ALU.add),
                            r=[f"Sm{di}", "rtab", pt], w=[f"Sm{di}"])
            if pa.kind == "p":
                P.dma("sp", lambda e, s=s: e.dma_start(out=self.nsr[s, l].rearrange("a h d e -> d a h e"), in_=self.Sm[:]),
                      r=["Sm0", "Sm1"], w=["nsr"])
        if self.stop_after == "retA":
            return
        for ci in range(pa.n_seq * pa.CPS):
            i = ci % pa.CPS
            tks = slice(ci * 128, (ci + 1) * 128)
            th = (ci * 128) // 512
            osb = self.osb[ci % 2]
            ost = f"osb{ci % 2}"
            banks = [self.next_f() for _ in range(NHD)]
            ocs = [slice(hd * 128, (hd + 1) * 128) for hd in range(NHD)]
            for hd in range(NHD):
                ps, pt = banks[hd]
                P.pe(lambda e, hd=hd, ps=ps: e.matmul(ps[:, 0:128], lhsT=self.rkT[:, hd, tks], rhs=self.rqT[:, hd, tks], start=True, stop=True),
                     r=[f"rkT{hd}_{th}", f"rqT{hd}_{th}"], w=[pt])
            for hd in range(NHD):
                ps, pt = banks[hd]
                P.dve(lambda e, hd=hd, ps=ps: e.tensor_tensor(out=self.PT4[hd][:], in0=ps[:, 0:128], in1=self.Dc[:, hd, :], op=ALU.mult),
                      r=[pt, "Dc"], w=[f"PT{hd}"])
            for hd in range(NHD):
                ps, pt = banks[hd]
                P.pe(lambda e, hd=hd, ps=ps: e.matmul(ps[:, 128:256], lhsT=self.PT4[hd][:], rhs=self.rv[:, ci, hd * 128:(hd + 1) * 128],
                                                      start=True, stop=True), r=[f"PT{hd}", f"rv{ci}"], w=[pt])
                P.pe(lambda e, hd=hd, ps=ps: e.matmul(ps[:, 256:384], lhsT=self.rqT[:, hd, tks], rhs=self.Sin[:, ci, 0, hd, :], start=True, stop=True),
                     r=[f"rqT{hd}_{th}", f"Sin{ci}_0"], w=[pt])
                P.pe(lambda e, hd=hd, ps=ps: e.matmul(ps[:, 384:512], lhsT=self.rqT[:, hd, tks], rhs=self.Sin[:, ci, 1, hd, :], start=True, stop=True),
                     r=[f"rqT{hd}_{th}", f"Sin{ci}_1"], w=[pt])
            for hd in range(NHD):
                ps, pt = banks[hd]
                P.act(lambda e, hd=hd, ps=ps: e.activation(out=osb[:, ocs[hd]], in_=ps[:, 256:384], func=AF.Copy, scale=self.rtab[:, 0, hd:hd + 1]),
                      r=[pt, "rtab"], w=[ost + f"h{hd}"])
            for hd in range(NHD):
                ps, pt = banks[hd]
                P.dve(lambda e, hd=hd, ps=ps: e.scalar_tensor_tensor(out=osb[:, ocs[hd]], in0=ps[:, 384:512], scalar=self.rtab[:, 0, 4 + hd:5 + hd],
                                                                     in1=osb[:, ocs[hd]], op0=ALU.mult, op1=ALU.add),
                      r=[pt, "rtab", ost + f"h{hd}"], w=[ost + f"h{hd}"])
            for hd in range(NHD):
                ps, pt = banks[hd]
                P.dve(lambda e, hd=hd, ps=ps: e.tensor_tensor(out=osb[:, ocs[hd]], in0=osb[:, ocs[hd]], in1=ps[:, 128:256], op=ALU.add),
                      r=[pt, ost + f"h{hd}"], w=[ost + f"h{hd}"])
            for hd in range(NHD):
                P.dve(lambda e, hd=hd: e.bn_stats(out=self.bnst[:, hd, 0:6], in_=osb[:, ocs[hd]]), r=[ost + f"h{hd}"], w=[f"bnst{hd}"])
            for hd in range(NHD):
                P.dve(lambda e, hd=hd: e.bn_aggr(out=self.bnag[:, hd, 0:2], in_=self.bnst[:, hd, 0:6]), r=[f"bnst{hd}"], w=[f"bnag{hd}"])
            var4 = self.bnag[:, 0:4, 1]
            P.act(lambda e: e.activation(out=var4, in_=var4, func=AF.Ln, bias=self.epsc[:, 0:1], scale=1.0),
                  r=[f"bnag{hd}" for hd in range(NHD)] + ["epsc"], w=[f"bnag{hd}" for hd in range(NHD)])
            P.act(lambda e: e.activation(out=var4, in_=var4, func=AF.Exp, scale=-0.5),
                  r=[f"bnag{hd}" for hd in range(NHD)], w=[f"bnag{hd}" for hd in range(NHD)])
            for hd in range(NHD):
                P.dve(lambda e, hd=hd: e.tensor_scalar(out=osb[:, ocs[hd]], in0=osb[:, ocs[hd]], scalar1=self.bnag[:, hd, 0:1],
                                                       scalar2=self.bnag[:, hd, 1:2], op0=ALU.subtract, op1=ALU.mult),
                      r=[f"bnag{hd}", ost + f"h{hd}"], w=[ost + f"h{hd}"])
            hb = self.hb[ci % 2]
            hbt = f"hb{ci % 2}"
            P.dve(lambda e, osb=osb, hb=hb, ci=ci: e.tensor_tensor(out=hb[:, 0:512], in0=osb[:], in1=self.rg[:, ci, :], op=ALU.mult),
                  r=[ost + f"h{h}" for h in range(4)] + [f"rg{ci}"], w=[hbt])
            pb, pbt = self.next_b()
            for c in range(4):
                P.pe(lambda e, c=c, hb=hb, pb=pb: e.transpose(out=pb[:, c, :], in_=hb[:, c * 128:(c + 1) * 128],
                                                              identity=self.ident[:]), r=[hbt, "ident"], w=[pbt])
            P.act(lambda e, pb=pb, tks=tks: e.copy(out=self.routT[:, :, tks], in_=pb[:, 0:4, :]), r=[pbt], w=[f"routT{ci}"])


    def alloc_s5(self, pa):
        av = self.av
        J = pa.T // 8
        self.su_bm = av([128, 32, 8, 16], BF16)
        self.Yacc = av([128, 32, J], F32)
        self.s5_ymark = self.aoff
        self.Ub = av([128, 8, J], BF16)
        self.KK = av([128, 8, 128], BF16)
        self.QQ = av([128, 8, 128], BF16)
        self.MinT = av([128, 8, 128], BF16)
        self.MinTs = av([128, 8, 128], BF16)
        self.Mintra = av([128, 8, 128], BF16)
        self.Min = av([128, 8, 128], BF16)
        self.Mins = av([128, 8, 128], BF16)
        self.Et = av([128, 3, 8, 32], F32)
        self.braw = av([128, 2, 8, 16], F32)
        self.craw = av([128, 2, 2, 64], F32)
        self.cT = av([128, 2, 8, 16], F32)
        self.araw = av([64, 2, 2, 64], F32)
        self.aT = av([128, 6, 64], F32)
        self.qq = av([128, 4, 8], F32)
        self.t4 = self.mod[:, 0:2048].rearrange("p (a b c d) -> p a b c d", a=2, b=8, c=8)
        self.A8 = av([128, 3, 32], F32)
        NL = 8
        self.Xs = [av([128, 32, NL], F32) for _ in range(2)]
        self.Xw = [av([128, 32, NL], F32) for _ in range(2)]
        self.rt = [av([128, 32, NL], F32) for _ in range(4)]
        self.Xin = av([128, 32, NL], F32)
        self.Xinw = av([128, 32, NL], F32)
        self.Cs = [av([128, 32, pa.n_seq], F32) for _ in range(4)]
        self.PRt = av([128, 32, 17], F32)
        self.PIt = av([128, 32, 17], F32)
        self.PWt = av([128, 4, 8, 17], F32)
        self.AS = av([128, 3, 32], F32)
        self.tcor = av([128, 32, 16], F32)
        self.BBs = self.hb[0][:, :].bitcast(F32).rearrange("p (a b c) -> p a b c", a=4, b=8)
        self.Etmp = self.hb[1][:, :].bitcast(F32).rearrange("p (a b c) -> p a b c", a=2, b=8)
        self.dbc = self.t1024[0][:, 512:1024]

    def s5(self, pa, l):
        P = self.P
        W = self.W
        J = pa.T // 8
        Jps = pa.L // 8
        NSEQ = pa.n_seq
        TWO_PI = 2.0 * math.pi
        iA = (self.wcnt - 1) % self.NS if self.pref else self.wcnt % self.NS
        (slot, wt), = list(self.wstream([(W["w_in"][l][:, 2048:2560], 8, 512)]))
        i1, i2 = (iA + 1) % self.NS, (iA + 2) % self.NS
        tokV, tokW = f"wslot{i1}", f"wslot{i2}"
        flat = lambda t: t[:, :, :].rearrange("p a b -> p (a b)")[:, 0:32 * J].rearrange("p (g j) -> p g j", j=J)
        self.Vv, self.Vsw = flat(self.wring[i1]), flat(self.wring[i2])
        self.wcnt += 2
        tokM = f"wslot{iA}"
        self.Mout = self.wring[iA][:, :, :].rearrange("p a b -> p (a b)").rearrange("p (g k) -> p g k", k=128)
        for t0 in range(8):
            ps, pt = self.next_f()
            for c in range(8):
                P.pe(lambda e, c=c, ps=ps, t0=t0, slot=slot: e.matmul(ps[0:J, :], lhsT=self.hT[:, c, t0:pa.T:8], rhs=slot[:, c, :],
                                                           start=(c == 0), stop=(c == 7)),
                     r=[wt] + [f"hT{i}" for i in range(pa.NT)], w=[pt])
            P.act(lambda e, ps=ps, t0=t0: e.copy(out=self.su_bm[0:J, :, t0, :], in_=ps[0:J, :].rearrange("p (g c) -> p g c", c=16)), r=[pt], w=["su_bm"])
        if "s5u" in self.debug and l == 0:
            o2 = self.dbg_out(f"{pa.kind}_subm0", [128, 4096])
            P.dve(lambda e: e.tensor_copy(out=self.t1024[0][0:J, :], in_=self.su_bm[0:J, 0:8, :, :].rearrange("p g a b -> p (g a b)")), r=["su_bm"], w=["t1024_0"])
            P.dma("sp", lambda e, o2=o2: e.dma_start(out=o2[0:J, 0:1024], in_=self.t1024[0][0:J, :]), r=["t1024_0"], w=["dbg"])
        if self.stop_after == "s5u":
            return
        for k, nm in enumerate(("ssm_a_re", "ssm_a_im")):
            for dup in range(2):
                P.dma("sp", lambda e, k=k, nm=nm, dup=dup: e.dma_start(out=self.araw[0:64, k, dup, :],
                                                                       in_=W[nm][l].rearrange("d g p -> (d g) p")), w=["araw"])
        for k in range(2):
            ps, pt = self.next_f()
            P.pe(lambda e, k=k, ps=ps: e.transpose(out=ps[:, 0:64], in_=self.araw[0:64, k, :, :], identity=self.ident_f[0:64, 0:64]),
                 r=["araw", "ident_f"], w=[pt])
            P.act(lambda e, k=k, ps=ps: e.copy(out=self.aT[:, k, :], in_=ps[:, 0:64]), r=[pt], w=["aT"])
        P.dma("sp", lambda e: e.dma_start(out=self.aT[:, 2, :], in_=W["ssm_log_dt"][l].rearrange("d g -> (d g)").partition_broadcast(128)),
              w=["aT"])
        P.act(lambda e: e.activation(out=self.aT[:, 2, :], in_=self.aT[:, 2, :], func=AF.Exp), r=["aT"], w=["aT"])
        P.dve(lambda e: e.tensor_scalar(out=self.aT[:, 0, :], in0=self.aT[:, 0, :], scalar1=-1e-4, scalar2=None, op0=ALU.min),
              r=["aT"], w=["aT"])
        P.dve(lambda e: e.tensor_tensor(out=self.aT[:, 3, :], in0=self.aT[:, 0, :], in1=self.aT[:, 2, :], op=ALU.mult), r=["aT"], w=["aT"])
        P.dve(lambda e: e.tensor_tensor(out=self.aT[:, 4, :], in0=self.aT[:, 1, :], in1=self.aT[:, 2, :], op=ALU.mult), r=["aT"], w=["aT"])
        P.dve(lambda e: e.tensor_tensor(out=self.aT[:, 5, :], in0=self.aT[:, 0, :], in1=self.aT[:, 0, :], op=ALU.mult), r=["aT"], w=["aT"])
        P.dve(lambda e: e.tensor_tensor(out=self.aT[:, 2, :], in0=self.aT[:, 1, :], in1=self.aT[:, 1, :], op=ALU.mult), r=["aT"], w=["aT"])
        P.dve(lambda e: e.tensor_tensor(out=self.aT[:, 5, :], in0=self.aT[:, 5, :], in1=self.aT[:, 2, :], op=ALU.add), r=["aT"], w=["aT"])
        P.dve(lambda e: e.reciprocal(out=self.aT[:, 5, :], in_=self.aT[:, 5, :]), r=["aT"], w=["aT"])
        P.dma("sp", lambda e: e.dma_start(out=self.dbc[:], in_=W["ssm_d"][l].partition_broadcast(128)), w=["dbc"])

        def do_batch(d, b):
            gsl = slice(d * 32 + b * 8, d * 32 + b * 8 + 8)
            gen0 = (pa.kind == "p") or ("nocache" in self.debug)
            if (not gen0) and b % 2 == 1:
                Ubuf, ubt = self.MinTs[:, :, 0:J], "UbB"
            else:
                Ubuf, ubt = self.Ub, "Ub"
            pb, pbt = self.next_b()
            for gi in range(8):
                g = b * 8 + gi
                P.pe(lambda e, gi=gi, g=g, pb=pb: e.transpose(out=pb[:, gi, 0:J], in_=self.su_bm[0:J, g, :, :].rearrange("p a b -> p (a b)"),
                                                              identity=self.ident[0:J, 0:J]), r=["su_bm", "ident"], w=[pbt])
            P.act(lambda e, pb=pb: e.copy(out=Ubuf, in_=pb[:, :, 0:J]), r=[pbt], w=[ubt])
            gen = (pa.kind == "p") or ("nocache" in self.debug)
            Mi, Mn, Ms, sfx = self.Mintra, self.Min, self.Mins, ""
            ctok = f"s5c{l}_{d}_{b}"
            f2 = lambda v: v.rearrange("p a b -> p (a b)")
            if gen:
                lrdt = self.aT[:, 3, gsl].unsqueeze(2).to_broadcast([128, 8, 32])
                lidt = self.aT[:, 4, gsl].unsqueeze(2).to_broadcast([128, 8, 32])
                evb = self.ev[:, d, :].unsqueeze(1).to_broadcast([128, 8, 32])
                Et, Etmp = self.Et, self.Etmp
                P.dve(lambda e: e.tensor_tensor(out=Etmp[:, 0], in0=lrdt, in1=evb, op=ALU.mult), r=["aT", "c_ev"], w=["Etmp"])
                P.act(lambda e: e.activation(out=Etmp[:, 0], in_=Etmp[:, 0], func=AF.Exp), r=["Etmp"], w=["Etmp"])
                P.dve(lambda e: e.tensor_tensor(out=Etmp[:, 1], in0=lidt, in1=evb, op=ALU.mult), r=["aT", "c_ev", "Etmp"], w=["Etmp"])
                MAGIC = 12582912.0
                PI_LO = 3.1415925
                P.dve(lambda e: e.tensor_copy(out=Et[:, 1], in_=Etmp[:, 1]), r=["Etmp"], w=["Et"])
                P.dve(lambda e: e.tensor_scalar(out=Et[:, 0], in0=Etmp[:, 1], scalar1=0.5 * math.pi, scalar2=None, op0=ALU.add),
                      r=["Etmp", "Et"], w=["Et"])
                P.dve(lambda e: e.tensor_scalar(out=Et[:, 2, :, :], in0=Et[:, 0, :, :], scalar1=1.0 / TWO_PI, scalar2=MAGIC, op0=ALU.mult, op1=ALU.add),
                      r=["Et"], w=["Et"])
                P.dve(lambda e: e.tensor_scalar(out=Etmp[:, 1], in0=Et[:, 1, :, :], scalar1=1.0 / TWO_PI, scalar2=MAGIC, op0=ALU.mult, op1=ALU.add),
                      r=["Et", "Etmp"], w=["Etmp"])
                P.dve(lambda e: e.tensor_scalar(out=Et[:, 2], in0=Et[:, 2], scalar1=-MAGIC, scalar2=None, op0=ALU.add), r=["Et"], w=["Et"])
                P.dve(lambda e: e.tensor_scalar(out=Etmp[:, 1], in0=Etmp[:, 1], scalar1=-MAGIC, scalar2=None, op0=ALU.add), r=["Etmp"], w=["Etmp"])
                P.dve(lambda e: e.scalar_tensor_tensor(out=Et[:, 0], in0=Et[:, 2], scalar=-TWO_PI, in1=Et[:, 0], op0=ALU.mult, op1=ALU.add),
                      r=["Et"], w=["Et"])
                P.dve(lambda e: e.scalar_tensor_tensor(out=Et[:, 1], in0=Etmp[:, 1], scalar=-TWO_PI, in1=Et[:, 1], op0=ALU.mult, op1=ALU.add),
                      r=["Et", "Etmp"], w=["Et"])
                P.dve(lambda e: e.tensor_scalar(out=Et[:, 0:2], in0=Et[:, 0:2], scalar1=PI_LO, scalar2=None, op0=ALU.min), r=["Et"], w=["Et"])
                P.dve(lambda e: e.tensor_scalar(out=Et[:, 0:2], in0=Et[:, 0:2], scalar1=-PI_LO, scalar2=None, op0=ALU.max), r=["Et"], w=["Et"])
                P.act(lambda e: e.activation(out=Et[:, 0:2], in_=Et[:, 0:2], func=AF.Sin), r=["Et"], w=["Et"])
                P.dve(lambda e: e.tensor_tensor(out=Et[:, 0], in0=Et[:, 0], in1=Etmp[:, 0], op=ALU.mult), r=["Et", "Etmp"], w=["Et"])
                P.dve(lambda e: e.tensor_tensor(out=Et[:, 1], in0=Et[:, 1], in1=Etmp[:, 0], op=ALU.mult), r=["Et", "Etmp"], w=["Et"])
                P.dve(lambda e: e.tensor_scalar(out=Et[:, 2], in0=Et[:, 1], scalar1=self.sgn[:, 0:1], scalar2=None, op0=ALU.mult),
                      r=["Et", "c_sgn"], w=["Et"])
                i1 = 16 + (0 if d == 0 else 7)
                i8 = 16 + (7 if d == 0 else 0)
                P.dve(lambda e, b=b, i8=i8: e.tensor_copy(out=self.A8[:, 0, b * 8:(b + 1) * 8], in_=Et[:, 0, :, i8]), r=["Et"], w=["A8"])
                P.dve(lambda e, b=b, i8=i8: e.tensor_copy(out=self.A8[:, 1, b * 8:(b + 1) * 8], in_=Et[:, 2, :, i8]), r=["Et"], w=["A8"])
                qq = self.qq
                lr = self.aT[:, 0, gsl]
                li = self.aT[:, 1, gsl]
                rden = self.aT[:, 5, gsl]
                P.dve(lambda e, i1=i1: e.tensor_scalar(out=qq[:, 2, :], in0=Et[:, 0, :, i1], scalar1=-1.0, scalar2=None, op0=ALU.add),
                      r=["Et"], w=["qq"])
                P.dve(lambda e: e.tensor_tensor(out=qq[:, 0, :], in0=qq[:, 2, :], in1=lr, op=ALU.mult), r=["qq", "aT"], w=["qq"])
                P.dve(lambda e, i1=i1: e.tensor_tensor(out=qq[:, 3, :], in0=Et[:, 1, :, i1], in1=li, op=ALU.mult), r=["Et", "aT", "qq"], w=["qq"])
                P.dve(lambda e: e.tensor_tensor(out=qq[:, 0, :], in0=qq[:, 0, :], in1=qq[:, 3, :], op=ALU.add), r=["qq"], w=["qq"])
                P.dve(lambda e: e.tensor_tensor(out=qq[:, 0, :], in0=qq[:, 0, :], in1=rden, op=ALU.mult), r=["qq", "aT"], w=["qq"])
                P.dve(lambda e, i1=i1: e.tensor_tensor(out=qq[:, 1, :], in0=Et[:, 1, :, i1], in1=lr, op=ALU.mult), r=["Et", "aT", "qq"], w=["qq"])
                P.dve(lambda e: e.tensor_tensor(out=qq[:, 3, :], in0=qq[:, 2, :], in1=li, op=ALU.mult), r=["qq", "aT"], w=["qq"])
                P.dve(lambda e: e.tensor_tensor(out=qq[:, 1, :], in0=qq[:, 1, :], in1=qq[:, 3, :], op=ALU.subtract), r=["qq"], w=["qq"])
                P.dve(lambda e: e.tensor_tensor(out=qq[:, 1, :], in0=qq[:, 1, :], in1=rden, op=ALU.mult), r=["qq", "aT"], w=["qq"])
                for k, nm in enumerate(("ssm_b_re", "ssm_b_im")):
                    for dup in range(2):
                        P.dma("sp", lambda e, k=k, nm=nm, dup=dup, b=b: e.dma_start(
                            out=self.braw[dup * 64:(dup + 1) * 64, k, :, :],
                            in_=W[nm][l, b * 8:(b + 1) * 8].rearrange("g p c -> p g c")), w=["braw"])
                for k, nm in enumerate(("ssm_c_re", "ssm_c_im")):
                    for dup in range(2):
                        P.dma("sp", lambda e, k=k, nm=nm, dup=dup, b=b, d=d: e.dma_start(
                            out=self.craw[:, k, dup, :], in_=W[nm][l, d, b * 8:(b + 1) * 8].rearrange("g c p -> (g c) p")), w=["craw"])
                for k in range(2):
                    ps, pt = self.next_f()
                    P.pe(lambda e, k=k, ps=ps: e.transpose(out=ps[:, 0:128], in_=self.craw[:, k, :, :], identity=self.ident_f[:]),
                         r=["craw", "ident_f"], w=[pt])
                    P.act(lambda e, k=k, ps=ps: e.copy(out=self.cT[:, k, :, :], in_=ps[:, 0:128].rearrange("p (g c) -> p g c", c=16)), r=[pt], w=["cT"])
                BB = self.BBs
                qrb = qq[:, 0, :].unsqueeze(2).to_broadcast([128, 8, 16])
                qib = qq[:, 1, :].unsqueeze(2).to_broadcast([128, 8, 16])
                sc = self.t4
                s_bbr, s_bbi, s_t = sc[:, 0, 0], sc[:, 0, 1], sc[:, 0, 2]
                P.dve(lambda e: e.tensor_tensor(out=s_bbr, in0=self.braw[:, 0], in1=qrb, op=ALU.mult), r=["braw", "qq"], w=["t4"])
                P.dve(lambda e: e.tensor_tensor(out=s_t, in0=self.braw[:, 1], in1=qib, op=ALU.mult), r=["braw", "qq", "t4"], w=["t4"])
                P.dve(lambda e: e.tensor_tensor(out=s_bbr, in0=s_bbr, in1=s_t, op=ALU.subtract), r=["t4"], w=["t4"])
                P.dve(lambda e: e.tensor_tensor(out=s_bbi, in0=self.braw[:, 1], in1=qrb, op=ALU.mult), r=["braw", "qq", "t4"], w=["t4"])
                P.dve(lambda e: e.tensor_tensor(out=s_t, in0=self.braw[:, 0], in1=qib, op=ALU.mult), r=["braw", "qq", "t4"], w=["t4"])
                P.dve(lambda e: e.tensor_tensor(out=s_bbi, in0=s_bbi, in1=s_t, op=ALU.add), r=["t4"], w=["t4"])
                lo, hi = slice(0, 64), slice(64, 128)
                P.dve(lambda e: e.tensor_copy(out=BB[lo, 0], in_=s_bbr[lo]), r=["t4"], w=["BBs"])
                P.dve(lambda e: e.tensor_copy(out=BB[hi, 0], in_=s_bbi[hi]), r=["t4"], w=["BBs"])
                P.dve(lambda e: e.tensor_copy(out=BB[lo, 1], in_=s_bbi[lo]), r=["t4"], w=["BBs"])
                P.dve(lambda e: e.tensor_copy(out=BB[hi, 1], in_=s_bbr[hi]), r=["t4"], w=["BBs"])
                P.dve(lambda e: e.tensor_copy(out=BB[lo, 2], in_=self.cT[lo, 0]), r=["cT"], w=["BBs"])
                P.dve(lambda e: e.tensor_scalar(out=BB[hi, 2], in0=self.cT[hi, 1], scalar1=-1.0, scalar2=None, op0=ALU.mult), r=["cT"], w=["BBs"])
                P.dve(lambda e: e.tensor_scalar(out=BB[lo, 3], in0=self.cT[lo, 1], scalar1=-1.0, scalar2=None, op0=ALU.mult), r=["cT"], w=["BBs"])
                P.dve(lambda e: e.tensor_scalar(out=BB[hi, 3], in0=self.cT[hi, 0], scalar1=-1.0, scalar2=None, op0=ALU.mult), r=["cT"], w=["BBs"])

                def build(out, tab, X1, X2, esel, neg=False):
                    erb = Et[:, 0, :, tab * 8:(tab + 1) * 8].unsqueeze(3).to_broadcast([128, 8, 8, 16])
                    esb = Et[:, esel, :, tab * 8:(tab + 1) * 8].unsqueeze(3).to_broadcast([128, 8, 8, 16])
                    x1 = X1.unsqueeze(2).to_broadcast([128, 8, 8, 16])
                    x2 = X2.unsqueeze(2).to_broadcast([128, 8, 8, 16])
                    ov = out.rearrange("p g (s c) -> p g s c", c=16)
                    P.dve(lambda e: e.tensor_tensor(out=sc[:, 0], in0=x1, in1=erb, op=ALU.mult), r=["BBs", "Et", "t4"], w=["t4"])
                    P.dve(lambda e: e.tensor_tensor(out=sc[:, 1], in0=x2, in1=esb, op=ALU.mult), r=["BBs", "Et", "t4"], w=["t4"])
                    P.dve(lambda e: e.tensor_tensor(out=ov, in0=sc[:, 0], in1=sc[:, 1], op=(ALU.subtract if neg else ALU.add)),
                          r=["t4"], w=["S5tab"])
                build(self.QQ, 0, BB[:, 2], BB[:, 3], 1)
                build(self.KK, 1, BB[:, 0], BB[:, 1], 2)
                build(self.Mout[:, b * 8:(b + 1) * 8, :], 2, BB[:, 2], BB[:, 3], 1)
                build(self.MinT, 3, BB[:, 0], BB[:, 1], 2)
                build(self.MinTs, 3, BB[:, 1], BB[:, 0], 2, neg=True)
                mask = self.mf if d == 0 else self.mb
                mtok = "c_mf" if d == 0 else "c_mb"
                for gi in range(8):
                    g = b * 8 + gi
                    ps, pt = self.next_f()
                    P.pe(lambda e, gi=gi, ps=ps: e.matmul(ps[:, 0:128], lhsT=self.KK[:, gi, :], rhs=self.QQ[:, gi, :], start=True, stop=True),
                         r=["S5tab"], w=[pt])
                    P.dve(lambda e, gi=gi, ps=ps, mask=mask: e.tensor_tensor(out=self.Mintra[:, gi, :], in0=ps[:, 0:128], in1=mask[:], op=ALU.mult),
                          r=[pt, mtok], w=[f"Mintra{gi}"])
                pb, pbt = self.next_b()
                for gi in range(8):
                    P.pe(lambda e, gi=gi, pb=pb: e.transpose(out=pb[:, gi, :], in_=self.MinT[:, gi, :], identity=self.ident[:]),
                         r=["S5tab", "ident"], w=[pbt])
                P.act(lambda e, pb=pb: e.copy(out=self.Min[:], in_=pb[:]), r=[pbt], w=["Min"])
                pb, pbt = self.next_b()
                for gi in range(8):
                    P.pe(lambda e, gi=gi, pb=pb: e.transpose(out=pb[:, gi, :], in_=self.MinTs[:, gi, :], identity=self.ident[:]),
                         r=["S5tab", "ident"], w=[pbt])
                P.act(lambda e, pb=pb: e.copy(out=self.Mins[:], in_=pb[:]), r=[pbt], w=["Mins"])

                if pa.kind == "p":
                    for k, (tab, rt_) in enumerate(((self.Mintra, [f"Mintra{gi}" for gi in range(8)]), (self.Min, ["Min"]), (self.Mins, ["Mins"]),
                                                   (self.Mout[:, b * 8:(b + 1) * 8, :], ["S5tab", tokM]))):
                        P.dma("sp", lambda e, k=k, tab=tab: e.dma_start(out=self.s5c[l, d, b, k], in_=f2(tab)), r=rt_, w=[ctok + f"_{k}"])
                    if b == 3:
                        P.dma("sp", lambda e: e.dma_start(out=self.s5a[l, d], in_=self.A8[:, 0:2, :]), r=["A8"], w=[f"s5a{l}_{d}"])
            else:
                if b % 2 == 1:
                    Mi, Mn, Ms, sfx = self.KK, self.QQ, self.MinT, "B"
                for k, (tab, wt_) in enumerate(((Mi, [f"Mintra{sfx}{gi}" for gi in range(8)]), (Mn, ["Min" + sfx]), (Ms, ["Mins" + sfx]),
                                               (self.Mout[:, b * 8:(b + 1) * 8, :], ["S5tab", tokM]))):
                    P.dma("sp", lambda e, k=k, tab=tab: e.dma_start(out=f2(tab), in_=self.s5c[l, d, b, k]), r=[ctok + f"_{k}"], w=wt_)
                if b == 0:
                    P.dma("sp", lambda e: e.dma_start(out=self.A8[:, 0:2, :], in_=self.s5a[l, d]), r=[f"s5a{l}_{d}"], w=["A8"])
            for gi in range(8):
                g = b * 8 + gi
                ps, pt = self.next_f()
                P.pe(lambda e, gi=gi, ps=ps: e.matmul(ps[:, 0:J], lhsT=Mi[:, gi, :], rhs=Ubuf[:, gi, :], start=True, stop=True),
                     r=[f"Mintra{sfx}{gi}", ubt], w=[pt])
                P.pe(lambda e, gi=gi, ps=ps: e.matmul(ps[:, 128:128 + J], lhsT=Mn[:, gi, :], rhs=Ubuf[:, gi, :], start=True, stop=True),
                     r=["Min" + sfx, ubt], w=[pt])
                P.pe(lambda e, gi=gi, ps=ps: e.matmul(ps[:, 256:256 + J], lhsT=Ms[:, gi, :], rhs=Ubuf[:, gi, :], start=True, stop=True),
                     r=["Mins" + sfx, ubt], w=[pt])
                if d == 0:
                    P.act(lambda e, g=g, ps=ps: e.copy(out=self.Yacc[:, g, :], in_=ps[:, 0:J]), r=[pt], w=[f"Yacc{g}"])
                else:
                    P.dve(lambda e, g=g, ps=ps: e.tensor_tensor(out=self.Yacc[:, g, :], in0=self.Yacc[:, g, :], in1=ps[:, 0:J], op=ALU.add),
                          r=[pt, f"Yacc{g}"], w=[f"Yacc{g}"])
                P.act(lambda e, g=g, ps=ps: e.copy(out=self.Vv[:, g, :], in_=ps[:, 128:128 + J]), r=[pt], w=[tokV])
                P.act(lambda e, g=g, ps=ps: e.copy(out=self.Vsw[:, g, :], in_=ps[:, 256:256 + J]), r=[pt], w=[tokW])

        if self.stop_after == "s5a":
            return
        if self.stop_after == "s5b":
            do_batch(0, 0)
            f2 = lambda v: v.rearrange("p a b -> p (a b)")
            self.dump_view("aT", self.aT[:, :, 0:64].rearrange("p a b -> p (a b)"), 384, ["aT"])
            self.dump_view("ER", f2(self.Et[:, 0]), 256, ["Et"])
            self.dump_view("EI", f2(self.Et[:, 1]), 256, ["Et"])
            self.dump_view("qq", f2(self.qq[:, 0:2, :]), 16, ["qq"])
            self.dump_view("BB1", f2(self.BBs[:, 0]), 128, ["BBs"])
            self.dump_view("BB2", f2(self.BBs[:, 1]), 128, ["BBs"])
            self.dump_view("CC1", f2(self.BBs[:, 2]), 128, ["BBs"])
            self.dump_view("CC2", f2(self.BBs[:, 3]), 128, ["BBs"])
            self.dump_view("KK", f2(self.KK), 1024, ["S5tab"])
            self.dump_view("QQ", f2(self.QQ), 1024, ["S5tab"])
            self.dump_view("MinT", f2(self.MinT), 1024, ["S5tab"])
            self.dump_view("Mout", f2(self.Mout[:, 0:8, :]), 1024, ["S5tab"])
            self.dump_view("Mintra", f2(self.Mintra), 1024, [f"Mintra{g}" for g in range(8)])
            self.dump_view("Min", f2(self.Min), 1024, ["Min"])
            self.dump_view(tokV, f2(self.Vv[:, 0:8, :]), 8 * J, [tokV])
            self.dump_view("Yacc", f2(self.Yacc[:, 0:8, :]), 8 * J, [f"Yacc{g}" for g in range(8)])
            return
        for d in range(2):
            for b in range(4):
                if self.stop_after == "s5b" and (d, b) != (0, 0):
                    return
                do_batch(d, b)
            if self.stop_after == "s5c":
                return
            RE = P.dve
            A8 = self.A8
            S = 16 if Jps == 128 else 8
            G = Jps // S
            NL = NSEQ * G
            RE(lambda e: e.tensor_scalar(out=A8[:, 2, :], in0=A8[:, 1, :], scalar1=-1.0, scalar2=None, op0=ALU.mult), r=["A8"], w=["A8"])
            lst = 0 if d == 0 else (1 if S == 16 else 2)
            MAGIC = 12582912.0
            PI_LO = 3.1415925
            for c4 in range(4):
                gs4 = slice(d * 32 + c4 * 8, d * 32 + c4 * 8 + 8)
                n = S + 1
                lrb = self.aT[:, 3, gs4].unsqueeze(2).to_broadcast([128, 8, n])
                lib = self.aT[:, 4, gs4].unsqueeze(2).to_broadcast([128, 8, n])
                evb8 = self.ev8[:, lst, 0:n].unsqueeze(1).to_broadcast([128, 8, n])
                W0, W1, W2, W3 = (self.PWt[:, k, :, 0:n] for k in range(4))
                RE(lambda e: e.tensor_tensor(out=W0, in0=lrb, in1=evb8, op=ALU.mult), r=["aT", "c_ev8"], w=["PWt"])
                P.act(lambda e: e.activation(out=W0, in_=W0, func=AF.Exp), r=["PWt"], w=["PWt"])
                RE(lambda e: e.tensor_tensor(out=W1, in0=lib, in1=evb8, op=ALU.mult), r=["aT", "c_ev8", "PWt"], w=["PWt"])
                RE(lambda e: e.tensor_scalar(out=W2, in0=W1, scalar1=0.5 * math.pi, scalar2=None, op0=ALU.add), r=["PWt"], w=["PWt"])
                for Wx in (W1, W2):
                    RE(lambda e, Wx=Wx: e.tensor_scalar(out=W3, in0=Wx, scalar1=1.0 / TWO_PI, scalar2=MAGIC, op0=ALU.mult, op1=ALU.add), r=["PWt"], w=["PWt"])
                    RE(lambda e: e.tensor_scalar(out=W3, in0=W3, scalar1=-MAGIC, scalar2=None, op0=ALU.add), r=["PWt"], w=["PWt"])
                    RE(lambda e, Wx=Wx: e.scalar_tensor_tensor(out=Wx, in0=W3, scalar=-TWO_PI, in1=Wx, op0=ALU.mult, op1=ALU.add), r=["PWt"], w=["PWt"])
                    RE(lambda e, Wx=Wx: e.tensor_scalar(out=Wx, in0=Wx, scalar1=PI_LO, scalar2=None, op0=ALU.min), r=["PWt"], w=["PWt"])
                    RE(lambda e, Wx=Wx: e.tensor_scalar(out=Wx, in0=Wx, scalar1=-PI_LO, scalar2=None, op0=ALU.max), r=["PWt"], w=["PWt"])
                    P.act(lambda e, Wx=Wx: e.activation(out=Wx, in_=Wx, func=AF.Sin), r=["PWt"], w=["PWt"])
                gq = slice(c4 * 8, c4 * 8 + 8)
                RE(lambda e: e.tensor_tensor(out=self.PRt[:, gq, 0:n], in0=W2, in1=W0, op=ALU.mult), r=["PWt"], w=["PRt"])
                RE(lambda e: e.tensor_tensor(out=W1, in0=W1, in1=W0, op=ALU.mult), r=["PWt"], w=["PWt"])
                RE(lambda e: e.tensor_scalar(out=self.PIt[:, gq, 0:n], in0=W1, scalar1=self.sgn[:, 0:1], scalar2=None, op0=ALU.mult),
                   r=["PWt", "c_sgn"], w=["PIt"])
            AS = self.AS
            RE(lambda e: e.tensor_copy(out=AS[:, 0, :], in_=self.PRt[:, :, S]), r=["PRt"], w=["AS"])
            RE(lambda e: e.tensor_copy(out=AS[:, 1, :], in_=self.PIt[:, :, S]), r=["PIt"], w=["AS"])
            RE(lambda e: e.tensor_scalar(out=AS[:, 2, :], in0=AS[:, 1, :], scalar1=-1.0, scalar2=None, op0=ALU.mult), r=["AS"], w=["AS"])

            def cstep(X, Xw, Xn, Xwn, tk, vj, vwj, vtoks, Atab, atok, nl, store=None, stok=None, rts=None):
                xt, xwt, xnt, xwnt = tk
                Ab = [Atab[:, k, :].unsqueeze(2).to_broadcast([128, 32, nl]) for k in range(3)]
                r0, r1, r2, r3 = rts
                RE(lambda e: e.tensor_tensor(out=r0, in0=X, in1=Ab[0], op=ALU.mult), r=[xt, atok], w=["rt0"])
                RE(lambda e: e.tensor_tensor(out=r1, in0=Xw, in1=Ab[1], op=ALU.mult), r=[xwt, atok], w=["rt1"])
                RE(lambda e: e.tensor_tensor(out=r2, in0=Xw, in1=Ab[0], op=ALU.mult), r=[xwt, atok], w=["rt2"])
                RE(lambda e: e.tensor_tensor(out=r3, in0=X, in1=Ab[2], op=ALU.mult), r=[xt, atok], w=["rt3"])
                RE(lambda e: e.tensor_tensor(out=r0, in0=r0, in1=vj, op=ALU.add), r=["rt0"] + vtoks, w=["rt0"])
                RE(lambda e: e.tensor_tensor(out=r2, in0=r2, in1=vwj, op=ALU.add), r=["rt2"] + vtoks, w=["rt2"])
                if store is not None:
                    RE(lambda e: e.tensor_copy(out=store, in_=X), r=[xt, "rt0"], w=[stok])
                RE(lambda e: e.tensor_tensor(out=Xn, in0=r0, in1=r1, op=ALU.add), r=["rt0", "rt1"], w=[xnt])
                RE(lambda e: e.tensor_tensor(out=Xwn, in0=r2, in1=r3, op=ALU.add), r=["rt2", "rt3"], w=[xwnt])

            RE(lambda e: e.memset(self.Xs[0], 0.0), w=["X0"])
            RE(lambda e: e.memset(self.Xw[0], 0.0), w=["Xw0"])
            cur = 0
            for st in range(S):
                blk = st if d == 0 else S - 1 - st
                vj = self.Vv[:, :, blk:J:S]
                vwj = self.Vsw[:, :, blk:J:S]
                cstep(self.Xs[cur], self.Xw[cur], self.Xs[1 - cur], self.Xw[1 - cur],
                      (f"X{cur}", f"Xw{cur}", f"X{1 - cur}", f"Xw{1 - cur}"), vj, vwj, [tokV, tokW], A8, "A8", NL,
                      store=vj, stok=tokV, rts=self.rt)
                cur = 1 - cur
            Xend, Xwend = self.Xs[cur], self.Xw[cur]
            xet, xwet = f"X{cur}", f"Xw{cur}"
            XendV = Xend.rearrange("p g (q s) -> p g q s", s=G)
            XwendV = Xwend.rearrange("p g (q s) -> p g q s", s=G)
            XinV = self.Xin[:, :, 0:NL].rearrange("p g (q s) -> p g q s", s=G)
            XinwV = self.Xinw[:, :, 0:NL].rearrange("p g (q s) -> p g q s", s=G)
            C0, Cw0, C1, Cw1 = self.Cs
            if pa.kind == "s":
                for half in range(2):
                    hs = slice(half * 64, half * 64 + 64)
                    hw = slice((1 - half) * 64, (1 - half) * 64 + 64)
                    P.dma("sp", lambda e, half=half, hs=hs: e.dma_start(out=C0[hs, :, 0], in_=self.sssm[l, d, :, :, half].rearrange("g p -> p g"),
                                                                        allow_slow_non_contiguous=True), w=["C0"])
                    P.dma("sp", lambda e, half=half, hw=hw: e.dma_start(out=Cw0[hw, :, 0], in_=self.sssm[l, d, :, :, half].rearrange("g p -> p g"),
                                                                        allow_slow_non_contiguous=True), w=["Cw0"])
            else:
                RE(lambda e: e.memset(C0, 0.0), w=["C0"])
                RE(lambda e: e.memset(Cw0, 0.0), w=["Cw0"])
            cs_ = [(C0, Cw0, "C0", "Cw0"), (C1, Cw1, "C1", "Cw1")]
            cc = 0
            rts_c = [r[:, :, 0:NSEQ] for r in self.rt]
            for g_ in (range(G) if d == 0 else range(G - 1, -1, -1)):
                Cc, Cwc, ct, cwt = cs_[cc]
                Cn, Cwn, cnt, cwnt = cs_[1 - cc]
                RE(lambda e, Cc=Cc, g_=g_: e.tensor_copy(out=XinV[:, :, :, g_], in_=Cc), r=[ct], w=["Xin"])
                RE(lambda e, Cwc=Cwc, g_=g_: e.tensor_copy(out=XinwV[:, :, :, g_], in_=Cwc), r=[cwt], w=["Xinw"])
                cstep(Cc, Cwc, Cn, Cwn, (ct, cwt, cnt, cwnt), XendV[:, :, :, g_], XwendV[:, :, :, g_], [xet, xwet], AS, "AS", NSEQ, rts=rts_c)
                cc = 1 - cc
            Cfin, cft = cs_[cc][0], cs_[cc][2]
            if pa.kind == "p":
                for sq in range(NSEQ):
                    for half in range(2):
                        P.dma("sp", lambda e, sq=sq, half=half: e.dma_start(
                            out=self.nss[sq, l, d, :, :, half].rearrange("g p -> p g"), in_=Cfin[half * 64:(half + 1) * 64, :, sq],
                            allow_slow_non_contiguous=True), r=[cft], w=["nss"])
            tc = self.tcor[:, :, 0:S]
            for lane in range(NL):
                xh = self.Vv[:, :, lane * S:(lane + 1) * S]
                xb = self.Xin[:, :, lane:lane + 1].to_broadcast([128, 32, S])
                xwb = self.Xinw[:, :, lane:lane + 1].to_broadcast([128, 32, S])
                RE(lambda e, xb=xb: e.tensor_tensor(out=tc, in0=self.PRt[:, :, 0:S], in1=xb, op=ALU.mult), r=["PRt", "Xin"], w=["tcor"])
                RE(lambda e, xh=xh: e.tensor_tensor(out=xh, in0=xh, in1=tc, op=ALU.add), r=["tcor", tokV], w=[tokV])
                RE(lambda e, xwb=xwb: e.tensor_tensor(out=tc, in0=self.PIt[:, :, 0:S], in1=xwb, op=ALU.mult), r=["PIt", "Xinw", tokV], w=["tcor"])
                RE(lambda e, xh=xh: e.tensor_tensor(out=xh, in0=xh, in1=tc, op=ALU.add), r=["tcor", tokV], w=[tokV])
            if self.stop_after == "s5d":
                return
            for g in range(32):
                ps, pt = self.next_f()
                P.pe(lambda e, g=g, ps=ps: e.matmul(ps[:, 0:J], lhsT=self.Mout[:, g, :], rhs=self.Vv[:, g, :], start=True, stop=True),
                     r=["S5tab", tokV, tokM], w=[pt])
                P.dve(lambda e, g=g, ps=ps: e.tensor_tensor(out=self.Yacc[:, g, :], in0=self.Yacc[:, g, :], in1=ps[:, 0:J], op=ALU.add),
                      r=[pt, f"Yacc{g}"], w=[f"Yacc{g}"])
            if self.stop_after == "s5e":
                f2 = lambda v: v.rearrange("p a b -> p (a b)")
                self.dump_view("Xh", f2(self.Vv[:, 0:8, :]), 8 * J, [tokV])
                self.dump_view("Yf", f2(self.Yacc[:, 0:8, :]), 8 * J, [f"Yacc{g}" for g in range(8)])
                self.dump_view("A8", f2(self.A8[:, :, :]), 96, ["A8"])
                return
        if self.stop_after == "s5f":
            return
        self.P.barrier()
        ybm = self.av_at(self.s5_ymark, [128, 8, 512], F32)
        self.ygT = self.av_at(self.aoff_after, [128, 4, 1024], BF16)
        for gq in range(8):
            ps, pt = self.next_f()
            for k in range(4):
                g = gq * 4 + k
                P.pe(lambda e, g=g, k=k, ps=ps: e.transpose(out=ps[0:J, k * 128:(k + 1) * 128], in_=self.Yacc[:, g, :], identity=self.ident_f[:]),
                     r=[f"Yacc{g}", "ident_f"], w=[pt])
            for k in range(4):
                g = gq * 4 + k
                P.act(lambda e, g=g, k=k, ps=ps: e.copy(out=ybm[0:J, :, g * 16:(g + 1) * 16],
                                                        in_=ps[0:J, k * 128:(k + 1) * 128].rearrange("p (s c) -> p s c", c=16)),
                      r=[pt], w=["ybm"])
        if "s5y" in self.debug and l == 0:
            o = self.dbg_out(f"{pa.kind}_ybm", [128, 4096])
            P.dma("sp", lambda e, o=o: e.dma_start(out=o[0:J, :], in_=ybm[0:J, :, :].rearrange("p a b -> p (a b)")), r=["ybm"], w=["dbg"])
            o2 = self.dbg_out(f"{pa.kind}_subm", [128, 4096])
            P.dve(lambda e: e.tensor_copy(out=self.t1024[0][0:J, :], in_=self.su_bm[0:J, 0:8, :, :].rearrange("p g a b -> p (g a b)")), r=["su_bm"], w=["t1024_0"])
            P.dma("sp", lambda e, o2=o2: e.dma_start(out=o2[0:J, 0:1024], in_=self.t1024[0][0:J, :]), r=["t1024_0"], w=["dbg"])
        for t0 in range(8):
            t = self.t1024[0]
            P.dve(lambda e, t0=t0, t=t: e.tensor_tensor(out=t[0:J, 0:512].rearrange("p (g c) -> p g c", c=16), in0=self.su_bm[0:J, :, t0, :], in1=self.dbc[0:J, :].rearrange("p (g c) -> p g c", c=16), op=ALU.mult),
                  r=["su_bm", "dbc"], w=["t1024_0"])
            P.dve(lambda e, t0=t0, t=t: e.tensor_tensor(out=t[0:J, 0:512], in0=t[0:J, 0:512], in1=ybm[0:J, t0, :], op=ALU.add),
                  r=["t1024_0", "ybm"], w=["t1024_0"])
            hb = self.hb[t0 % 2]
            hbt = f"hb{t0 % 2}"
            P.act(lambda e, t=t, hb=hb: e.activation(out=hb[0:J, 0:512], in_=t[0:J, 0:512], func=AF.Gelu_apprx_tanh), r=["t1024_0"], w=[hbt])
            pb, pbt = self.next_b()
            for c in range(4):
                P.pe(lambda e, c=c, hb=hb, pb=pb: e.transpose(out=pb[:, c, 0:J], in_=hb[0:J, c * 128:(c + 1) * 128], identity=self.ident[0:J, 0:J]),
                     r=[hbt, "ident"], w=[pbt])
            P.act(lambda e, pb=pb, t0=t0: e.copy(out=self.ygT[:, :, t0:pa.T:8], in_=pb[:, 0:4, 0:J]), r=[pbt], w=["ygT"])
        if self.stop_after == "s5g":
            return
        (slot, wt), = list(self.wstream([(W["ssm_w_glu"][l], 4, 512)]))
        for cc in range(4):
            for th in range(pa.NH):
                cols = slice(th * 512, (th + 1) * 512)
                ps, pt = self.next_f()
                for k in range(4):
                    P.pe(lambda e, k=k, cc=cc, ps=ps, cols=cols: e.matmul(ps[:], lhsT=slot[:, k, cc * 128:(cc + 1) * 128], rhs=self.ygT[:, k, cols],
                                                                         start=(k == 0), stop=(k == 3)), r=[wt, "ygT"], w=[pt])
                t = self.t1024[0]
                P.act(lambda e, ps=ps, t=t: e.activation(out=t[:, 0:512], in_=ps[:], func=AF.Sigmoid), r=[pt], w=["t1024_0"])
                P.dve(lambda e, cc=cc, cols=cols, t=t: e.tensor_tensor(out=self.soutT[:, cc, cols], in0=t[:, 0:512], in1=self.ygT[:, cc, cols], op=ALU.mult),
                      r=["t1024_0", "ygT"], w=[f"soutT{cc}_{th}"])

    def dump_view(self, name, view, n, toks):
        P = self.P
        o = self.dbg_out(name, [128, 1024])
        t = self.t1024[0]
        P.dve(lambda e: e.tensor_copy(out=t[:, 0:n], in_=view), r=list(toks), w=["t1024_0"])
        P.dma("sp", lambda e: e.dma_start(out=o[:, 0:n], in_=t[:, 0:n]), r=["t1024_0"], w=["dbg"])

    def dump_x(self, pa, nm):
        o = self.dbg_out(f"{pa.kind}_{nm}", [pa.T, 1024])
        for i in range(pa.NT):
            self.P.dma("sp", lambda e, i=i: e.dma_start(out=o[i * 128:(i + 1) * 128, :], in_=self.xres[i][:]), r=[f"x{i}"], w=["dbg"])

    def av_at(self, off, shape, dtype):
        save = self.aoff
        self.aoff = off
        v = self.av(shape, dtype)
        self.aoff_after = self.aoff
        self.aoff = save
        return v


    def alloc_attn(self, pa):
        av = self.av
        self.nqT = av([128, 4, 1024], BF16)
        self.nkT = av([128, 4, 1024], BF16)
        self.nv = av([128, 8, 512], BF16)
        self.nout_tm = [av([128, 512], BF16) for _ in range(2)]
        self.asm = av([128, 16], F32)
        if pa.kind == "p":
            self.stg = [av([128, 512], F32) for _ in range(2)]
            self.Pb = [av([128, 256], BF16) for _ in range(2)]
            self.PTt = [av([128, 2, 128], BF16) for _ in range(2)]
        else:
            self.kctxT = av([128, 4, 512], BF16)
            self.vctx = av([128, 4, 512], BF16)
            self.Tpad = av([128, 8, 19, 64], BF16)
            self.scl = [av([128, 640], F32) for _ in range(2)]
            self.Pb = [av([128, 1152], BF16) for _ in range(2)]
            self.PTt = [av([128, 9, 128], BF16) for _ in range(2)]
            self.rowm = [av([128, 640], BF16) for _ in range(2)]

    def attn_proj(self, pa, l):
        P = self.P
        W = self.W
        w_in = W["w_in"][l]
        specs = [(w_in[:, 2560 + b * 512:2560 + (b + 1) * 512], 8, 512) for b in range(3)]
        ws = self.wstream(specs)
        slot, wt = next(ws)
        for j in range(4):
            def cq(ps, pt, th, j=j):
                cols = slice(th * 512, (th + 1) * 512)
                P.act(lambda e: e.copy(out=self.nqT[:, j, cols], in_=ps[:]), r=[pt], w=[f"nqT{j}_{th}"])
            self.proj_feat(pa, slot, wt, j, cq)
        slot, wt = next(ws)
        if pa.kind == "s":
            for j in range(4):
                def ck_(ps, pt, th, j=j):
                    cols = slice(th * 512, (th + 1) * 512)
                    P.act(lambda e: e.copy(out=self.nkT[:, j, cols], in_=ps[:]), r=[pt], w=[f"nkT{j}_{th}"])
                self.proj_feat(pa, slot, wt, j, ck_)
        else:
            def ck_(ps, pt, i):
                sq, ti = divmod(i, pa.L // 128)
                stg = self.stg[i % 2]
                st = f"stg{i % 2}"
                P.act(lambda e: e.copy(out=stg[:], in_=ps[:]), r=[pt], w=[st])
                P.dma("sp", lambda e: e.dma_start(out=self.nck[sq, l, ti * 128:(ti + 1) * 128, :], in_=stg[:]), r=[st], w=["nck"])
                hb = self.hb[i % 2]
                hbt = f"hb{i % 2}"
                P.dve(lambda e: e.tensor_copy(out=hb[:, 0:512], in_=ps[:]), r=[pt], w=[hbt])
                pb, pbt = self.next_b()
                for c in range(4):
                    P.pe(lambda e, c=c: e.transpose(out=pb[:, c, :], in_=hb[:, c * 128:(c + 1) * 128], identity=self.ident[:]),
                         r=[hbt, "ident"], w=[pbt])
                P.act(lambda e: e.copy(out=self.nkT[:, :, i * 128:(i + 1) * 128], in_=pb[:, 0:4, :]), r=[pbt], w=[f"nkT_t{i}"])
            self.proj_tok(pa, slot, wt, ck_)
        slot, wt = next(ws)

        def cv_(ps, pt, i):
            if pa.kind == "p":
                sq, ti = divmod(i, pa.L // 128)
                stg = self.stg[i % 2]
                st = f"stg{i % 2}"
                P.act(lambda e: e.copy(out=stg[:], in_=ps[:]), r=[pt], w=[st])
                P.dma("sp", lambda e: e.dma_start(out=self.ncv[sq, l, ti * 128:(ti + 1) * 128, :], in_=stg[:]), r=[st], w=["ncv"])
            P.dve(lambda e: e.tensor_copy(out=self.nv[:, i, :], in_=ps[:]), r=[pt], w=[f"nv{i}"])
        self.proj_tok(pa, slot, wt, cv_)
        for _ in ws:
            pass

    def finish_nout(self, ti, nout, nt):
        P = self.P
        pb, pbt = self.next_b()
        for c in range(4):
            P.pe(lambda e, c=c: e.transpose(out=pb[:, c, :], in_=nout[:, c * 128:(c + 1) * 128], identity=self.ident[:]),
                 r=[nt, "ident"], w=[pbt])
        P.act(lambda e: e.copy(out=self.noutT[:, :, ti * 128:(ti + 1) * 128], in_=pb[:, 0:4, :]), r=[pbt], w=[f"noutT{ti}"])

    def ctx_attention(self, pa, l):
        P = self.P
        SC = 64.0 ** -0.5
        TPS = pa.L // 128
        def unit(sq, qi):
            ti = sq * TPS + qi
            qtok = slice(ti * 128, (ti + 1) * 128)
            ktok = slice(sq * pa.L, (sq + 1) * pa.L)
            po, pot = self.next_f()
            nout = self.nout_tm[ti % 2]
            nt = f"nout{ti % 2}"
            def head(h):
                ch, hp = h // 2, h % 2
                prt = slice(hp * 64, hp * 64 + 64)
                ps, pt = self.next_f()
                P.pe(lambda e, ps=ps, ch=ch, prt=prt: e.matmul(ps[:, 0:pa.L], lhsT=self.nqT[prt, ch, qtok], rhs=self.nkT[prt, ch, ktok],
                                                               start=True, stop=True),
                     r=[f"nqT{ch}_{ti // 4}"] + [f"nkT_t{k}" for k in range(sq * TPS, (sq + 1) * TPS)], w=[pt])
                a = self.asm
                P.dve(lambda e, ps=ps: e.reduce_max(out=a[:, 0:1], in_=ps[:, 0:pa.L], axis=AX.X), r=[pt], w=["asm"])
                P.dve(lambda e: e.tensor_scalar(out=a[:, 1:2], in0=a[:, 0:1], scalar1=-SC, scalar2=None, op0=ALU.mult), r=["asm"], w=["asm"])
                Pb = self.Pb[h % 2]
                pbk = f"Pb{h % 2}"
                P.act(lambda e, ps=ps, Pb=Pb: e.activation(out=Pb[:, 0:pa.L], in_=ps[:, 0:pa.L], func=AF.Exp, bias=a[:, 1:2], scale=SC,
                                                           accum_out=a[:, 2:3]), r=[pt, "asm"], w=[pbk, "asm"])
                pb, pbt = self.next_b()
                for kt in range(TPS):
                    P.pe(lambda e, kt=kt, Pb=Pb, pb=pb: e.transpose(out=pb[:, kt, :], in_=Pb[:, kt * 128:(kt + 1) * 128], identity=self.ident[:]),
                         r=[pbk, "ident"], w=[pbt])
                PTt = self.PTt[h % 2]
                ptk = f"PTt{h % 2}"
                P.act(lambda e, pb=pb, PTt=PTt: e.copy(out=PTt[:, 0:TPS, :], in_=pb[:, 0:TPS, :]), r=[pbt], w=[ptk])
                for kt in range(TPS):
                    P.pe(lambda e, kt=kt, PTt=PTt, h=h, po=po: e.matmul(po[:, h * 64:(h + 1) * 64], lhsT=PTt[:, kt, :],
                                                                      rhs=self.nv[:, sq * TPS + kt, h * 64:(h + 1) * 64],
                                                                      start=(kt == 0), stop=(kt == TPS - 1)),
                         r=[ptk] + [f"nv{sq * TPS + kt}"], w=[pot])
                P.dve(lambda e: e.reciprocal(out=a[:, 3:4], in_=a[:, 2:3]), r=["asm"], w=["asm"])
                P.dve(lambda e, h=h, po=po, nout=nout: e.tensor_scalar(out=nout[:, h * 64:(h + 1) * 64], in0=po[:, h * 64:(h + 1) * 64],
                                                                      scalar1=a[:, 3:4], scalar2=None, op0=ALU.mult),
                      r=[pot, "asm"], w=[nt])

            for h in range(8):
                head(h)
            self.finish_nout(ti, nout, nt)

        for sq in range(pa.n_seq):
            for qi in range(TPS):
                unit(sq, qi)

    def na_attention(self, pa, l):
        P = self.P
        SC = 64.0 ** -0.5
        for hh in range(2):
            P.dma("pool", lambda e, hh=hh: e.dma_start(out=self.hb[hh][:, :].rearrange("p (t n) -> p t n", t=2),
                                                       in_=self.ck[l, hh * 256:(hh + 1) * 256, :].rearrange("(t p) n -> p t n", p=128)), w=[f"hb{hh}"])
        P.dma("pool", lambda e: e.dma_start(out=self.vctx[:], in_=self.cv[l].rearrange("(t p) n -> p t n", p=128)), w=["vctx"])
        P.dma("pool", lambda e: e.dma_start(out=self.Tpad[:], in_=self.rpbt[l]), w=["Tpad"])
        for t in range(4):
            pb, pbt = self.next_b()
            for c in range(4):
                P.pe(lambda e, c=c, t=t, pb=pb: e.transpose(out=pb[:, c, :], in_=self.hb[t // 2][:, (t % 2) * 512 + c * 128:(t % 2) * 512 + (c + 1) * 128],
                                                            identity=self.ident[:]), r=[f"hb{t // 2}", "ident"], w=[pbt])
            P.act(lambda e, t=t, pb=pb: e.copy(out=self.kctxT[:, :, t * 128:(t + 1) * 128], in_=pb[:, 0:4, :]), r=[pbt], w=["kctxT"])
        saved_psb = self.psb
        sets = [((self.psf[0], "psf0"), (self.psf[1], "psf1"), (self.psf[2], "psf2")),
                ((self.psf[3], "psf3"), (self.psf[4], "psf4"),
                 (saved_psb[2][:, :, :].rearrange("p a b -> p (a b)").bitcast(F32), "psb2"))]
        self.psb = saved_psb[0:2]
        self.nb = 0
        units = [(i, h) for i in range(8) for h in range(8)]
        a = self.asm

        def geom(i):
            ust = NA_UST[i]
            return ust, ust - 2 * i + 7, ust * 64, (ust * 64) // 128

        def stageA(u):
            i, h = units[u]
            ust, base, w0, wt0 = geom(i)
            (bA, tA), (bB, tB), (bC, tC) = sets[u % 2]
            qtok = slice(i * 128, (i + 1) * 128)
            rowm = self.rowm[i % 2]
            rmt = f"rowm{i % 2}"
            if h == 0:
                P.dma("pool", lambda e: e.dma_start(out=rowm[0:2, :], in_=self.C["rowm"][i]), w=[rmt])
            ch, hp = h // 2, h % 2
            prt = slice(hp * 64, hp * 64 + 64)
            qr = [f"nqT{ch}_{i // 4}"]
            kr = [f"nkT{ch}_{t}" for t in range(2)]
            P.pe(lambda e: e.matmul(bA[:, 0:512], lhsT=self.nqT[prt, ch, qtok], rhs=self.nkT[prt, ch, w0:w0 + 512], start=True, stop=False),
                 r=qr + kr, w=[tA])
            P.pe(lambda e: e.matmul(bA[:, 0:512], lhsT=self.ind[0:2, :], rhs=rowm[0:2, 0:512], start=False, stop=True), r=["c_ind", rmt], w=[tA])
            P.pe(lambda e: e.matmul(bB[:, 0:128], lhsT=self.nqT[prt, ch, qtok], rhs=self.nkT[prt, ch, w0 + 512:w0 + 640], start=True, stop=False),
                 r=qr + kr, w=[tB])
            P.pe(lambda e: e.matmul(bB[:, 0:128], lhsT=self.ind[0:2, :], rhs=rowm[0:2, 512:640], start=False, stop=True), r=["c_ind", rmt], w=[tB])
            P.pe(lambda e: e.matmul(bC[:, 0:512], lhsT=self.nqT[prt, ch, qtok], rhs=self.kctxT[prt, ch, :], start=True, stop=True),
                 r=qr + ["kctxT"], w=[tC])

        def stageBC(u):
            i, h = units[u]
            ust, base, w0, wt0 = geom(i)
            (bA, tA), (bB, tB), (bC, tC) = sets[u % 2]
            scl, sct = self.scl[u % 2], f"scl{u % 2}"
            Pb, pbk = self.Pb[u % 2], f"Pb{u % 2}"
            PTt, ptk = self.PTt[u % 2], f"PTt{u % 2}"
            nout, nt = self.nout_tm[i % 2], f"nout{i % 2}"
            as_ = a[:, (u % 2) * 8:(u % 2) * 8 + 8]
            ast = f"asm{u % 2}"
            e0 = base + 2
            P.dve(lambda e: e.scalar_tensor_tensor(out=scl[:, 0:512], in0=bA[:, 0:512], scalar=SC,
                                                   in1=self.Tpad[:, h, e0:e0 + 8, :].rearrange("p a b -> p (a b)"), op0=ALU.mult, op1=ALU.add),
                  r=[tA, "Tpad"], w=[sct])
            P.dve(lambda e: e.scalar_tensor_tensor(out=scl[:, 512:640], in0=bB[:, 0:128], scalar=SC,
                                                   in1=self.Tpad[:, h, e0 + 8:e0 + 10, :].rearrange("p a b -> p (a b)"), op0=ALU.mult, op1=ALU.add),
                  r=[tB, "Tpad", sct], w=[sct])
            P.dve(lambda e: e.reduce_max(out=as_[:, 0:1], in_=scl[:], axis=AX.X), r=[sct], w=[ast])
            P.dve(lambda e: e.reduce_max(out=as_[:, 1:2], in_=bC[:, 0:512], axis=AX.X), r=[tC, ast], w=[ast])
            P.dve(lambda e: e.scalar_tensor_tensor(out=as_[:, 2:3], in0=as_[:, 1:2], scalar=SC, in1=as_[:, 0:1], op0=ALU.mult, op1=ALU.max),
                  r=[ast], w=[ast])
            P.dve(lambda e: e.tensor_scalar(out=as_[:, 3:4], in0=as_[:, 2:3], scalar1=-1.0, scalar2=None, op0=ALU.mult), r=[ast], w=[ast])
            P.act(lambda e: e.activation(out=Pb[:, 0:640], in_=scl[:], func=AF.Exp, bias=as_[:, 3:4], scale=1.0, accum_out=as_[:, 4:5]),
                  r=[sct, ast], w=[pbk, ast])
            P.act(lambda e: e.activation(out=Pb[:, 640:1152], in_=bC[:, 0:512], func=AF.Exp, bias=as_[:, 3:4], scale=SC, accum_out=as_[:, 5:6]),
                  r=[tC, ast, pbk], w=[pbk, ast])
            pb, pbt = self.next_b()
            for kt in range(8):
                P.pe(lambda e, kt=kt: e.transpose(out=pb[:, kt, :], in_=Pb[:, kt * 128:(kt + 1) * 128], identity=self.ident[:]),
                     r=[pbk, "ident"], w=[pbt])
            P.act(lambda e: e.copy(out=PTt[:, 0:8, :], in_=pb[:]), r=[pbt], w=[ptk])
            pb9, pb9t = self.next_b()
            P.pe(lambda e: e.transpose(out=pb9[:, 0, :], in_=Pb[:, 1024:1152], identity=self.ident[:]), r=[pbk, "ident"], w=[pb9t])
            P.dve(lambda e: e.tensor_copy(out=PTt[:, 8, :], in_=pb9[:, 0, :]), r=[pb9t, ptk], w=[ptk])
            po = bB[:, 128:192]
            for kt in range(9):
                if kt < 5:
                    rhs, rt_ = self.nv[:, wt0 + kt, h * 64:(h + 1) * 64], f"nv{wt0 + kt}"
                else:
                    rhs, rt_ = self.vctx[:, kt - 5, h * 64:(h + 1) * 64], "vctx"
                P.pe(lambda e, kt=kt, rhs=rhs: e.matmul(po, lhsT=PTt[:, kt, :], rhs=rhs, start=(kt == 0), stop=(kt == 8)), r=[ptk, rt_], w=[tB])
            P.dve(lambda e: e.tensor_tensor(out=as_[:, 6:7], in0=as_[:, 4:5], in1=as_[:, 5:6], op=ALU.add), r=[ast], w=[ast])
            P.dve(lambda e: e.reciprocal(out=as_[:, 7:8], in_=as_[:, 6:7]), r=[ast], w=[ast])
            P.dve(lambda e: e.tensor_scalar(out=nout[:, h * 64:(h + 1) * 64], in0=po, scalar1=as_[:, 7:8], scalar2=None, op0=ALU.mult),
                  r=[tB, ast], w=[nt])
            if h == 7:
                self.finish_nout(i, nout, nt)

        stageA(0)
        for u in range(len(units)):
            if u + 1 < len(units):
                stageA(u + 1)
            stageBC(u)
        self.psb = saved_psb

    def layer_norm(self, pa, i, gname, bname, l):
        P = self.P
        x = self.xres[i]
        xt = f"x{i}"
        for k in range(2):
            P.dve(lambda e, k=k: e.bn_stats(out=self.lnst[:, k, 0:6], in_=x[:, k * 512:(k + 1) * 512]), r=[xt], w=["lnst"])
        P.dve(lambda e: e.bn_aggr(out=self.lnag[:, 0:2], in_=self.lnst[:, :, 0:6]), r=["lnst"], w=["lnag"])
        self.rstd(self.lnag[:, 1:2], "lnag")
        P.dve(lambda e: e.scalar_tensor_tensor(out=x[:], in0=x[:], scalar=self.lnag[:, 0:1], in1=self.lng[:], op0=ALU.subtract, op1=ALU.mult),
              r=[xt, "lnag", "lng"], w=[xt])
        P.dve(lambda e: e.scalar_tensor_tensor(out=x[:], in0=x[:], scalar=self.lnag[:, 1:2], in1=self.lnb[:], op0=ALU.mult, op1=ALU.add),
              r=[xt, "lnag", "lnb"], w=[xt])

    def load_ln(self, gname, bname, l):
        P = self.P
        P.dma("sp", lambda e: e.dma_start(out=self.lng[:], in_=self.W[gname][l].partition_broadcast(128)), w=["lng"])
        P.dma("sp", lambda e: e.dma_start(out=self.lnb[:], in_=self.W[bname][l].partition_broadcast(128)), w=["lnb"])

    def residual_out(self, pa, l, slot_of, nk_total, lhs_of, gcol0):
        pass

    def merge(self, pa, l):
        P = self.P
        W = self.W
        av = self.av
        self.mT = av([128, 8, 1024], BF16)
        self.macc = [av([128, 512], F32) for _ in range(4 * pa.NH)]
        self.sig = [av([128, 512], F32) for _ in range(2)]
        self.lng = av([128, 1024], F32)
        self.lnb = av([128, 1024], F32)
        self.lnst = av([128, 2, 8], F32)
        self.lnag = av([128, 8], F32)
        self.load_ln("ln1_g", "ln1_b", l)
        outs = (self.routT, self.soutT, self.noutT)
        otoks = ([f"routT{c}" for c in range(pa.NT)], [f"soutT{c}_{t}" for c in range(4) for t in range(pa.NH)],
                 [f"noutT{c}" for c in range(pa.NT)])
        for jb in range(2):
            for b in range(3):
                specs = [(W["w_in"][l][:, 4096 + b * 1024 + jb * 512:4096 + b * 1024 + (jb + 1) * 512], 8, 512),
                         (W["w_branch"][l, b][:, jb * 512:(jb + 1) * 512], 4, 512)]
                (gs, gt), (bs, bt) = list(self.wstream(specs))

                def unit(dc, th, b=b, jb=jb, gs=gs, gt=gt, bs=bs, bt=bt):
                    cols = slice(th * 512, (th + 1) * 512)
                    pg, pgt = self.next_f()
                    for c in range(8):
                        P.pe(lambda e, c=c: e.matmul(pg[:], lhsT=gs[:, c, dc * 128:(dc + 1) * 128], rhs=self.hT[:, c, cols],
                                                     start=(c == 0), stop=(c == 7)), r=[gt] + self.hT_tokens(pa, th * 512, (th + 1) * 512), w=[pgt])
                    sg = self.sig[(dc + th) % 2]
                    sgt = f"sig{(dc + th) % 2}"
                    P.act(lambda e: e.activation(out=sg[:], in_=pg[:], func=AF.Sigmoid), r=[pgt], w=[sgt])
                    pp, ppt = self.next_f()
                    for c in range(4):
                        P.pe(lambda e, c=c: e.matmul(pp[:], lhsT=bs[:, c, dc * 128:(dc + 1) * 128], rhs=outs[b][:, c, cols],
                                                     start=(c == 0), stop=(c == 3)), r=[bt] + otoks[b], w=[ppt])
                    acc = self.macc[dc * pa.NH + th]
                    at = f"macc{dc * pa.NH + th}"
                    if b == 0:
                        P.dve(lambda e: e.tensor_tensor(out=acc[:], in0=sg[:], in1=pp[:], op=ALU.mult), r=[sgt, ppt], w=[at])
                    else:
                        P.dve(lambda e: e.tensor_tensor(out=sg[:], in0=sg[:], in1=pp[:], op=ALU.mult), r=[sgt, ppt], w=[sgt])
                        if b == 1:
                            P.dve(lambda e: e.tensor_tensor(out=acc[:], in0=acc[:], in1=sg[:], op=ALU.add), r=[sgt, at], w=[at])
                        else:
                            P.dve(lambda e: e.tensor_tensor(out=self.mT[:, jb * 4 + dc, cols], in0=acc[:], in1=sg[:], op=ALU.add),
                                  r=[sgt, at], w=[f"mT{jb * 4 + dc}_{th}"])
                for dc in range(4):
                    for th in range(pa.NH):
                        unit(dc, th)
        mtoks = [f"mT{c}_{t}" for c in range(8) for t in range(pa.NH)]
        specs = [(W["w_o"][l][:, ob * 512:(ob + 1) * 512], 8, 512) for ob in range(2)]
        for ob, (slot, wt) in enumerate(self.wstream(specs)):
            def unit2(i, ob=ob, slot=slot, wt=wt):
                ps, pt = self.next_f()
                for c in range(8):
                    P.pe(lambda e, c=c: e.matmul(ps[:], lhsT=self.mT[:, c, i * 128:(i + 1) * 128], rhs=slot[:, c, :], start=(c == 0), stop=(c == 7)),
                         r=[wt] + mtoks, w=[pt])
                t = self.t1024[0]
                cs = slice(ob * 512, (ob + 1) * 512)
                P.dve(lambda e: e.tensor_tensor(out=t[:, 0:512], in0=ps[:], in1=self.mod[:, 2048 + ob * 512:2048 + (ob + 1) * 512], op=ALU.mult),
                      r=[pt, "mod"], w=["t1024_0"])
                P.dve(lambda e: e.scalar_tensor_tensor(out=self.xres[i][:, cs], in0=self.xres[i][:, cs], scalar=ALPHA, in1=t[:, 0:512],
                                                       op0=ALU.mult, op1=ALU.add), r=["t1024_0", f"x{i}"], w=[f"x{i}"])
            for i in range(pa.NT):
                unit2(i)
        for i in range(pa.NT):
            self.layer_norm(pa, i, "ln1_g", "ln1_b", l)

    def ffn(self, pa, l, last):
        P = self.P
        W = self.W
        av = self.av
        T, L, NSQ = pa.T, pa.L, pa.n_seq
        self.uT = av([128, 22, 1024], BF16)
        self.za = [av([128, NSQ, L + 2], F32) for _ in range(2)]
        self.zb = [av([128, NSQ, L + 2], F32) for _ in range(2)]
        self.acca = av([128, NSQ, L], F32)
        self.accb = av([128, NSQ, L], F32)
        self.ctmp = av([128, NSQ, L], F32)
        self.cwr = av([128, 2, 128], F32)
        self.cbr = av([128, 128], F32)
        self.cwT = av([128, 3, 44], F32)
        self.cbT = av([128, 44], F32)
        self.lng = av([128, 1024], F32)
        self.lnb = av([128, 1024], F32)
        self.lnst = av([128, 2, 8], F32)
        self.lnag = av([128, 8], F32)
        self.load_ln("ln2_g", "ln2_b", l)
        cw = W["conv_w"][l].rearrange("k (c p) -> (k c) p", p=128)
        P.dma("sp", lambda e: e.dma_start(out=self.cwr[:, 0, :], in_=cw[0:128, :]), w=["cwr"])
        P.dma("sp", lambda e: e.dma_start(out=self.cwr[0:4, 1, :], in_=cw[128:132, :]), w=["cwr"])
        P.dma("sp", lambda e: e.dma_start(out=self.cbr[0:44, :], in_=W["conv_b"][l].rearrange("(c p) -> c p", p=128)), w=["cbr"])
        ps, pt = self.next_f()
        P.pe(lambda e: e.transpose(out=ps[:, 0:128], in_=self.cwr[:, 0, :], identity=self.ident_f[:]), r=["cwr", "ident_f"], w=[pt])
        P.pe(lambda e: e.transpose(out=ps[:, 128:132], in_=self.cwr[0:4, 1, :], identity=self.ident_f[0:4, 0:4]), r=["cwr", "ident_f"], w=[pt])
        P.pe(lambda e: e.transpose(out=ps[:, 256:300], in_=self.cbr[0:44, :], identity=self.ident_f[0:44, 0:44]), r=["cbr", "ident_f"], w=[pt])
        P.act(lambda e: e.copy(out=self.cwT[:, :, :].rearrange("p k c -> p (k c)"), in_=ps[:, 0:132]), r=[pt], w=["cwT"])
        P.act(lambda e: e.copy(out=self.cbT[:], in_=ps[:, 256:300]), r=[pt], w=["cbT"])
        for zz, nm in ((self.za, "za"), (self.zb, "zb")):
            for k in range(2):
                P.pool(lambda e, zz=zz, k=k: e.memset(zz[k][:], 0.0), w=[f"{nm}{k}"])
        if "cw" in self.debug and l == 0:
            self.dump_view("cwT", self.cwT[:, :, :].rearrange("p k c -> p (k c)"), 132, ["cwT"])
            self.dump_view("cbT", self.cbT[:], 44, ["cbT"])
        w_up = W["w_up"][l]
        nblk = 6
        specs = []
        for m in range(nblk):
            nc_ = 512 if m < 5 else 256
            specs.append((w_up[:, m * 512:m * 512 + nc_], 8, nc_))
            specs.append((w_up[:, 2816 + m * 512:2816 + m * 512 + nc_], 8, nc_))
        ws = self.wstream(specs)
        for m in range(nblk):
            sa, sat = next(ws)
            sbk, sbt = next(ws)
            nq = 4 if m < 5 else 2

            def chunk(q, m=m, sa=sa, sat=sat, sbk=sbk, sbt=sbt):
                k = m * 4 + q
                za, zb = self.za[k % 2], self.zb[k % 2]
                zat, zbt = f"za{k % 2}", f"zb{k % 2}"
                for (slot, wt, z, zt) in ((sa, sat, za, zat), (sbk, sbt, zb, zbt)):
                    for th in range(pa.NH):
                        ps, pt = self.next_f()
                        for c in range(8):
                            P.pe(lambda e, c=c, ps=ps, slot=slot, th=th: e.matmul(ps[:], lhsT=slot[:, c, q * 128:(q + 1) * 128],
                                                                                 rhs=self.hT[:, c, th * 512:(th + 1) * 512],
                                                                                 start=(c == 0), stop=(c == 7)),
                                 r=[wt] + self.hT_tokens(pa, th * 512, (th + 1) * 512), w=[pt])
                        if NSQ == 1:
                            dst = z[:, 0, 1 + th * 512:1 + (th + 1) * 512]
                            src = ps[:]
                        else:
                            dst = z[:, :, 1:L + 1]
                            src = ps[:].rearrange("p (s t) -> p s t", s=NSQ)
                        P.act(lambda e, dst=dst, src=src: e.copy(out=dst, in_=src), r=[pt], w=[zt])
                for (z, zt, acc, at, kk) in ((za, zat, self.acca, "acca", k), (zb, zbt, self.accb, "accb", 22 + k)):
                    P.act(lambda e, z=z, acc=acc, kk=kk: e.activation(out=acc[:], in_=z[:, :, 0:L], func=AF.Identity,
                                                                      scale=self.cwT[:, 0, kk:kk + 1], bias=self.cbT[:, kk:kk + 1]),
                          r=[zt, "cwT", "cbT"], w=[at])
                    for tap in (1, 2):
                        P.dve(lambda e, z=z, acc=acc, kk=kk, tap=tap: e.scalar_tensor_tensor(
                            out=acc[:], in0=z[:, :, tap:L + tap], scalar=self.cwT[:, tap, kk:kk + 1], in1=acc[:], op0=ALU.mult, op1=ALU.add),
                            r=[zt, "cwT", at], w=[at])
                if "zdbg" in self.debug and l == 0 and k == 0:
                    f2 = lambda v: v.rearrange("p a b -> p (a b)")
                    self.dump_view("za", f2(za[:, :, :]), NSQ * (L + 2), [zat])
                    self.dump_view("zb", f2(zb[:, :, :]), NSQ * (L + 2), [zbt])
                    self.dump_view("acca", f2(self.acca[:, :, :]), NSQ * L, ["acca"])
                    self.dump_view("accb", f2(self.accb[:, :, :]), NSQ * L, ["accb"])
                P.act(lambda e: e.activation(out=self.acca[:], in_=self.acca[:], func=AF.Gelu_apprx_tanh), r=["acca"], w=["acca"])
                P.dve(lambda e, k=k: e.tensor_tensor(out=self.uT[:, k, 0:T].rearrange("p (s t) -> p s t", s=NSQ), in0=self.acca[:], in1=self.accb[:],
                                                     op=ALU.mult), r=["acca", "accb"], w=[f"uT{k}"])
            for q in range(nq):
                chunk(q)
        for _ in ws:
            pass
        utoks = [f"uT{k}" for k in range(22)]
        if "uT" in self.debug and l == 0:
            self.dump_T(f"{pa.kind}_uT", self.uT, 4, pa.T, utoks)
            self.dump_T(f"{pa.kind}_uTb", self.uT[:, 18:22, :], 4, pa.T, utoks)
        for oh in range(2):
            parts = []
            for part, nk_ in ((0, 8), (1, 8), (2, 6)):
                parts.append(self.wload(W["w_down"][l][part * 1024:part * 1024 + nk_ * 128, oh * 512:(oh + 1) * 512], nk_, 512))

            def unit(i, oh=oh, parts=parts):
                ps, pt = self.next_f()
                for k in range(22):
                    slot, wt = parts[k // 8]
                    P.pe(lambda e, k=k, slot=slot: e.matmul(ps[:], lhsT=self.uT[:, k, i * 128:(i + 1) * 128], rhs=slot[:, k % 8, :],
                                                            start=(k == 0), stop=(k == 21)), r=[wt] + utoks, w=[pt])
                t = self.t1024[0]
                cs = slice(oh * 512, (oh + 1) * 512)
                P.dve(lambda e: e.tensor_tensor(out=t[:, 0:512], in0=ps[:], in1=self.mod[:, 2048 + oh * 512:2048 + (oh + 1) * 512], op=ALU.mult),
                      r=[pt, "mod"], w=["t1024_0"])
                P.dve(lambda e: e.scalar_tensor_tensor(out=self.xres[i][:, cs], in0=self.xres[i][:, cs], scalar=ALPHA, in1=t[:, 0:512],
                                                       op0=ALU.mult, op1=ALU.add), r=["t1024_0", f"x{i}"], w=[f"x{i}"])
            for i in range(pa.NT):
                unit(i)
        for i in range(pa.NT):
            self.layer_norm(pa, i, "ln2_g", "ln2_b", l)
            if last:
                P.dma("sp", lambda e, i=i: e.dma_start(out=self.yout[pa.kind][i * 128:(i + 1) * 128, :], in_=self.xres[i][:]),
                      r=[f"x{i}"], w=["yout"])

    def rstd(self, ap, tok):
        P = self.P
        P.act(lambda e: e.activation(out=ap, in_=ap, func=AF.Ln, bias=self.epsc[:, 0:1], scale=1.0), r=[tok, "epsc"], w=[tok])
        P.act(lambda e: e.activation(out=ap, in_=ap, func=AF.Exp, scale=-0.5), r=[tok], w=[tok])

    def dump_T(self, name, src, nchunk, T, rtoks):
        P = self.P
        out = self.dbg_out(name, [nchunk * 128, T])
        for c in range(nchunk):
            for th in range(T // 512):
                t = self.t1024[c % 2]
                P.dve(lambda e, c=c, th=th, t=t: e.tensor_copy(out=t[:, 0:512], in_=src[:, c, th * 512:(th + 1) * 512]),
                      r=rtoks, w=["t1024_0"])
                P.dma("sp", lambda e, c=c, th=th, t=t: e.dma_start(out=out[c * 128:(c + 1) * 128, th * 512:(th + 1) * 512],
                                                                   in_=t[:, 0:512]), r=["t1024_0"], w=["dbg"])

    def run_path(self, pa):
        P = self.P
        for i in range(pa.NT):
            P.dma("sp", lambda e, i=i: e.dma_start(out=self.xres[i][:], in_=self.xin[pa.kind][i * 128:(i + 1) * 128, :]),
                  w=[f"x{i}"])
        for l in range(DEPTH):
            self.ada_half(pa, l, 0)
            if "ada" in self.debug and l == 0:
                o = self.dbg_out(f"{pa.kind}_ada0", [128, 3072])
                P.dma("sp", lambda e, o=o: e.dma_start(out=o, in_=self.mod[:]), r=["mod"], w=["dbg"])
            if self.stop_after == "ada":
                return
            self.modulate_T(pa)
            if self.stop_after == "mod":
                if "hT" in self.debug and l == 0:
                    self.dump_T(f"{pa.kind}_hT", self.hT, 8, pa.T, [f"hT{i}" for i in range(pa.NT)])
                return
            if "hT" in self.debug and l == 0:
                self.dump_T(f"{pa.kind}_hT", self.hT, 8, pa.T, [f"hT{i}" for i in range(pa.NT)])
            self.ret_tables(l)
            if self.stop_after == "rettab":
                return
            self.prefetch([(self.W["w_in"][l][:, b * 512:(b + 1) * 512], 8, 512) for b in range(2)])
            self.phase()
            self.alloc_ret()
            self.retention(pa, l)
            if self.stop_after in ("retproj", "retA"):
                if "ret" in self.debug:
                    self.dump_T(f"{pa.kind}_rq", self.rqT, 4, pa.T, [f"rqT{h}_{t}" for h in range(4) for t in range(pa.NH)])
                return
            if "ret" in self.debug and l == 0:
                self.dump_T(f"{pa.kind}_rq", self.rqT, 4, pa.T, [f"rqT{h}_{t}" for h in range(4) for t in range(pa.NH)])
                self.dump_T(f"{pa.kind}_rout", self.routT, 4, pa.T, [f"routT{c}" for c in range(pa.NT)])
            if self.stop_after == "ret":
                return
            self.prefetch([(self.W["w_in"][l][:, 2048:2560], 8, 512)])
            self.phase()
            self.alloc_s5(pa)
            self.s5(pa, l)
            if self.stop_after in ("s5u", "s5a", "s5b", "s5c", "s5d", "s5e", "s5f", "s5g"):
                return
            if "s5" in self.debug and l == 0:
                self.dump_T(f"{pa.kind}_sout", self.soutT, 4, pa.T, [f"soutT{c}_{t}" for c in range(4) for t in range(pa.NH)])
            if self.stop_after == "s5":
                return
            self.prefetch([(self.W["w_in"][l][:, 2560 + b * 512:2560 + (b + 1) * 512], 8, 512) for b in range(2)])
            self.phase()
            self.alloc_attn(pa)
            self.attn_proj(pa, l)
            if pa.kind == "p":
                self.ctx_attention(pa, l)
            else:
                self.na_attention(pa, l)
            if "attn" in self.debug and l == 0:
                self.dump_T(f"{pa.kind}_nout", self.noutT, 4, pa.T, [f"noutT{c}" for c in range(pa.NT)])
            if self.stop_after == "attn":
                return
            self.prefetch([(self.W["w_in"][l][:, 4096:4608], 8, 512), (self.W["w_branch"][l, 0][:, 0:512], 4, 512)])
            self.phase()
            self.merge(pa, l)
            if "x1" in self.debug and l == 0:
                self.dump_x(pa, "x1")
            if self.stop_after == "merge":
                return
            self.ada_half(pa, l, 1)
            if "ada2" in self.debug and l == 0:
                o = self.dbg_out(f"{pa.kind}_ada1", [128, 3072])
                P.dma("sp", lambda e, o=o: e.dma_start(out=o, in_=self.mod[:]), r=["mod"], w=["dbg"])
            self.modulate_T(pa)
            if "ada2" in self.debug and l == 0:
                self.dump_T(f"{pa.kind}_h2T", self.hT, 8, pa.T, [f"hT{i}" for i in range(pa.NT)])
            self.prefetch([(self.W["w_up"][l][:, 0:512], 8, 512), (self.W["w_up"][l][:, 2816:2816 + 512], 8, 512)])
            self.phase(at=0)
            self.ffn(pa, l, last=(l == DEPTH - 1))
            if l + 1 < DEPTH and not (pa.kind == "s" and "noadashare" not in self.debug):
                self.prefetch([(self.W["w_ada"][l + 1, :, b * 512:(b + 1) * 512], 8, 512) for b in range(2)])
            self.phase(at=0)
            self.aoff = self.amark
            if "x2" in self.debug and l == 0:
                self.dump_x(pa, "x2")
            if self.stop_after == "ffn":
                return

    def build(self, paths=("p", "s")):
        self.load_consts()
        for k in paths:
            self.run_path(Path(k))
        self.P.emit()
        return self.nc


def make_in_maps(inputs):
    consts = _consts()
    maps = []
    f = lambda a: np.ascontiguousarray(a, dtype=np.float32)
    shared = {k: f(inputs[k]) for k in W_SHAPES}
    for k, v in consts.items():
        shared["c_" + k] = f(v)
    rpbt = np.stack([_rpb_table(f(inputs["na_rpb"][l])) for l in range(2)], 0)
    for c in range(N_CORES):
        b = c % 4
        m = dict(shared)
        m["xin_p"] = f(inputs["x_prompt"][2 * c:2 * c + 2].reshape(512, 1024))
        m["xin_s"] = f(inputs["x_sample"][b])
        m["cond_p"] = f(inputs["c_ctx"])
        m["cond_s"] = f(inputs["c"][b])
        m["sret"] = f(inputs["state_ret"][b])
        m["sssm"] = f(inputs["state_ssm"][b])
        m["ck"] = f(inputs["cache_na_k"][b].reshape(2, 512, 512))
        m["cv"] = f(inputs["cache_na_v"][b].reshape(2, 512, 512))
        m["rpbt"] = rpbt
        maps.append(m)
    return maps


def kernel(**inputs):
    b = Builder()
    nc = b.build()
    maps = make_in_maps(inputs)
    res = run_bass_kernel_spmd(nc, maps, core_ids=list(range(N_CORES)))
    R = res.results
    yp = np.concatenate([R[c]["yp"].reshape(2, 256, 1024) for c in range(8)], 0)
    ys = np.stack([R[c]["ys"] for c in range(4)], 0)
    nsr = np.concatenate([R[c]["nsr"] for c in range(8)], 0)
    nss = np.concatenate([R[c]["nss"] for c in range(8)], 0)
    nck = np.concatenate([R[c]["nck"] for c in range(8)], 0).reshape(16, 2, 256, 8, 64)
    ncv = np.concatenate([R[c]["ncv"] for c in range(8)], 0).reshape(16, 2, 256, 8, 64)
    return (yp.astype(np.float32), ys.astype(np.float32), nsr.astype(np.float32), nss.astype(np.float32),
            nck.astype(np.float32), ncv.astype(np.float32))
```

```python
from contextlib import ExitStack
import math
import numpy as np
import concourse.bass as bass
import concourse.mybir as mybir
from concourse.bass_utils import run_bass_kernel_spmd

F32 = mybir.dt.float32
BF16 = mybir.dt.bfloat16
AF = mybir.ActivationFunctionType
ALU = mybir.AluOpType
AX = mybir.AxisListType

ENGS = ("pe", "act", "dve", "pool", "sp")
N_DMA_SEM = 20
INLINE_WAIT = True

D = 1024
DEPTH = 2
MIX = 512
DFF = 2816
ALPHA = (2 * DEPTH) ** 0.25
LN_EPS = 1e-5
N_CORES = 8


class Op:
    __slots__ = ("eng", "fn", "deps", "dma", "idx", "sig", "sem", "val")

    def __init__(self, eng, fn, deps, dma, idx):
        self.eng, self.fn, self.deps, self.dma, self.idx = eng, fn, deps, dma, idx
        self.sig = False
        self.sem = None
        self.val = 0


class _Rec:
    def __init__(self):
        self.name = None

    def __getattr__(self, name):
        def call(*args, **kwargs):
            assert self.name is None, "one engine call per op"
            self.name, self.args, self.kwargs = name, args, kwargs
            return None
        return call


class Prog:
    def __init__(self, nc):
        self.nc = nc
        self.ops = []
        self.last_w = {}
        self.readers = {}
        self.stack = ExitStack()
        self.n_alloc = 0
        self.last_eng = {}

    def sb(self, shape, dtype, name=None):
        self.n_alloc += 1
        name = name or f"sb{self.n_alloc}"
        return self.stack.enter_context(self.nc.sbuf_tensor(name, list(shape), dtype))

    def ps(self, shape, dtype=F32, name=None):
        self.n_alloc += 1
        name = name or f"ps{self.n_alloc}"
        return self.stack.enter_context(self.nc.psum_tensor(name, list(shape), dtype))

    def add(self, eng, fn, reads=(), writes=(), dma=False, extra_deps=()):
        idx = len(self.ops)
        deps = set(extra_deps)
        for t in reads:
            w = self.last_w.get(t)
            if w is not None:
                deps.add(w)
            if t.startswith("ps"):
                for r in self.readers.get(t, ()):
                    if self.ops[r].eng != eng:
                        deps.add(r)
        for t in writes:
            w = self.last_w.get(t)
            if w is not None:
                deps.add(w)
            for r in self.readers.get(t, ()):
                deps.add(r)
        deps.discard(idx)
        if eng == "pe" and not dma:
            deps = {d for d in deps if not (self.ops[d].eng == "pe" and not self.ops[d].dma)}
        rec = _Rec()
        fn(rec)
        assert rec.name is not None
        op = Op(eng, (lambda e, rec=rec: getattr(e, rec.name)(*rec.args, **rec.kwargs)), sorted(deps), dma, idx)
        self.ops.append(op)
        for t in reads:
            self.readers.setdefault(t, []).append(idx)
        for t in writes:
            self.last_w[t] = idx
            self.readers[t] = []
        if not dma:
            self.last_eng[eng] = idx
        return idx

    def pe(self, fn, r=(), w=()):
        return self.add("pe", fn, r, w)

    def act(self, fn, r=(), w=()):
        return self.add("act", fn, r, w)

    def dve(self, fn, r=(), w=()):
        return self.add("dve", fn, r, w)

    def pool(self, fn, r=(), w=()):
        return self.add("pool", fn, r, w)

    def dma(self, q, fn, r=(), w=()):
        return self.add(q, fn, r, w, dma=True)

    def barrier(self):
        deps = list(self.last_eng.values())
        dmas = [o.idx for o in self.ops if o.dma]
        for e in ("pe", "act", "dve", "pool", "sp"):
            self.add(e, (lambda eng: eng.nop()), extra_deps=deps + dmas[-3 * N_DMA_SEM:])

    def emit(self):
        nc = self.nc
        ops = self.ops
        for op in ops:
            for d in op.deps:
                ops[d].sig = True
        st = self.stack
        esem = {e: st.enter_context(nc.semaphore(f"s_{e}")) for e in ENGS}
        dsem = {e: [st.enter_context(nc.semaphore(f"d_{e}{i}")) for i in range(N_DMA_SEM)]
                for e in ("sp", "pool", "act")}
        ecount = {e: 0 for e in ENGS}
        dcount = {e: [0] * N_DMA_SEM for e in dsem}
        dnum = {e: 0 for e in dsem}
        per_eng = {e: [] for e in ENGS}
        for op in ops:
            if op.dma:
                k = dnum[op.eng] % N_DMA_SEM
                dnum[op.eng] += 1
                dcount[op.eng][k] += 16
                op.sem, op.val = dsem[op.eng][k], dcount[op.eng][k]
            elif op.sig:
                ecount[op.eng] += 1
                op.sem, op.val = esem[op.eng], ecount[op.eng]
            per_eng[op.eng].append(op)
        final = {}
        for e in dsem:
            for k in range(N_DMA_SEM):
                if dcount[e][k]:
                    final[id(dsem[e][k])] = (dsem[e][k], dcount[e][k])
        for e in ENGS:
            if ecount[e]:
                final[id(esem[e])] = (esem[e], ecount[e])

        def run(eng_name, e):
            waited = {}
            for op in per_eng[eng_name]:
                need = {}
                for d in op.deps:
                    p = ops[d]
                    key = id(p.sem)
                    if waited.get(key, 0) < p.val and need.get(key, (None, 0))[1] < p.val:
                        need[key] = (p.sem, p.val)
                if op.dma and op.val > 16:
                    key = id(op.sem)
                    if waited.get(key, 0) < op.val - 16 and need.get(key, (None, 0))[1] < op.val - 16:
                        need[key] = (op.sem, op.val - 16)
                need = list(need.items())
                inline = None
                if INLINE_WAIT and need and not op.dma:
                    inline = need.pop()
                for key, (sm, vl) in need:
                    e.wait_ge(sm, vl)
                    waited[key] = vl
                ins = op.fn(e)
                if inline is not None:
                    key, (sm, vl) = inline
                    ins._wait_ge(sm, vl)
                    waited[key] = vl
                if op.dma:
                    ins.then_inc(op.sem, 16)
                elif op.sig:
                    ins.then_inc(op.sem, 1)
            if eng_name == "sp":
                for key, (s, v) in final.items():
                    if waited.get(key, 0) < v:
                        e.wait_ge(s, v)

        with nc.Block() as block:
            @block.tensor
            def _(e):
                run("pe", e)

            @block.scalar
            def _(e):
                run("act", e)

            @block.vector
            def _(e):
                run("dve", e)

            @block.gpsimd
            def _(e):
                run("pool", e)

            @block.sync
            def _(e):
                run("sp", e)
        self.stats = {e: len(per_eng[e]) for e in ENGS}
        self.stack.close()


def _consts():
    c = {}
    c["ident"] = np.eye(128, dtype=np.float32)
    k = np.arange(128, dtype=np.float32)
    dq = k[None, :] - k[:, None]
    c["e1"] = np.maximum(dq, 0.0).astype(np.float32)
    c["e2"] = np.maximum(-dq, 0.0).astype(np.float32)
    c["m1"] = (dq >= 0).astype(np.float32)
    c["m2"] = (dq <= 0).astype(np.float32)
    pidx = np.zeros((128, 4), np.float32)
    pidx[:, 0] = k + 1.0
    pidx[:, 1] = 128.0 - k
    pidx[:, 2] = 127.0 - k
    pidx[:, 3] = k
    c["pidx"] = pidx
    pos = np.arange(1024)
    row = (pos // 64).astype(np.float32)
    col = (pos % 64).astype(np.float32)
    inv_freq = (10000.0 ** (-np.arange(32, dtype=np.float32) / 32)).astype(np.float32)
    cosT = np.zeros((128, 1024), np.float32)
    sinT = np.zeros((128, 1024), np.float32)
    for d in range(128):
        p = row if d < 64 else col
        f = inv_freq[d % 32]
        ang = (p * f).astype(np.float32)
        cosT[d] = np.cos(ang)
        sinT[d] = np.sin(ang)
    c["cosT"] = cosT
    c["sinT"] = sinT
    rm = np.zeros((128, 128), np.float32)
    for dp in range(128):
        if (dp % 64) < 32:
            rm[dp + 32, dp] = -1.0
        else:
            rm[dp - 32, dp] = 1.0
    c["rotm"] = rm
    ev = np.zeros((2, 4, 8), np.float32)
    i8 = np.arange(8, dtype=np.float32)
    ev[0, 0] = i8; ev[0, 1] = -i8; ev[0, 2] = i8 + 1; ev[0, 3] = 7 - i8
    ev[1, 0] = -i8; ev[1, 1] = i8; ev[1, 2] = 8 - i8; ev[1, 3] = i8
    c["ev"] = np.broadcast_to(ev.reshape(1, 64), (128, 64)).copy()
    s0 = np.arange(128) // 16
    ev8 = np.zeros((3, 17), np.float32)
    ev8[0] = 8.0 * np.arange(17)
    ev8[1, :16] = 8.0 * (15 - np.arange(16)); ev8[1, 16] = 128.0
    ev8[2, :8] = 8.0 * (7 - np.arange(8)); ev8[2, 8] = 64.0
    c["ev8"] = np.broadcast_to(ev8.reshape(1, 51), (128, 51)).copy()
    c["mf"] = (s0[None, :] >= s0[:, None]).astype(np.float32)
    c["mb"] = (s0[:, None] >= s0[None, :]).astype(np.float32)
    sg = np.ones((128, 2), np.float32)
    sg[:64, 0] = -1.0
    sg[64:, 1] = -1.0
    c["sgn"] = sg
    ust = [0, 0, 0, 2, 4, 6, 6, 6]
    rowm = np.zeros((8, 2, 10, 64), np.float32)
    for i in range(8):
        for rl in range(2):
            r = 2 * i + rl
            st = min(max(r - 4, 0), 8)
            for kr in range(10):
                ka = ust[i] + kr
                if not (st <= ka < st + 8):
                    rowm[i, rl, kr, :] = -1e30
    c["rowm"] = rowm.reshape(8, 2, 640)
    ind = np.zeros((2, 128), np.float32)
    ind[0, :64] = 1.0
    ind[1, 64:] = 1.0
    c["ind"] = ind
    return c


NA_UST = [0, 0, 0, 2, 4, 6, 6, 6]


def _rpb_table(rpb):
    cq = np.arange(64)
    ck = np.arange(64)
    ws = np.clip(cq - 8, 0, 48)
    colok = (ck[None, :] >= ws[:, None]) & (ck[None, :] < ws[:, None] + 16)
    coff = np.clip(ck[None, :] - cq[:, None] + 15, 0, 30)
    out = np.full((2, 64, 8, 19, 64), -1e30, np.float32)
    for rl in range(2):
        for e in range(19):
            dr = e - 2 - rl
            if 0 <= dr <= 14:
                g = rpb[:, dr, :][:, coff]
                out[rl, :, :, e, :] = np.where(colok[None], g, np.float32(-1e30)).transpose(1, 0, 2)
            else:
                out[rl, :, :, e, :] = np.where(colok[:, None, :], np.float32(0.0), np.float32(-1e30))
    return out.reshape(128, 8, 19, 64)


CONST_SHAPES = {"ident": [128, 128], "e1": [128, 128], "e2": [128, 128], "m1": [128, 128], "m2": [128, 128],
                "pidx": [128, 4], "cosT": [128, 1024], "sinT": [128, 1024], "rotm": [128, 128],
                "ev": [128, 64], "mf": [128, 128], "mb": [128, 128], "sgn": [128, 2],
                "rowm": [8, 2, 640], "ind": [2, 128], "ev8": [128, 51]}

W_SHAPES = {
    'w_ada': [2, 1024, 6144], 'b_ada': [2, 6144], 'w_in': [2, 1024, 7168], 'ret_decay': [2, 2, 4],
    'ssm_a_re': [2, 2, 32, 64], 'ssm_a_im': [2, 2, 32, 64], 'ssm_log_dt': [2, 2, 32],
    'ssm_b_re': [2, 32, 64, 16], 'ssm_b_im': [2, 32, 64, 16], 'ssm_c_re': [2, 2, 32, 16, 64],
    'ssm_c_im': [2, 2, 32, 16, 64], 'ssm_d': [2, 512], 'ssm_w_glu': [2, 512, 512], 'na_rpb': [2, 8, 15, 31],
    'w_branch': [2, 3, 512, 1024], 'w_o': [2, 1024, 1024], 'ln1_g': [2, 1024], 'ln1_b': [2, 1024],
    'w_up': [2, 1024, 5632], 'conv_w': [2, 3, 5632], 'conv_b': [2, 5632], 'w_down': [2, 2816, 1024],
    'ln2_g': [2, 1024], 'ln2_b': [2, 1024],
}


class Path:
    def __init__(self, kind):
        self.kind = kind
        if kind == "p":
            self.n_seq, self.L = 2, 256
        else:
            self.n_seq, self.L = 1, 1024
        self.T = self.n_seq * self.L
        self.NT = self.T // 128
        self.CPS = self.L // 128
        self.NH = max(1, self.T // 512)


class Builder:
    def __init__(self, debug=None, stop_after=None):
        self.debug = debug or []
        self.stop_after = stop_after
        nc = self.nc = bass.Bass("TRN2", target_bir_lowering=False)
        self.P = Prog(nc)
        di = lambda n, s: nc.dram_tensor(n, list(s), F32, kind="ExternalInput").ap()
        do = lambda n, s: nc.dram_tensor(n, list(s), F32, kind="ExternalOutput").ap()
        self.xin = {"p": di("xin_p", [512, 1024]), "s": di("xin_s", [1024, 1024])}
        self.cond = {"p": di("cond_p", [1024]), "s": di("cond_s", [1024])}
        self.sret = di("sret", [2, 2, 4, 128, 128])
        self.sssm = di("sssm", [2, 2, 32, 64, 2])
        self.ck = di("ck", [2, 512, 512])
        self.cv = di("cv", [2, 512, 512])
        self.rpbt = di("rpbt", [2, 128, 8, 19, 64])
        self.adas = nc.dram_tensor("adas", [2, 6144], F32, kind="Internal").ap()
        self.s5c = nc.dram_tensor("s5c", [2, 2, 4, 4, 128, 1024], BF16, kind="Internal").ap()
        self.s5a = nc.dram_tensor("s5a", [2, 2, 128, 2, 32], F32, kind="Internal").ap()
        self.W = {k: di(k, s) for k, s in W_SHAPES.items()}
        self.C = {k: di("c_" + k, s) for k, s in CONST_SHAPES.items()}
        self.yout = {"p": do("yp", [512, 1024]), "s": do("ys", [1024, 1024])}
        self.nsr = do("nsr", [2, 2, 2, 4, 128, 128])
        self.nss = do("nss", [2, 2, 2, 32, 64, 2])
        self.nck = do("nck", [2, 2, 256, 512])
        self.ncv = do("ncv", [2, 2, 256, 512])
        self.dbg = {}
        self.uid = 0
        self.alloc()

    def dbg_out(self, name, shape):
        if name not in self.dbg:
            self.dbg[name] = self.nc.dram_tensor("dbg_" + name, list(shape), F32, kind="ExternalOutput").ap()
        return self.dbg[name]

    def tok(self, base):
        self.uid += 1
        return f"{base}#{self.uid}"

    def alloc(self):
        P = self.P
        sb = P.sb
        self.ident_f = sb([128, 128], F32)
        self.ident = sb([128, 128], BF16)
        self.ones = sb([128, 128], BF16)
        self.epsc = sb([128, 1], F32)
        self.cE1 = sb([128, 128], F32)
        self.cE2 = sb([128, 128], F32)
        self.cM1 = sb([128, 128], F32)
        self.cM2 = sb([128, 128], F32)
        self.pidx = sb([128, 4], F32)
        self.rotm = sb([128, 128], BF16)
        self.cosT = sb([128, 1024], F32)
        self.sinT = sb([128, 1024], F32)
        self.ev = sb([128, 2, 32], F32)
        self.mf = sb([128, 128], F32)
        self.mb = sb([128, 128], F32)
        self.sgn = sb([128, 2], F32)
        self.ind = sb([2, 128], BF16)
        self.ev8 = sb([128, 3, 17], F32)
        self.xres = [sb([128, 1024], F32) for _ in range(8)]
        self.mod = sb([128, 3072], F32)
        self.hT = sb([128, 8, 1024], BF16)
        self.NS = 3
        self.wring = [sb([128, 8, 512], BF16) for _ in range(self.NS)]
        self.wcnt = 0
        self.pref = []
        self.condc = sb([128, 8], F32)
        self.condb = sb([128, 8], BF16)
        self.scondT = sb([128, 8, 128], BF16)
        self.scondT2 = sb([128, 8, 128], BF16)
        self.rdec = sb([128, 8], F32)
        self.rtab = sb([128, 5, 8], F32)
        self.Dc = sb([128, 4, 128], F32)
        _t = sb([128, 1024], F32)
        self.t1024 = [_t, _t]
        self.tmpA = _t[:, 0:128]
        self.tmpB = _t[:, 128:256]
        self.hb = [sb([128, 1024], BF16) for _ in range(2)]
        self.bnst = sb([128, 4, 8], F32)
        self.bnag = sb([128, 4, 8], F32)
        self.ARENA = 96 * 1024
        self.arena = sb([128, self.ARENA // 2], BF16)
        self.aoff = 0
        self.routT = self.av([128, 4, 1024], BF16)
        self.soutT = self.av([128, 4, 1024], BF16)
        self.noutT = self.av([128, 4, 1024], BF16)
        self.amark = self.aoff
        self.psf = [P.ps([128, 512], F32) for _ in range(5)]
        self.psb = [P.ps([128, 8, 128], BF16) for _ in range(3)]
        self.nf = 0
        self.nb = 0

    def av(self, shape, dtype):
        n = 1
        for d in shape[1:]:
            n *= d
        nbytes = n * (4 if dtype == F32 else 2)
        off = (self.aoff + 63) // 64 * 64
        assert off + nbytes <= self.ARENA, ("arena overflow", off, nbytes)
        self.aoff = off + nbytes
        v = self.arena[:, off // 2:(off + nbytes) // 2]
        if dtype == F32:
            v = v.bitcast(F32)
        if len(shape) == 3:
            v = v.rearrange("p (a b) -> p a b", b=shape[2])
        elif len(shape) == 4:
            v = v.rearrange("p (a b c) -> p a b c", b=shape[2], c=shape[3])
        elif len(shape) == 5:
            v = v.rearrange("p (a b c d) -> p a b c d", b=shape[2], c=shape[3], d=shape[4])
        return v

    def phase(self, at=None):
        self.P.barrier()
        self.aoff = self.amark if at is None else at

    def alloc_ret(self):
        av = self.av
        self.rqT = av([128, 4, 1024], BF16)
        self.rkT = av([128, 4, 1024], BF16)
        self.rv = av([128, 8, 512], BF16)
        self.rg = av([128, 8, 512], BF16)
        self.Sm = av([128, 2, 4, 128], F32)
        self.Sin = av([128, 8, 2, 4, 128], BF16)
        self.PT4 = [av([128, 128], BF16) for _ in range(4)]
        self.osb = [av([128, 512], F32) for _ in range(2)]
        self.qtmp = [av([128, 512], BF16) for _ in range(2)]
        self.kdall = [av([128, 4, 128], BF16) for _ in range(2)]

    def next_f(self):
        i = self.nf % len(self.psf)
        self.nf += 1
        return self.psf[i], f"psf{i}"

    def next_b(self):
        i = self.nb % len(self.psb)
        self.nb += 1
        return self.psb[i], f"psb{i}"

    def wload(self, src, nk, ncols):
        i = self.wcnt % self.NS
        self.wcnt += 1
        slot = self.wring[i]
        t = f"wslot{i}"
        self.P.dma("pool", lambda e: e.dma_start(out=slot[:, 0:nk, 0:ncols],
                                                  in_=src.rearrange("(c p) n -> p c n", p=128)), w=[t])
        return slot, t

    @staticmethod
    def _wkey(src):
        return (src.name, src.offset, tuple(src.shape))

    def prefetch(self, specs):
        if "nopf" in self.debug:
            return
        for sp in specs:
            self.pref.append((self._wkey(sp[0]), self.wload(*sp)))

    def wget(self, *spec):
        if self.pref and self.pref[0][0] == self._wkey(spec[0]):
            return self.pref.pop(0)[1]
        assert not self.pref, ("prefetch mismatch", self.pref[0][0], self._wkey(spec[0]))
        return self.wload(*spec)

    def wstream(self, specs, depth=2):
        specs = list(specs)
        n = len(specs)
        loaded = []
        k = 0
        while k < min(depth, n):
            loaded.append(self.wget(*specs[k]))
            k += 1
        for i in range(n):
            yield loaded[i]
            if k < n:
                loaded.append(self.wget(*specs[k]))
                k += 1

    def load_consts(self):
        P = self.P
        C = self.C
        P.dma("sp", lambda e: e.dma_start(out=self.ident_f[:], in_=C["ident"]), w=["ident_f"])
        P.dve(lambda e: e.tensor_copy(out=self.ident[:], in_=self.ident_f[:]), r=["ident_f"], w=["ident"])
        P.pool(lambda e: e.memset(self.ones[:], 1.0), w=["ones"])
        P.pool(lambda e: e.memset(self.epsc[:], LN_EPS), w=["epsc"])
        for nm, t in (("e1", self.cE1), ("e2", self.cE2), ("m1", self.cM1), ("m2", self.cM2), ("pidx", self.pidx),
                      ("cosT", self.cosT), ("sinT", self.sinT), ("mf", self.mf), ("mb", self.mb), ("sgn", self.sgn)):
            P.dma("sp", lambda e, nm=nm, t=t: e.dma_start(out=t[:], in_=C[nm]), w=["c_" + nm])
        P.dma("pool", lambda e: e.dma_start(out=self.rotm[:], in_=C["rotm"]), w=["rotm"])
        P.dma("sp", lambda e: e.dma_start(out=self.ev[:], in_=C["ev"].rearrange("p (a b) -> p a b", a=2)), w=["c_ev"])
        P.dma("pool", lambda e: e.dma_start(out=self.ind[:], in_=C["ind"]), w=["c_ind"])
        P.dma("sp", lambda e: e.dma_start(out=self.ev8[:], in_=C["ev8"].rearrange("p (a b) -> p a b", a=3)), w=["c_ev8"])

    def ada_half(self, pa, l, hh):
        P = self.P
        W = self.W
        c0 = hh * 3072
        share = "noadashare" not in self.debug
        if pa.kind == "s" and share:
            P.dma("sp", lambda e: e.dma_start(out=self.mod[:], in_=self.adas[l, c0:c0 + 3072].partition_broadcast(128)),
                  r=[f"adas{l}_{hh}_{b}" for b in range(6)], w=["mod"])
            P.dve(lambda e: e.tensor_scalar(out=self.mod[:, 1024:2048], in0=self.mod[:, 1024:2048], scalar1=1.0, scalar2=None,
                                            op0=ALU.add), r=["mod"], w=["mod"])
            return
        if hh == 0 and l == 0:
            kinds = ("p", "s") if (pa.kind == "p" and share) else (pa.kind,)
            for kd in kinds:
                dst = self.scondT if kd == pa.kind else self.scondT2
                dtok = "scondT" if kd == pa.kind else "scondT2"
                P.dma("sp", lambda e, kd=kd: e.dma_start(out=self.condc[:], in_=self.cond[kd].rearrange("(c p) -> p c", p=128),
                                                         allow_slow_non_contiguous=True), w=["condc"])
                P.act(lambda e: e.activation(out=self.condb[:], in_=self.condc[:], func=AF.Silu), r=["condc"], w=["condb"])
                for c in range(8):
                    P.dve(lambda e, c=c, dst=dst: e.tensor_scalar(out=dst[:, c, :], in0=self.ones[:], scalar1=self.condb[:, c:c + 1],
                                                                  scalar2=None, op0=ALU.mult), r=["ones", "condb"], w=[dtok])
        P.dma("sp", lambda e: e.dma_start(out=self.mod[:], in_=W["b_ada"][l, c0:c0 + 3072].partition_broadcast(128)),
              w=["mod"])
        specs = [(W["w_ada"][l, :, c0 + b * 512:c0 + (b + 1) * 512], 8, 512) for b in range(6)]
        for b, (slot, wt) in enumerate(self.wstream(specs)):
            blk = slice(b * 512, (b + 1) * 512)
            if pa.kind == "p" and share:
                ps2, pt2 = self.next_f()
                for c in range(8):
                    P.pe(lambda e, c=c: e.matmul(ps2[:], lhsT=self.scondT2[:, c, :], rhs=slot[:, c, :], start=(c == 0), stop=(c == 7)),
                         r=["scondT2", wt], w=[pt2])
                t = self.t1024[0]
                P.dve(lambda e: e.tensor_tensor(out=t[:, 0:512], in0=self.mod[:, blk], in1=ps2[:], op=ALU.add), r=[pt2, "mod"], w=["t1024_0"])
                P.dma("sp", lambda e, b=b: e.dma_start(out=self.adas[l:l + 1, c0 + b * 512:c0 + (b + 1) * 512], in_=t[0:1, 0:512]),
                      r=["t1024_0"], w=[f"adas{l}_{hh}_{b}"])
            ps, pt = self.next_f()
            for c in range(8):
                P.pe(lambda e, c=c: e.matmul(ps[:], lhsT=self.scondT[:, c, :], rhs=slot[:, c, :], start=(c == 0), stop=(c == 7)),
                     r=["scondT", wt], w=[pt])
            P.dve(lambda e: e.tensor_tensor(out=self.mod[:, blk], in0=self.mod[:, blk], in1=ps[:], op=ALU.add), r=[pt, "mod"], w=["mod"])
        P.dve(lambda e: e.tensor_scalar(out=self.mod[:, 1024:2048], in0=self.mod[:, 1024:2048], scalar1=1.0, scalar2=None,
                                        op0=ALU.add), r=["mod"], w=["mod"])

    def modulate_T(self, pa):
        P = self.P
        for i in range(pa.NT):
            t = self.t1024[i % 2]
            hb = self.hb[i % 2]
            tt, ht = "t1024_0", f"hb{i % 2}"
            P.dve(lambda e, i=i, t=t: e.tensor_tensor(out=t[:], in0=self.xres[i][:], in1=self.mod[:, 1024:2048], op=ALU.mult),
                  r=[f"x{i}", "mod"], w=[tt])
            P.dve(lambda e, t=t, hb=hb: e.tensor_tensor(out=hb[:], in0=t[:], in1=self.mod[:, 0:1024], op=ALU.add),
                  r=[tt, "mod"], w=[ht])
            pb, pbt = self.next_b()
            for c in range(8):
                P.pe(lambda e, c=c, hb=hb, pb=pb: e.transpose(out=pb[:, c, :], in_=hb[:, c * 128:(c + 1) * 128],
                                                              identity=self.ident[:]), r=[ht, "ident"], w=[pbt])
            P.act(lambda e, i=i, pb=pb: e.copy(out=self.hT[:, :, i * 128:(i + 1) * 128], in_=pb[:]), r=[pbt], w=[f"hT{i}"])

    def hT_tokens(self, pa, lo, hi):
        return [f"hT{i}" for i in range(lo // 128, (hi + 127) // 128)]

    def proj_feat(self, pa, slot, wt, j, consume):
        P = self.P
        for th in range(pa.NH):
            ps, pt = self.next_f()
            for c in range(8):
                P.pe(lambda e, c=c, ps=ps, th=th: e.matmul(ps[:], lhsT=slot[:, c, j * 128:(j + 1) * 128],
                                                           rhs=self.hT[:, c, th * 512:(th + 1) * 512],
                                                           start=(c == 0), stop=(c == 7)),
                     r=[wt] + self.hT_tokens(pa, th * 512, (th + 1) * 512), w=[pt])
            consume(ps, pt, th)

    def proj_tok(self, pa, slot, wt, consume, ncols=512):
        P = self.P
        for i in range(pa.NT):
            ps, pt = self.next_f()
            for c in range(8):
                P.pe(lambda e, c=c, ps=ps, i=i: e.matmul(ps[:, 0:ncols], lhsT=self.hT[:, c, i * 128:(i + 1) * 128],
                                                         rhs=slot[:, c, 0:ncols], start=(c == 0), stop=(c == 7)),
                     r=[wt, f"hT{i}"], w=[pt])
            consume(ps, pt, i)

    def ret_tables(self, l):
        P = self.P
        P.dma("sp", lambda e: e.dma_start(out=self.rdec[:], in_=self.W["ret_decay"][l].rearrange("a b -> (a b)")
                                          .partition_broadcast(128)), w=["rdec"])
        P.act(lambda e: e.activation(out=self.rdec[:], in_=self.rdec[:], func=AF.Exp, scale=-1.0), r=["rdec"], w=["rdec"])
        P.act(lambda e: e.activation(out=self.rdec[:], in_=self.rdec[:], func=AF.Ln, bias=1.0, scale=1.0), r=["rdec"], w=["rdec"])
        P.dve(lambda e: e.tensor_scalar(out=self.rdec[:], in0=self.rdec[:], scalar1=-1.0, scalar2=None, op0=ALU.mult),
              r=["rdec"], w=["rdec"])
        for di in range(2):
            for h in range(4):
                col = di * 4 + h
                P.act(lambda e, col=col, di=di: e.activation(out=self.rtab[:, 0, col:col + 1], in_=self.pidx[:, di:di + 1],
                                                             func=AF.Exp, scale=self.rdec[:, col:col + 1]),
                      r=["rdec", "c_pidx"], w=["rtab"])
                P.act(lambda e, col=col, di=di: e.activation(out=self.rtab[:, 2, col:col + 1], in_=self.pidx[:, 2 + di:3 + di],
                                                             func=AF.Exp, scale=self.rdec[:, col:col + 1]),
                      r=["rdec", "c_pidx"], w=["rtab"])
        P.act(lambda e: e.activation(out=self.rtab[:, 4, :], in_=self.rdec[:], func=AF.Exp, scale=128.0), r=["rdec"], w=["rtab"])
        sc = 128.0 ** -0.5
        P.dve(lambda e: e.tensor_scalar(out=self.rtab[:, 2, :], in0=self.rtab[:, 2, :], scalar1=sc, scalar2=None, op0=ALU.mult),
              r=["rtab"], w=["rtab"])
        for h in range(4):
            P.act(lambda e, h=h: e.activation(out=self.tmpA[:], in_=self.cE1[:], func=AF.Exp, scale=self.rdec[:, h:h + 1]),
                  r=["rdec", "c_e1"], w=["t1024_0"])
            P.act(lambda e, h=h: e.activation(out=self.tmpB[:], in_=self.cE2[:], func=AF.Exp, scale=self.rdec[:, 4 + h:5 + h]),
                  r=["rdec", "c_e2"], w=["t1024_0"])
            P.dve(lambda e: e.tensor_tensor(out=self.tmpA[:], in0=self.tmpA[:], in1=self.cM1[:], op=ALU.mult),
                  r=["t1024_0", "c_m1"], w=["t1024_0"])
            P.dve(lambda e: e.tensor_tensor(out=self.tmpB[:], in0=self.tmpB[:], in1=self.cM2[:], op=ALU.mult),
                  r=["t1024_0", "c_m2"], w=["t1024_0"])
            P.dve(lambda e, h=h: e.scalar_tensor_tensor(out=self.Dc[:, h, :], in0=self.tmpA[:], scalar=sc, in1=self.tmpB[:],
                                                        op0=ALU.mult, op1=ALU.add), r=["t1024_0", "t1024_0"], w=["Dc"])
            P.dve(lambda e, h=h: e.scalar_tensor_tensor(out=self.Dc[:, h, :], in0=self.tmpB[:], scalar=sc - 1.0,
                                                        in1=self.Dc[:, h, :], op0=ALU.mult, op1=ALU.add),
                  r=["t1024_0", "Dc"], w=["Dc"])

    def retention(self, pa, l):
        P = self.P
        W = self.W
        rope = pa.kind == "s" and "norope" not in self.debug
        w_in = W["w_in"][l]
        specs = [(w_in[:, b * 512:(b + 1) * 512], 8, 512) for b in range(4)]
        ws = self.wstream(specs)
        for which, dst, dname in ((0, self.rqT, "rqT"), (1, self.rkT, "rkT")):
            slot, wt = next(ws)
            for hd in range(4):
                def consume(ps, pt, th, hd=hd, dst=dst, dname=dname):
                    cols = slice(th * 512, (th + 1) * 512)
                    otok = f"{dname}{hd}_{th}"
                    if not rope:
                        P.act(lambda e: e.copy(out=dst[:, hd, cols], in_=ps[:]), r=[pt], w=[otok])
                        return
                    qt = self.qtmp[(hd + th) % 2]
                    qtt = f"qtmp{(hd + th) % 2}"
                    P.act(lambda e: e.copy(out=qt[:], in_=ps[:]), r=[pt], w=[qtt])
                    ps2, pt2 = self.next_f()
                    P.pe(lambda e: e.matmul(ps2[:], lhsT=(self.ident[:] if "norot" in self.debug else self.rotm[:]), rhs=qt[:], start=True, stop=True),
                         r=[qtt, "rotm", "ident"], w=[pt2])
                    t1 = self.t1024[0]
                    if "v1" in self.debug:
                        P.dve(lambda e: e.tensor_copy(out=dst[:, hd, cols], in_=ps2[:]), r=[pt2], w=[otok])
                        return
                    if "v2" in self.debug:
                        src = self.mod[:, 0:512] if "v3" in self.debug else self.cosT[:, cols]
                        P.dve(lambda e: e.tensor_tensor(out=t1[:, 0:512], in0=ps[:], in1=src, op=ALU.mult),
                              r=[pt, "c_cosT", "mod"], w=["t1024_0"])
                        P.dve(lambda e: e.tensor_copy(out=dst[:, hd, cols], in_=t1[:, 0:512]), r=["t1024_0"], w=[otok])
                        return
                    P.dve(lambda e: e.tensor_tensor(out=t1[:, 0:512], in0=ps[:], in1=self.cosT[:, cols], op=ALU.mult),
                          r=[pt, "c_cosT"], w=["t1024_0"])
                    P.dve(lambda e: e.tensor_tensor(out=t1[:, 512:1024], in0=ps2[:], in1=self.sinT[:, cols], op=ALU.mult),
                          r=[pt2, "c_sinT"], w=["t1024_0"])
                    P.dve(lambda e: e.tensor_tensor(out=dst[:, hd, cols], in0=t1[:, 0:512], in1=t1[:, 512:1024], op=ALU.add),
                          r=["t1024_0"], w=[otok])
                self.proj_feat(pa, slot, wt, hd, consume)
        slot, wt = next(ws)
        self.proj_tok(pa, slot, wt, lambda ps, pt, i: P.act(lambda e: e.copy(out=self.rv[:, i, :], in_=ps[:]),
                                                            r=[pt], w=[f"rv{i}"]))
        slot, wt = next(ws)
        self.proj_tok(pa, slot, wt, lambda ps, pt, i: P.act(lambda e: e.activation(out=self.rg[:, i, :], in_=ps[:], func=AF.Silu),
                                                            r=[pt], w=[f"rg{i}"]))
        for _ in ws:
            pass
        NHD = 4
        if self.stop_after == "retproj":
            return
        for s in range(pa.n_seq):
            if pa.kind == "s":
                P.dma("sp", lambda e: e.dma_start(out=self.Sm[:], in_=self.sret[l].rearrange("a h d e -> d a h e")), w=["Sm0", "Sm1"])
            else:
                P.pool(lambda e: e.memset(self.Sm[:], 0.0), w=["Sm0", "Sm1"])
            for ii in range(pa.CPS):
                for di in range(2):
                    i = ii if di == 0 else pa.CPS - 1 - ii
                    ci = s * pa.CPS + i
                    tks = slice(ci * 128, (ci + 1) * 128)
                    th = (ci * 128) // 512
                    P.act(lambda e, ci=ci, di=di: e.copy(out=self.Sin[:, ci, di, :, :], in_=self.Sm[:, di, :, :]),
                          r=[f"Sm{di}"], w=[f"Sin{ci}_{di}"])
                    if ii == pa.CPS - 1 and pa.kind == "s":
                        continue
                    pb, pbt = self.next_b()
                    for hd in range(NHD):
                        P.pe(lambda e, hd=hd, pb=pb, tks=tks: e.transpose(out=pb[:, hd, :], in_=self.rkT[:, hd, tks],
                                                                          identity=self.ident[:]),
                             r=[f"rkT{hd}_{th}", "ident"], w=[pbt])
                    kdall = self.kdall[di]
                    kdat = f"kdall{di}"
                    for hd in range(NHD):
                        col = di * 4 + hd
                        if hd % 2 == 0:
                            P.act(lambda e, hd=hd, col=col, pb=pb, kdall=kdall: e.activation(
                                out=kdall[:, hd, :], in_=pb[:, hd, :], func=AF.Copy, scale=self.rtab[:, 2, col:col + 1]),
                                r=[pbt, "rtab"], w=[kdat])
                        else:
                            P.dve(lambda e, hd=hd, col=col, pb=pb, kdall=kdall: e.tensor_scalar(
                                out=kdall[:, hd, :], in0=pb[:, hd, :], scalar1=self.rtab[:, 2, col:col + 1], scalar2=None,
                                op0=ALU.mult), r=[pbt, "rtab"], w=[kdat])
                    ps, pt = self.next_f()
                    for hd in range(NHD):
                        P.pe(lambda e, hd=hd, ps=ps, kdall=kdall, ci=ci: e.matmul(
                            ps[:, hd * 128:(hd + 1) * 128], lhsT=kdall[:, hd, :], rhs=self.rv[:, ci, hd * 128:(hd + 1) * 128],
                            start=True, stop=True), r=[kdat, f"rv{ci}"], w=[pt])
                    for hd in range(NHD):
                        col = di * 4 + hd
                        P.dve(lambda e, hd=hd, col=col, ps=ps, di=di: e.scalar_tensor_tensor(
                            out=self.Sm[:, di, hd, :], in0=self.Sm[:, di, hd, :], scalar=self.rtab[:, 4, col:col + 1],
                            in1=ps[:, hd * 128:(hd + 1) * 128], op0=ALU.mult, op1=ALU.add),
                            r=[f"Sm{di}", "rtab", pt], w=[f"Sm{di}"])
            if pa.kind == "p":
                P.dma("sp", lambda e, s=s: e.dma_start(out=self.nsr[s, l].rearrange("a h d e -> d a h e"), in_=self.Sm[:]),
                      r=["Sm0", "Sm1"], w=["nsr"])
        if self.stop_after == "retA":
            return
        for ci in range(pa.n_seq * pa.CPS):
            i = ci % pa.CPS
            tks = slice(ci * 128, (ci + 1) * 128)
            th = (ci * 128) // 512
            osb = self.osb[ci % 2]
            ost = f"osb{ci % 2}"
            banks = [self.next_f() for _ in range(NHD)]
            ocs = [slice(hd * 128, (hd + 1) * 128) for hd in range(NHD)]
            for hd in range(NHD):
                ps, pt = banks[hd]
                P.pe(lambda e, hd=hd, ps=ps: e.matmul(ps[:, 0:128], lhsT=self.rkT[:, hd, tks], rhs=self.rqT[:, hd, tks], start=True, stop=True),
                     r=[f"rkT{hd}_{th}", f"rqT{hd}_{th}"], w=[pt])
            for hd in range(NHD):
                ps, pt = banks[hd]
                P.dve(lambda e, hd=hd, ps=ps: e.tensor_tensor(out=self.PT4[hd][:], in0=ps[:, 0:128], in1=self.Dc[:, hd, :], op=ALU.mult),
                      r=[pt, "Dc"], w=[f"PT{hd}"])
            for hd in range(NHD):
                ps, pt = banks[hd]
                P.pe(lambda e, hd=hd, ps=ps: e.matmul(ps[:, 128:256], lhsT=self.PT4[hd][:], rhs=self.rv[:, ci, hd * 128:(hd + 1) * 128],
                                                      start=True, stop=True), r=[f"PT{hd}", f"rv{ci}"], w=[pt])
                P.pe(lambda e, hd=hd, ps=ps: e.matmul(ps[:, 256:384], lhsT=self.rqT[:, hd, tks], rhs=self.Sin[:, ci, 0, hd, :], start=True, stop=True),
                     r=[f"rqT{hd}_{th}", f"Sin{ci}_0"], w=[pt])
                P.pe(lambda e, hd=hd, ps=ps: e.matmul(ps[:, 384:512], lhsT=self.rqT[:, hd, tks], rhs=self.Sin[:, ci, 1, hd, :], start=True, stop=True),
                     r=[f"rqT{hd}_{th}", f"Sin{ci}_1"], w=[pt])
            for hd in range(NHD):
                ps, pt = banks[hd]
                P.act(lambda e, hd=hd, ps=ps: e.activation(out=osb[:, ocs[hd]], in_=ps[:, 256:384], func=AF.Copy, scale=self.rtab[:, 0, hd:hd + 1]),
                      r=[pt, "rtab"], w=[ost + f"h{hd}"])
            for hd in range(NHD):
                ps, pt = banks[hd]
                P.dve(lambda e, hd=hd, ps=ps: e.scalar_tensor_tensor(out=osb[:, ocs[hd]], in0=ps[:, 384:512], scalar=self.rtab[:, 0, 4 + hd:5 + hd],
                                                                     in1=osb[:, ocs[hd]], op0=ALU.mult, op1=ALU.add),
                      r=[pt, "rtab", ost + f"h{hd}"], w=[ost + f"h{hd}"])
            for hd in range(NHD):
                ps, pt = banks[hd]
                P.dve(lambda e, hd=hd, ps=ps: e.tensor_tensor(out=osb[:, ocs[hd]], in0=osb[:, ocs[hd]], in1=ps[:, 128:256], op=ALU.add),
                      r=[pt, ost + f"h{hd}"], w=[ost + f"h{hd}"])
            for hd in range(NHD):
                P.dve(lambda e, hd=hd: e.bn_stats(out=self.bnst[:, hd, 0:6], in_=osb[:, ocs[hd]]), r=[ost + f"h{hd}"], w=[f"bnst{hd}"])
            for hd in range(NHD):
                P.dve(lambda e, hd=hd: e.bn_aggr(out=self.bnag[:, hd, 0:2], in_=self.bnst[:, hd, 0:6]), r=[f"bnst{hd}"], w=[f"bnag{hd}"])
            var4 = self.bnag[:, 0:4, 1]
            P.act(lambda e: e.activation(out=var4, in_=var4, func=AF.Ln, bias=self.epsc[:, 0:1], scale=1.0),
                  r=[f"bnag{hd}" for hd in range(NHD)] + ["epsc"], w=[f"bnag{hd}" for hd in range(NHD)])
            P.act(lambda e: e.activation(out=var4, in_=var4, func=AF.Exp, scale=-0.5),
                  r=[f"bnag{hd}" for hd in range(NHD)], w=[f"bnag{hd}" for hd in range(NHD)])
            for hd in range(NHD):
                P.dve(lambda e, hd=hd: e.tensor_scalar(out=osb[:, ocs[hd]], in0=osb[:, ocs[hd]], scalar1=self.bnag[:, hd, 0:1],
                                                       scalar2=self.bnag[:, hd, 1:2], op0=ALU.subtract, op1=ALU.mult),
                      r=[f"bnag{hd}", ost + f"h{hd}"], w=[ost + f"h{hd}"])
            hb = self.hb[ci % 2]
            hbt = f"hb{ci % 2}"
            P.dve(lambda e, osb=osb, hb=hb, ci=ci: e.tensor_tensor(out=hb[:, 0:512], in0=osb[:], in1=self.rg[:, ci, :], op=ALU.mult),
                  r=[ost + f"h{h}" for h in range(4)] + [f"rg{ci}"], w=[hbt])
            pb, pbt = self.next_b()
            for c in range(4):
                P.pe(lambda e, c=c, hb=hb, pb=pb: e.transpose(out=pb[:, c, :], in_=hb[:, c * 128:(c + 1) * 128],
                                                              identity=self.ident[:]), r=[hbt, "ident"], w=[pbt])
            P.act(lambda e, pb=pb, tks=tks: e.copy(out=self.routT[:, :, tks], in_=pb[:, 0:4, :]), r=[pbt], w=[f"routT{ci}"])


    def alloc_s5(self, pa):
        av = self.av
        J = pa.T // 8
        self.su_bm = av([128, 32, 8, 16], BF16)
        self.Yacc = av([128, 32, J], F32)
        self.s5_ymark = self.aoff
        self.Ub = av([128, 8, J], BF16)
        self.KK = av([128, 8, 128], BF16)
        self.QQ = av([128, 8, 128], BF16)
        self.MinT = av([128, 8, 128], BF16)
        self.MinTs = av([128, 8, 128], BF16)
        self.Mintra = av([128, 8, 128], BF16)
        self.Min = av([128, 8, 128], BF16)
        self.Mins = av([128, 8, 128], BF16)
        self.Et = av([128, 3, 8, 32], F32)
        self.braw = av([128, 2, 8, 16], F32)
        self.craw = av([128, 2, 2, 64], F32)
        self.cT = av([128, 2, 8, 16], F32)
        self.araw = av([64, 2, 2, 64], F32)
        self.aT = av([128, 6, 64], F32)
        self.qq = av([128, 4, 8], F32)
        self.t4 = self.mod[:, 0:2048].rearrange("p (a b c d) -> p a b c d", a=2, b=8, c=8)
        self.A8 = av([128, 3, 32], F32)
        NL = 8
        self.Xs = [av([128, 32, NL], F32) for _ in range(2)]
        self.Xw = [av([128, 32, NL], F32) for _ in range(2)]
        self.rt = [av([128, 32, NL], F32) for _ in range(4)]
        self.Xin = av([128, 32, NL], F32)
        self.Xinw = av([128, 32, NL], F32)
        self.Cs = [av([128, 32, pa.n_seq], F32) for _ in range(4)]
        self.PRt = av([128, 32, 17], F32)
        self.PIt = av([128, 32, 17], F32)
        self.PWt = av([128, 4, 8, 17], F32)
        self.AS = av([128, 3, 32], F32)
        self.tcor = av([128, 32, 16], F32)
        self.BBs = self.hb[0][:, :].bitcast(F32).rearrange("p (a b c) -> p a b c", a=4, b=8)
        self.Etmp = self.hb[1][:, :].bitcast(F32).rearrange("p (a b c) -> p a b c", a=2, b=8)
        self.dbc = self.t1024[0][:, 512:1024]

    def s5(self, pa, l):
        P = self.P
        W = self.W
        J = pa.T // 8
        Jps = pa.L // 8
        NSEQ = pa.n_seq
        TWO_PI = 2.0 * math.pi
        iA = (self.wcnt - 1) % self.NS if self.pref else self.wcnt % self.NS
        (slot, wt), = list(self.wstream([(W["w_in"][l][:, 2048:2560], 8, 512)]))
        i1, i2 = (iA + 1) % self.NS, (iA + 2) % self.NS
        tokV, tokW = f"wslot{i1}", f"wslot{i2}"
        flat = lambda t: t[:, :, :].rearrange("p a b -> p (a b)")[:, 0:32 * J].rearrange("p (g j) -> p g j", j=J)
        self.Vv, self.Vsw = flat(self.wring[i1]), flat(self.wring[i2])
        self.wcnt += 2
        tokM = f"wslot{iA}"
        self.Mout = self.wring[iA][:, :, :].rearrange("p a b -> p (a b)").rearrange("p (g k) -> p g k", k=128)
        for t0 in range(8):
            ps, pt = self.next_f()
            for c in range(8):
                P.pe(lambda e, c=c, ps=ps, t0=t0, slot=slot: e.matmul(ps[0:J, :], lhsT=self.hT[:, c, t0:pa.T:8], rhs=slot[:, c, :],
                                                           start=(c == 0), stop=(c == 7)),
                     r=[wt] + [f"hT{i}" for i in range(pa.NT)], w=[pt])
            P.act(lambda e, ps=ps, t0=t0: e.copy(out=self.su_bm[0:J, :, t0, :], in_=ps[0:J, :].rearrange("p (g c) -> p g c", c=16)), r=[pt], w=["su_bm"])
        if "s5u" in self.debug and l == 0:
            o2 = self.dbg_out(f"{pa.kind}_subm0", [128, 4096])
            P.dve(lambda e: e.tensor_copy(out=self.t1024[0][0:J, :], in_=self.su_bm[0:J, 0:8, :, :].rearrange("p g a b -> p (g a b)")), r=["su_bm"], w=["t1024_0"])
            P.dma("sp", lambda e, o2=o2: e.dma_start(out=o2[0:J, 0:1024], in_=self.t1024[0][0:J, :]), r=["t1024_0"], w=["dbg"])
        if self.stop_after == "s5u":
            return
        for k, nm in enumerate(("ssm_a_re", "ssm_a_im")):
            for dup in range(2):
                P.dma("sp", lambda e, k=k, nm=nm, dup=dup: e.dma_start(out=self.araw[0:64, k, dup, :],
                                                                       in_=W[nm][l].rearrange("d g p -> (d g) p")), w=["araw"])
        for k in range(2):
            ps, pt = self.next_f()
            P.pe(lambda e, k=k, ps=ps: e.transpose(out=ps[:, 0:64], in_=self.araw[0:64, k, :, :], identity=self.ident_f[0:64, 0:64]),
                 r=["araw", "ident_f"], w=[pt])
            P.act(lambda e, k=k, ps=ps: e.copy(out=self.aT[:, k, :], in_=ps[:, 0:64]), r=[pt], w=["aT"])
        P.dma("sp", lambda e: e.dma_start(out=self.aT[:, 2, :], in_=W["ssm_log_dt"][l].rearrange("d g -> (d g)").partition_broadcast(128)),
              w=["aT"])
        P.act(lambda e: e.activation(out=self.aT[:, 2, :], in_=self.aT[:, 2, :], func=AF.Exp), r=["aT"], w=["aT"])
        P.dve(lambda e: e.tensor_scalar(out=self.aT[:, 0, :], in0=self.aT[:, 0, :], scalar1=-1e-4, scalar2=None, op0=ALU.min),
              r=["aT"], w=["aT"])
        P.dve(lambda e: e.tensor_tensor(out=self.aT[:, 3, :], in0=self.aT[:, 0, :], in1=self.aT[:, 2, :], op=ALU.mult), r=["aT"], w=["aT"])
        P.dve(lambda e: e.tensor_tensor(out=self.aT[:, 4, :], in0=self.aT[:, 1, :], in1=self.aT[:, 2, :], op=ALU.mult), r=["aT"], w=["aT"])
        P.dve(lambda e: e.tensor_tensor(out=self.aT[:, 5, :], in0=self.aT[:, 0, :], in1=self.aT[:, 0, :], op=ALU.mult), r=["aT"], w=["aT"])
        P.dve(lambda e: e.tensor_tensor(out=self.aT[:, 2, :], in0=self.aT[:, 1, :], in1=self.aT[:, 1, :], op=ALU.mult), r=["aT"], w=["aT"])
        P.dve(lambda e: e.tensor_tensor(out=self.aT[:, 5, :], in0=self.aT[:, 5, :], in1=self.aT[:, 2, :], op=ALU.add), r=["aT"], w=["aT"])
        P.dve(lambda e: e.reciprocal(out=self.aT[:, 5, :], in_=self.aT[:, 5, :]), r=["aT"], w=["aT"])
        P.dma("sp", lambda e: e.dma_start(out=self.dbc[:], in_=W["ssm_d"][l].partition_broadcast(128)), w=["dbc"])

        def do_batch(d, b):
            gsl = slice(d * 32 + b * 8, d * 32 + b * 8 + 8)
            gen0 = (pa.kind == "p") or ("nocache" in self.debug)
            if (not gen0) and b % 2 == 1:
                Ubuf, ubt = self.MinTs[:, :, 0:J], "UbB"
            else:
                Ubuf, ubt = self.Ub, "Ub"
            pb, pbt = self.next_b()
            for gi in range(8):
                g = b * 8 + gi
                P.pe(lambda e, gi=gi, g=g, pb=pb: e.transpose(out=pb[:, gi, 0:J], in_=self.su_bm[0:J, g, :, :].rearrange("p a b -> p (a b)"),
                                                              identity=self.ident[0:J, 0:J]), r=["su_bm", "ident"], w=[pbt])
            P.act(lambda e, pb=pb: e.copy(out=Ubuf, in_=pb[:, :, 0:J]), r=[pbt], w=[ubt])
            gen = (pa.kind == "p") or ("nocache" in self.debug)
            Mi, Mn, Ms, sfx = self.Mintra, self.Min, self.Mins, ""
            ctok = f"s5c{l}_{d}_{b}"
            f2 = lambda v: v.rearrange("p a b -> p (a b)")
            if gen:
                lrdt = self.aT[:, 3, gsl].unsqueeze(2).to_broadcast([128, 8, 32])
                lidt = self.aT[:, 4, gsl].unsqueeze(2).to_broadcast([128, 8, 32])
                evb = self.ev[:, d, :].unsqueeze(1).to_broadcast([128, 8, 32])
                Et, Etmp = self.Et, self.Etmp
                P.dve(lambda e: e.tensor_tensor(out=Etmp[:, 0], in0=lrdt, in1=evb, op=ALU.mult), r=["aT", "c_ev"], w=["Etmp"])
                P.act(lambda e: e.activation(out=Etmp[:, 0], in_=Etmp[:, 0], func=AF.Exp), r=["Etmp"], w=["Etmp"])
                P.dve(lambda e: e.tensor_tensor(out=Etmp[:, 1], in0=lidt, in1=evb, op=ALU.mult), r=["aT", "c_ev", "Etmp"], w=["Etmp"])
                MAGIC = 12582912.0
                PI_LO = 3.1415925
                P.dve(lambda e: e.tensor_copy(out=Et[:, 1], in_=Etmp[:, 1]), r=["Etmp"], w=["Et"])
                P.dve(lambda e: e.tensor_scalar(out=Et[:, 0], in0=Etmp[:, 1], scalar1=0.5 * math.pi, scalar2=None, op0=ALU.add),
                      r=["Etmp", "Et"], w=["Et"])
                P.dve(lambda e: e.tensor_scalar(out=Et[:, 2, :, :], in0=Et[:, 0, :, :], scalar1=1.0 / TWO_PI, scalar2=MAGIC, op0=ALU.mult, op1=ALU.add),
                      r=["Et"], w=["Et"])
                P.dve(lambda e: e.tensor_scalar(out=Etmp[:, 1], in0=Et[:, 1, :, :], scalar1=1.0 / TWO_PI, scalar2=MAGIC, op0=ALU.mult, op1=ALU.add),
                      r=["Et", "Etmp"], w=["Etmp"])
                P.dve(lambda e: e.tensor_scalar(out=Et[:, 2], in0=Et[:, 2], scalar1=-MAGIC, scalar2=None, op0=ALU.add), r=["Et"], w=["Et"])
                P.dve(lambda e: e.tensor_scalar(out=Etmp[:, 1], in0=Etmp[:, 1], scalar1=-MAGIC, scalar2=None, op0=ALU.add), r=["Etmp"], w=["Etmp"])
                P.dve(lambda e: e.scalar_tensor_tensor(out=Et[:, 0], in0=Et[:, 2], scalar=-TWO_PI, in1=Et[:, 0], op0=ALU.mult, op1=ALU.add),
                      r=["Et"], w=["Et"])
                P.dve(lambda e: e.scalar_tensor_tensor(out=Et[:, 1], in0=Etmp[:, 1], scalar=-TWO_PI, in1=Et[:, 1], op0=ALU.mult, op1=ALU.add),
                      r=["Et", "Etmp"], w=["Et"])
                P.dve(lambda e: e.tensor_scalar(out=Et[:, 0:2], in0=Et[:, 0:2], scalar1=PI_LO, scalar2=None, op0=ALU.min), r=["Et"], w=["Et"])
                P.dve(lambda e: e.tensor_scalar(out=Et[:, 0:2], in0=Et[:, 0:2], scalar1=-PI_LO, scalar2=None, op0=ALU.max), r=["Et"], w=["Et"])
                P.act(lambda e: e.activation(out=Et[:, 0:2], in_=Et[:, 0:2], func=AF.Sin), r=["Et"], w=["Et"])
                P.dve(lambda e: e.tensor_tensor(out=Et[:, 0], in0=Et[:, 0], in1=Etmp[:, 0], op=ALU.mult), r=["Et", "Etmp"], w=["Et"])
                P.dve(lambda e: e.tensor_tensor(out=Et[:, 1], in0=Et[:, 1], in1=Etmp[:, 0], op=ALU.mult), r=["Et", "Etmp"], w=["Et"])
                P.dve(lambda e: e.tensor_scalar(out=Et[:, 2], in0=Et[:, 1], scalar1=self.sgn[:, 0:1], scalar2=None, op0=ALU.mult),
                      r=["Et", "c_sgn"], w=["Et"])
                i1 = 16 + (0 if d == 0 else 7)
                i8 = 16 + (7 if d == 0 else 0)
                P.dve(lambda e, b=b, i8=i8: e.tensor_copy(out=self.A8[:, 0, b * 8:(b + 1) * 8], in_=Et[:, 0, :, i8]), r=["Et"], w=["A8"])
                P.dve(lambda e, b=b, i8=i8: e.tensor_copy(out=self.A8[:, 1, b * 8:(b + 1) * 8], in_=Et[:, 2, :, i8]), r=["Et"], w=["A8"])
                qq = self.qq
                lr = self.aT[:, 0, gsl]
                li = self.aT[:, 1, gsl]
                rden = self.aT[:, 5, gsl]
                P.dve(lambda e, i1=i1: e.tensor_scalar(out=qq[:, 2, :], in0=Et[:, 0, :, i1], scalar1=-1.0, scalar2=None, op0=ALU.add),
                      r=["Et"], w=["qq"])
                P.dve(lambda e: e.tensor_tensor(out=qq[:, 0, :], in0=qq[:, 2, :], in1=lr, op=ALU.mult), r=["qq", "aT"], w=["qq"])
                P.dve(lambda e, i1=i1: e.tensor_tensor(out=qq[:, 3, :], in0=Et[:, 1, :, i1], in1=li, op=ALU.mult), r=["Et", "aT", "qq"], w=["qq"])
                P.dve(lambda e: e.tensor_tensor(out=qq[:, 0, :], in0=qq[:, 0, :], in1=qq[:, 3, :], op=ALU.add), r=["qq"], w=["qq"])
                P.dve(lambda e: e.tensor_tensor(out=qq[:, 0, :], in0=qq[:, 0, :], in1=rden, op=ALU.mult), r=["qq", "aT"], w=["qq"])
                P.dve(lambda e, i1=i1: e.tensor_tensor(out=qq[:, 1, :], in0=Et[:, 1, :, i1], in1=lr, op=ALU.mult), r=["Et", "aT", "qq"], w=["qq"])
                P.dve(lambda e: e.tensor_tensor(out=qq[:, 3, :], in0=qq[:, 2, :], in1=li, op=ALU.mult), r=["qq", "aT"], w=["qq"])
                P.dve(lambda e: e.tensor_tensor(out=qq[:, 1, :], in0=qq[:, 1, :], in1=qq[:, 3, :], op=ALU.subtract), r=["qq"], w=["qq"])
                P.dve(lambda e: e.tensor_tensor(out=qq[:, 1, :], in0=qq[:, 1, :], in1=rden, op=ALU.mult), r=["qq", "aT"], w=["qq"])
                for k, nm in enumerate(("ssm_b_re", "ssm_b_im")):
                    for dup in range(2):
                        P.dma("sp", lambda e, k=k, nm=nm, dup=dup, b=b: e.dma_start(
                            out=self.braw[dup * 64:(dup + 1) * 64, k, :, :],
                            in_=W[nm][l, b * 8:(b + 1) * 8].rearrange("g p c -> p g c")), w=["braw"])
                for k, nm in enumerate(("ssm_c_re", "ssm_c_im")):
                    for dup in range(2):
                        P.dma("sp", lambda e, k=k, nm=nm, dup=dup, b=b, d=d: e.dma_start(
                            out=self.craw[:, k, dup, :], in_=W[nm][l, d, b * 8:(b + 1) * 8].rearrange("g c p -> (g c) p")), w=["craw"])
                for k in range(2):
                    ps, pt = self.next_f()
                    P.pe(lambda e, k=k, ps=ps: e.transpose(out=ps[:, 0:128], in_=self.craw[:, k, :, :], identity=self.ident_f[:]),
                         r=["craw", "ident_f"], w=[pt])
                    P.act(lambda e, k=k, ps=ps: e.copy(out=self.cT[:, k, :, :], in_=ps[:, 0:128].rearrange("p (g c) -> p g c", c=16)), r=[pt], w=["cT"])
                BB = self.BBs
                qrb = qq[:, 0, :].unsqueeze(2).to_broadcast([128, 8, 16])
                qib = qq[:, 1, :].unsqueeze(2).to_broadcast([128, 8, 16])
                sc = self.t4
                s_bbr, s_bbi, s_t = sc[:, 0, 0], sc[:, 0, 1], sc[:, 0, 2]
                P.dve(lambda e: e.tensor_tensor(out=s_bbr, in0=self.braw[:, 0], in1=qrb, op=ALU.mult), r=["braw", "qq"], w=["t4"])
                P.dve(lambda e: e.tensor_tensor(out=s_t, in0=self.braw[:, 1], in1=qib, op=ALU.mult), r=["braw", "qq", "t4"], w=["t4"])
                P.dve(lambda e: e.tensor_tensor(out=s_bbr, in0=s_bbr, in1=s_t, op=ALU.subtract), r=["t4"], w=["t4"])
                P.dve(lambda e: e.tensor_tensor(out=s_bbi, in0=self.braw[:, 1], in1=qrb, op=ALU.mult), r=["braw", "qq", "t4"], w=["t4"])
                P.dve(lambda e: e.tensor_tensor(out=s_t, in0=self.braw[:, 0], in1=qib, op=ALU.mult), r=["braw", "qq", "t4"], w=["t4"])
                P.dve(lambda e: e.tensor_tensor(out=s_bbi, in0=s_bbi, in1=s_t, op=ALU.add), r=["t4"], w=["t4"])
                lo, hi = slice(0, 64), slice(64, 128)
                P.dve(lambda e: e.tensor_copy(out=BB[lo, 0], in_=s_bbr[lo]), r=["t4"], w=["BBs"])
                P.dve(lambda e: e.tensor_copy(out=BB[hi, 0], in_=s_bbi[hi]), r=["t4"], w=["BBs"])
                P.dve(lambda e: e.tensor_copy(out=BB[lo, 1], in_=s_bbi[lo]), r=["t4"], w=["BBs"])
                P.dve(lambda e: e.tensor_copy(out=BB[hi, 1], in_=s_bbr[hi]), r=["t4"], w=["BBs"])
                P.dve(lambda e: e.tensor_copy(out=BB[lo, 2], in_=self.cT[lo, 0]), r=["cT"], w=["BBs"])
                P.dve(lambda e: e.tensor_scalar(out=BB[hi, 2], in0=self.cT[hi, 1], scalar1=-1.0, scalar2=None, op0=ALU.mult), r=["cT"], w=["BBs"])
                P.dve(lambda e: e.tensor_scalar(out=BB[lo, 3], in0=self.cT[lo, 1], scalar1=-1.0, scalar2=None, op0=ALU.mult), r=["cT"], w=["BBs"])
                P.dve(lambda e: e.tensor_scalar(out=BB[hi, 3], in0=self.cT[hi, 0], scalar1=-1.0, scalar2=None, op0=ALU.mult), r=["cT"], w=["BBs"])

                def build(out, tab, X1, X2, esel, neg=False):
                    erb = Et[:, 0, :, tab * 8:(tab + 1) * 8].unsqueeze(3).to_broadcast([128, 8, 8, 16])
                    esb = Et[:, esel, :, tab * 8:(tab + 1) * 8].unsqueeze(3).to_broadcast([128, 8, 8, 16])
                    x1 = X1.unsqueeze(2).to_broadcast([128, 8, 8, 16])
                    x2 = X2.unsqueeze(2).to_broadcast([128, 8, 8, 16])
                    ov = out.rearrange("p g (s c) -> p g s c", c=16)
                    P.dve(lambda e: e.tensor_tensor(out=sc[:, 0], in0=x1, in1=erb, op=ALU.mult), r=["BBs", "Et", "t4"], w=["t4"])
                    P.dve(lambda e: e.tensor_tensor(out=sc[:, 1], in0=x2, in1=esb, op=ALU.mult), r=["BBs", "Et", "t4"], w=["t4"])
                    P.dve(lambda e: e.tensor_tensor(out=ov, in0=sc[:, 0], in1=sc[:, 1], op=(ALU.subtract if neg else ALU.add)),
                          r=["t4"], w=["S5tab"])
                build(self.QQ, 0, BB[:, 2], BB[:, 3], 1)
                build(self.KK, 1, BB[:, 0], BB[:, 1], 2)
                build(self.Mout[:, b * 8:(b + 1) * 8, :], 2, BB[:, 2], BB[:, 3], 1)
                build(self.MinT, 3, BB[:, 0], BB[:, 1], 2)
                build(self.MinTs, 3, BB[:, 1], BB[:, 0], 2, neg=True)
                mask = self.mf if d == 0 else self.mb
                mtok = "c_mf" if d == 0 else "c_mb"
                for gi in range(8):
                    g = b * 8 + gi
                    ps, pt = self.next_f()
                    P.pe(lambda e, gi=gi, ps=ps: e.matmul(ps[:, 0:128], lhsT=self.KK[:, gi, :], rhs=self.QQ[:, gi, :], start=True, stop=True),
                         r=["S5tab"], w=[pt])
                    P.dve(lambda e, gi=gi, ps=ps, mask=mask: e.tensor_tensor(out=self.Mintra[:, gi, :], in0=ps[:, 0:128], in1=mask[:], op=ALU.mult),
                          r=[pt, mtok], w=[f"Mintra{gi}"])
                pb, pbt = self.next_b()
                for gi in range(8):
                    P.pe(lambda e, gi=gi, pb=pb: e.transpose(out=pb[:, gi, :], in_=self.MinT[:, gi, :], identity=self.ident[:]),
                         r=["S5tab", "ident"], w=[pbt])
                P.act(lambda e, pb=pb: e.copy(out=self.Min[:], in_=pb[:]), r=[pbt], w=["Min"])
                pb, pbt = self.next_b()
                for gi in range(8):
                    P.pe(lambda e, gi=gi, pb=pb: e.transpose(out=pb[:, gi, :], in_=self.MinTs[:, gi, :], identity=self.ident[:]),
                         r=["S5tab", "ident"], w=[pbt])
                P.act(lambda e, pb=pb: e.copy(out=self.Mins[:], in_=pb[:]), r=[pbt], w=["Mins"])

                if pa.kind == "p":
                    for k, (tab, rt_) in enumerate(((self.Mintra, [f"Mintra{gi}" for gi in range(8)]), (self.Min, ["Min"]), (self.Mins, ["Mins"]),
                                                   (self.Mout[:, b * 8:(b + 1) * 8, :], ["S5tab", tokM]))):
                        P.dma("sp", lambda e, k=k, tab=tab: e.dma_start(out=self.s5c[l, d, b, k], in_=f2(tab)), r=rt_, w=[ctok + f"_{k}"])
                    if b == 3:
                        P.dma("sp", lambda e: e.dma_start(out=self.s5a[l, d], in_=self.A8[:, 0:2, :]), r=["A8"], w=[f"s5a{l}_{d}"])
            else:
                if b % 2 == 1:
                    Mi, Mn, Ms, sfx = self.KK, self.QQ, self.MinT, "B"
                for k, (tab, wt_) in enumerate(((Mi, [f"Mintra{sfx}{gi}" for gi in range(8)]), (Mn, ["Min" + sfx]), (Ms, ["Mins" + sfx]),
                                               (self.Mout[:, b * 8:(b + 1) * 8, :], ["S5tab", tokM]))):
                    P.dma("sp", lambda e, k=k, tab=tab: e.dma_start(out=f2(tab), in_=self.s5c[l, d, b, k]), r=[ctok + f"_{k}"], w=wt_)
                if b == 0:
                    P.dma("sp", lambda e: e.dma_start(out=self.A8[:, 0:2, :], in_=self.s5a[l, d]), r=[f"s5a{l}_{d}"], w=["A8"])
            for gi in range(8):
                g = b * 8 + gi
                ps, pt = self.next_f()
                P.pe(lambda e, gi=gi, ps=ps: e.matmul(ps[:, 0:J], lhsT=Mi[:, gi, :], rhs=Ubuf[:, gi, :], start=True, stop=True),
                     r=[f"Mintra{sfx}{gi}", ubt], w=[pt])
                P.pe(lambda e, gi=gi, ps=ps: e.matmul(ps[:, 128:128 + J], lhsT=Mn[:, gi, :], rhs=Ubuf[:, gi, :], start=True, stop=True),
                     r=["Min" + sfx, ubt], w=[pt])
                P.pe(lambda e, gi=gi, ps=ps: e.matmul(ps[:, 256:256 + J], lhsT=Ms[:, gi, :], rhs=Ubuf[:, gi, :], start=True, stop=True),
                     r=["Mins" + sfx, ubt], w=[pt])
                if d == 0:
                    P.act(lambda e, g=g, ps=ps: e.copy(out=self.Yacc[:, g, :], in_=ps[:, 0:J]), r=[pt], w=[f"Yacc{g}"])
                else:
                    P.dve(lambda e, g=g, ps=ps: e.tensor_tensor(out=self.Yacc[:, g, :], in0=self.Yacc[:, g, :], in1=ps[:, 0:J], op=ALU.add),
                          r=[pt, f"Yacc{g}"], w=[f"Yacc{g}"])
                P.act(lambda e, g=g, ps=ps: e.copy(out=self.Vv[:, g, :], in_=ps[:, 128:128 + J]), r=[pt], w=[tokV])
                P.act(lambda e, g=g, ps=ps: e.copy(out=self.Vsw[:, g, :], in_=ps[:, 256:256 + J]), r=[pt], w=[tokW])

        if self.stop_after == "s5a":
            return
        if self.stop_after == "s5b":
            do_batch(0, 0)
            f2 = lambda v: v.rearrange("p a b -> p (a b)")
            self.dump_view("aT", self.aT[:, :, 0:64].rearrange("p a b -> p (a b)"), 384, ["aT"])
            self.dump_view("ER", f2(self.Et[:, 0]), 256, ["Et"])
            self.dump_view("EI", f2(self.Et[:, 1]), 256, ["Et"])
            self.dump_view("qq", f2(self.qq[:, 0:2, :]), 16, ["qq"])
            self.dump_view("BB1", f2(self.BBs[:, 0]), 128, ["BBs"])
            self.dump_view("BB2", f2(self.BBs[:, 1]), 128, ["BBs"])
            self.dump_view("CC1", f2(self.BBs[:, 2]), 128, ["BBs"])
            self.dump_view("CC2", f2(self.BBs[:, 3]), 128, ["BBs"])
            self.dump_view("KK", f2(self.KK), 1024, ["S5tab"])
            self.dump_view("QQ", f2(self.QQ), 1024, ["S5tab"])
            self.dump_view("MinT", f2(self.MinT), 1024, ["S5tab"])
            self.dump_view("Mout", f2(self.Mout[:, 0:8, :]), 1024, ["S5tab"])
            self.dump_view("Mintra", f2(self.Mintra), 1024, [f"Mintra{g}" for g in range(8)])
            self.dump_view("Min", f2(self.Min), 1024, ["Min"])
            self.dump_view(tokV, f2(self.Vv[:, 0:8, :]), 8 * J, [tokV])
            self.dump_view("Yacc", f2(self.Yacc[:, 0:8, :]), 8 * J, [f"Yacc{g}" for g in range(8)])
            return
        for d in range(2):
            for b in range(4):
                if self.stop_after == "s5b" and (d, b) != (0, 0):
                    return
                do_batch(d, b)
            if self.stop_after == "s5c":
                return
            RE = P.dve
            A8 = self.A8
            S = 16 if Jps == 128 else 8
            G = Jps // S
            NL = NSEQ * G
            RE(lambda e: e.tensor_scalar(out=A8[:, 2, :], in0=A8[:, 1, :], scalar1=-1.0, scalar2=None, op0=ALU.mult), r=["A8"], w=["A8"])
            lst = 0 if d == 0 else (1 if S == 16 else 2)
            MAGIC = 12582912.0
            PI_LO = 3.1415925
            for c4 in range(4):
                gs4 = slice(d * 32 + c4 * 8, d * 32 + c4 * 8 + 8)
                n = S + 1
                lrb = self.aT[:, 3, gs4].unsqueeze(2).to_broadcast([128, 8, n])
                lib = self.aT[:, 4, gs4].unsqueeze(2).to_broadcast([128, 8, n])
                evb8 = self.ev8[:, lst, 0:n].unsqueeze(1).to_broadcast([128, 8, n])
                W0, W1, W2, W3 = (self.PWt[:, k, :, 0:n] for k in range(4))
                RE(lambda e: e.tensor_tensor(out=W0, in0=lrb, in1=evb8, op=ALU.mult), r=["aT", "c_ev8"], w=["PWt"])
                P.act(lambda e: e.activation(out=W0, in_=W0, func=AF.Exp), r=["PWt"], w=["PWt"])
                RE(lambda e: e.tensor_tensor(out=W1, in0=lib, in1=evb8, op=ALU.mult), r=["aT", "c_ev8", "PWt"], w=["PWt"])
                RE(lambda e: e.tensor_scalar(out=W2, in0=W1, scalar1=0.5 * math.pi, scalar2=None, op0=ALU.add), r=["PWt"], w=["PWt"])
                for Wx in (W1, W2):
                    RE(lambda e, Wx=Wx: e.tensor_scalar(out=W3, in0=Wx, scalar1=1.0 / TWO_PI, scalar2=MAGIC, op0=ALU.mult, op1=ALU.add), r=["PWt"], w=["PWt"])
                    RE(lambda e: e.tensor_scalar(out=W3, in0=W3, scalar1=-MAGIC, scalar2=None, op0=ALU.add), r=["PWt"], w=["PWt"])
                    RE(lambda e, Wx=Wx: e.scalar_tensor_tensor(out=Wx, in0=W3, scalar=-TWO_PI, in1=Wx, op0=ALU.mult, op1=ALU.add), r=["PWt"], w=["PWt"])
                    RE(lambda e, Wx=Wx: e.tensor_scalar(out=Wx, in0=Wx, scalar1=PI_LO, scalar2=None, op0=ALU.min), r=["PWt"], w=["PWt"])
                    RE(lambda e, Wx=Wx: e.tensor_scalar(out=Wx, in0=Wx, scalar1=-PI_LO, scalar2=None, op0=ALU.max), r=["PWt"], w=["PWt"])
                    P.act(lambda e, Wx=Wx: e.activation(out=Wx, in_=Wx, func=AF.Sin), r=["PWt"], w=["PWt"])
                gq = slice(c4 * 8, c4 * 8 + 8)
                RE(lambda e: e.tensor_tensor(out=self.PRt[:, gq, 0:n], in0=W2, in1=W0, op=ALU.mult), r=["PWt"], w=["PRt"])
                RE(lambda e: e.tensor_tensor(out=W1, in0=W1, in1=W0, op=ALU.mult), r=["PWt"], w=["PWt"])
                RE(lambda e: e.tensor_scalar(out=self.PIt[:, gq, 0:n], in0=W1, scalar1=self.sgn[:, 0:1], scalar2=None, op0=ALU.mult),
                   r=["PWt", "c_sgn"], w=["PIt"])
            AS = self.AS
            RE(lambda e: e.tensor_copy(out=AS[:, 0, :], in_=self.PRt[:, :, S]), r=["PRt"], w=["AS"])
            RE(lambda e: e.tensor_copy(out=AS[:, 1, :], in_=self.PIt[:, :, S]), r=["PIt"], w=["AS"])
            RE(lambda e: e.tensor_scalar(out=AS[:, 2, :], in0=AS[:, 1, :], scalar1=-1.0, scalar2=None, op0=ALU.mult), r=["AS"], w=["AS"])

            def cstep(X, Xw, Xn, Xwn, tk, vj, vwj, vtoks, Atab, atok, nl, store=None, stok=None, rts=None):
                xt, xwt, xnt, xwnt = tk
                Ab = [Atab[:, k, :].unsqueeze(2).to_broadcast([128, 32, nl]) for k in range(3)]
                r0, r1, r2, r3 = rts
                RE(lambda e: e.tensor_tensor(out=r0, in0=X, in1=Ab[0], op=ALU.mult), r=[xt, atok], w=["rt0"])
                RE(lambda e: e.tensor_tensor(out=r1, in0=Xw, in1=Ab[1], op=ALU.mult), r=[xwt, atok], w=["rt1"])
                RE(lambda e: e.tensor_tensor(out=r2, in0=Xw, in1=Ab[0], op=ALU.mult), r=[xwt, atok], w=["rt2"])
                RE(lambda e: e.tensor_tensor(out=r3, in0=X, in1=Ab[2], op=ALU.mult), r=[xt, atok], w=["rt3"])
                RE(lambda e: e.tensor_tensor(out=r0, in0=r0, in1=vj, op=ALU.add), r=["rt0"] + vtoks, w=["rt0"])
                RE(lambda e: e.tensor_tensor(out=r2, in0=r2, in1=vwj, op=ALU.add), r=["rt2"] + vtoks, w=["rt2"])
                if store is not None:
                    RE(lambda e: e.tensor_copy(out=store, in_=X), r=[xt, "rt0"], w=[stok])
                RE(lambda e: e.tensor_tensor(out=Xn, in0=r0, in1=r1, op=ALU.add), r=["rt0", "rt1"], w=[xnt])
                RE(lambda e: e.tensor_tensor(out=Xwn, in0=r2, in1=r3, op=ALU.add), r=["rt2", "rt3"], w=[xwnt])

            RE(lambda e: e.memset(self.Xs[0], 0.0), w=["X0"])
            RE(lambda e: e.memset(self.Xw[0], 0.0), w=["Xw0"])
            cur = 0
            for st in range(S):
                blk = st if d == 0 else S - 1 - st
                vj = self.Vv[:, :, blk:J:S]
                vwj = self.Vsw[:, :, blk:J:S]
                cstep(self.Xs[cur], self.Xw[cur], self.Xs[1 - cur], self.Xw[1 - cur],
                      (f"X{cur}", f"Xw{cur}", f"X{1 - cur}", f"Xw{1 - cur}"), vj, vwj, [tokV, tokW], A8, "A8", NL,
                      store=vj, stok=tokV, rts=self.rt)
                cur = 1 - cur
            Xend, Xwend = self.Xs[cur], self.Xw[cur]
            xet, xwet = f"X{cur}", f"Xw{cur}"
            XendV = Xend.rearrange("p g (q s) -> p g q s", s=G)
            XwendV = Xwend.rearrange("p g (q s) -> p g q s", s=G)
            XinV = self.Xin[:, :, 0:NL].rearrange("p g (q s) -> p g q s", s=G)
            XinwV = self.Xinw[:, :, 0:NL].rearrange("p g (q s) -> p g q s", s=G)
            C0, Cw0, C1, Cw1 = self.Cs
            if pa.kind == "s":
                for half in range(2):
                    hs = slice(half * 64, half * 64 + 64)
                    hw = slice((1 - half) * 64, (1 - half) * 64 + 64)
                    P.dma("sp", lambda e, half=half, hs=hs: e.dma_start(out=C0[hs, :, 0], in_=self.sssm[l, d, :, :, half].rearrange("g p -> p g"),
                                                                        allow_slow_non_contiguous=True), w=["C0"])
                    P.dma("sp", lambda e, half=half, hw=hw: e.dma_start(out=Cw0[hw, :, 0], in_=self.sssm[l, d, :, :, half].rearrange("g p -> p g"),
                                                                        allow_slow_non_contiguous=True), w=["Cw0"])
            else:
                RE(lambda e: e.memset(C0, 0.0), w=["C0"])
                RE(lambda e: e.memset(Cw0, 0.0), w=["Cw0"])
            cs_ = [(C0, Cw0, "C0", "Cw0"), (C1, Cw1, "C1", "Cw1")]
            cc = 0
            rts_c = [r[:, :, 0:NSEQ] for r in self.rt]
            for g_ in (range(G) if d == 0 else range(G - 1, -1, -1)):
                Cc, Cwc, ct, cwt = cs_[cc]
                Cn, Cwn, cnt, cwnt = cs_[1 - cc]
                RE(lambda e, Cc=Cc, g_=g_: e.tensor_copy(out=XinV[:, :, :, g_], in_=Cc), r=[ct], w=["Xin"])
                RE(lambda e, Cwc=Cwc, g_=g_: e.tensor_copy(out=XinwV[:, :, :, g_], in_=Cwc), r=[cwt], w=["Xinw"])
                cstep(Cc, Cwc, Cn, Cwn, (ct, cwt, cnt, cwnt), XendV[:, :, :, g_], XwendV[:, :, :, g_], [xet, xwet], AS, "AS", NSEQ, rts=rts_c)
                cc = 1 - cc
            Cfin, cft = cs_[cc][0], cs_[cc][2]
            if pa.kind == "p":
                for sq in range(NSEQ):
                    for half in range(2):
                        P.dma("sp", lambda e, sq=sq, half=half: e.dma_start(
                            out=self.nss[sq, l, d, :, :, half].rearrange("g p -> p g"), in_=Cfin[half * 64:(half + 1) * 64, :, sq],
                            allow_slow_non_contiguous=True), r=[cft], w=["nss"])
            tc = self.tcor[:, :, 0:S]
            for lane in range(NL):
                xh = self.Vv[:, :, lane * S:(lane + 1) * S]
                xb = self.Xin[:, :, lane:lane + 1].to_broadcast([128, 32, S])
                xwb = self.Xinw[:, :, lane:lane + 1].to_broadcast([128, 32, S])
                RE(lambda e, xb=xb: e.tensor_tensor(out=tc, in0=self.PRt[:, :, 0:S], in1=xb, op=ALU.mult), r=["PRt", "Xin"], w=["tcor"])
                RE(lambda e, xh=xh: e.tensor_tensor(out=xh, in0=xh, in1=tc, op=ALU.add), r=["tcor", tokV], w=[tokV])
                RE(lambda e, xwb=xwb: e.tensor_tensor(out=tc, in0=self.PIt[:, :, 0:S], in1=xwb, op=ALU.mult), r=["PIt", "Xinw", tokV], w=["tcor"])
                RE(lambda e, xh=xh: e.tensor_tensor(out=xh, in0=xh, in1=tc, op=ALU.add), r=["tcor", tokV], w=[tokV])
            if self.stop_after == "s5d":
                return
            for g in range(32):
                ps, pt = self.next_f()
                P.pe(lambda e, g=g, ps=ps: e.matmul(ps[:, 0:J], lhsT=self.Mout[:, g, :], rhs=self.Vv[:, g, :], start=True, stop=True),
                     r=["S5tab", tokV, tokM], w=[pt])
                P.dve(lambda e, g=g, ps=ps: e.tensor_tensor(out=self.Yacc[:, g, :], in0=self.Yacc[:, g, :], in1=ps[:, 0:J], op=ALU.add),
                      r=[pt, f"Yacc{g}"], w=[f"Yacc{g}"])
            if self.stop_after == "s5e":
                f2 = lambda v: v.rearrange("p a b -> p (a b)")
                self.dump_view("Xh", f2(self.Vv[:, 0:8, :]), 8 * J, [tokV])
                self.dump_view("Yf", f2(self.Yacc[:, 0:8, :]), 8 * J, [f"Yacc{g}" for g in range(8)])
                self.dump_view("A8", f2(self.A8[:, :, :]), 96, ["A8"])
                return
        if self.stop_after == "s5f":
            return
        self.P.barrier()
        ybm = self.av_at(self.s5_ymark, [128, 8, 512], F32)
        self.ygT = self.av_at(self.aoff_after, [128, 4, 1024], BF16)
        for gq in range(8):
            ps, pt = self.next_f()
            for k in range(4):
                g = gq * 4 + k
                P.pe(lambda e, g=g, k=k, ps=ps: e.transpose(out=ps[0:J, k * 128:(k + 1) * 128], in_=self.Yacc[:, g, :], identity=self.ident_f[:]),
                     r=[f"Yacc{g}", "ident_f"], w=[pt])
            for k in range(4):
                g = gq * 4 + k
                P.act(lambda e, g=g, k=k, ps=ps: e.copy(out=ybm[0:J, :, g * 16:(g + 1) * 16],
                                                        in_=ps[0:J, k * 128:(k + 1) * 128].rearrange("p (s c) -> p s c", c=16)),
                      r=[pt], w=["ybm"])
        if "s5y" in self.debug and l == 0:
            o = self.dbg_out(f"{pa.kind}_ybm", [128, 4096])
            P.dma("sp", lambda e, o=o: e.dma_start(out=o[0:J, :], in_=ybm[0:J, :, :].rearrange("p a b -> p (a b)")), r=["ybm"], w=["dbg"])
            o2 = self.dbg_out(f"{pa.kind}_subm", [128, 4096])
            P.dve(lambda e: e.tensor_copy(out=self.t1024[0][0:J, :], in_=self.su_bm[0:J, 0:8, :, :].rearrange("p g a b -> p (g a b)")), r=["su_bm"], w=["t1024_0"])
            P.dma("sp", lambda e, o2=o2: e.dma_start(out=o2[0:J, 0:1024], in_=self.t1024[0][0:J, :]), r=["t1024_0"], w=["dbg"])
        for t0 in range(8):
            t = self.t1024[0]
            P.dve(lambda e, t0=t0, t=t: e.tensor_tensor(out=t[0:J, 0:512].rearrange("p (g c) -> p g c", c=16), in0=self.su_bm[0:J, :, t0, :], in1=self.dbc[0:J, :].rearrange("p (g c) -> p g c", c=16), op=ALU.mult),
                  r=["su_bm", "dbc"], w=["t1024_0"])
            P.dve(lambda e, t0=t0, t=t: e.tensor_tensor(out=t[0:J, 0:512], in0=t[0:J, 0:512], in1=ybm[0:J, t0, :], op=ALU.add),
                  r=["t1024_0", "ybm"], w=["t1024_0"])
            hb = self.hb[t0 % 2]
            hbt = f"hb{t0 % 2}"
            P.act(lambda e, t=t, hb=hb: e.activation(out=hb[0:J, 0:512], in_=t[0:J, 0:512], func=AF.Gelu_apprx_tanh), r=["t1024_0"], w=[hbt])
            pb, pbt = self.next_b()
            for c in range(4):
                P.pe(lambda e, c=c, hb=hb, pb=pb: e.transpose(out=pb[:, c, 0:J], in_=hb[0:J, c * 128:(c + 1) * 128], identity=self.ident[0:J, 0:J]),
                     r=[hbt, "ident"], w=[pbt])
            P.act(lambda e, pb=pb, t0=t0: e.copy(out=self.ygT[:, :, t0:pa.T:8], in_=pb[:, 0:4, 0:J]), r=[pbt], w=["ygT"])
        if self.stop_after == "s5g":
            return
        (slot, wt), = list(self.wstream([(W["ssm_w_glu"][l], 4, 512)]))
        for cc in range(4):
            for th in range(pa.NH):
                cols = slice(th * 512, (th + 1) * 512)
                ps, pt = self.next_f()
                for k in range(4):
                    P.pe(lambda e, k=k, cc=cc, ps=ps, cols=cols: e.matmul(ps[:], lhsT=slot[:, k, cc * 128:(cc + 1) * 128], rhs=self.ygT[:, k, cols],
                                                                         start=(k == 0), stop=(k == 3)), r=[wt, "ygT"], w=[pt])
                t = self.t1024[0]
                P.act(lambda e, ps=ps, t=t: e.activation(out=t[:, 0:512], in_=ps[:], func=AF.Sigmoid), r=[pt], w=["t1024_0"])
                P.dve(lambda e, cc=cc, cols=cols, t=t: e.tensor_tensor(out=self.soutT[:, cc, cols], in0=t[:, 0:512], in1=self.ygT[:, cc, cols], op=ALU.mult),
                      r=["t1024_0", "ygT"], w=[f"soutT{cc}_{th}"])

    def dump_view(self, name, view, n, toks):
        P = self.P
        o = self.dbg_out(name, [128, 1024])
        t = self.t1024[0]
        P.dve(lambda e: e.tensor_copy(out=t[:, 0:n], in_=view), r=list(toks), w=["t1024_0"])
        P.dma("sp", lambda e: e.dma_start(out=o[:, 0:n], in_=t[:, 0:n]), r=["t1024_0"], w=["dbg"])

    def dump_x(self, pa, nm):
        o = self.dbg_out(f"{pa.kind}_{nm}", [pa.T, 1024])
        for i in range(pa.NT):
            self.P.dma("sp", lambda e, i=i: e.dma_start(out=o[i * 128:(i + 1) * 128, :], in_=self.xres[i][:]), r=[f"x{i}"], w=["dbg"])

    def av_at(self, off, shape, dtype):
        save = self.aoff
        self.aoff = off
        v = self.av(shape, dtype)
        self.aoff_after = self.aoff
        self.aoff = save
        return v


    def alloc_attn(self, pa):
        av = self.av
        self.nqT = av([128, 4, 1024], BF16)
        self.nkT = av([128, 4, 1024], BF16)
        self.nv = av([128, 8, 512], BF16)
        self.nout_tm = [av([128, 512], BF16) for _ in range(2)]
        self.asm = av([128, 16], F32)
        if pa.kind == "p":
            self.stg = [av([128, 512], F32) for _ in range(2)]
            self.Pb = [av([128, 256], BF16) for _ in range(2)]
            self.PTt = [av([128, 2, 128], BF16) for _ in range(2)]
        else:
            self.kctxT = av([128, 4, 512], BF16)
            self.vctx = av([128, 4, 512], BF16)
            self.Tpad = av([128, 8, 19, 64], BF16)
            self.scl = [av([128, 640], F32) for _ in range(2)]
            self.Pb = [av([128, 1152], BF16) for _ in range(2)]
            self.PTt = [av([128, 9, 128], BF16) for _ in range(2)]
            self.rowm = [av([128, 640], BF16) for _ in range(2)]

    def attn_proj(self, pa, l):
        P = self.P
        W = self.W
        w_in = W["w_in"][l]
        specs = [(w_in[:, 2560 + b * 512:2560 + (b + 1) * 512], 8, 512) for b in range(3)]
        ws = self.wstream(specs)
        slot, wt = next(ws)
        for j in range(4):
            def cq(ps, pt, th, j=j):
                cols = slice(th * 512, (th + 1) * 512)
                P.act(lambda e: e.copy(out=self.nqT[:, j, cols], in_=ps[:]), r=[pt], w=[f"nqT{j}_{th}"])
            self.proj_feat(pa, slot, wt, j, cq)
        slot, wt = next(ws)
        if pa.kind == "s":
            for j in range(4):
                def ck_(ps, pt, th, j=j):
                    cols = slice(th * 512, (th + 1) * 512)
                    P.act(lambda e: e.copy(out=self.nkT[:, j, cols], in_=ps[:]), r=[pt], w=[f"nkT{j}_{th}"])
                self.proj_feat(pa, slot, wt, j, ck_)
        else:
            def ck_(ps, pt, i):
                sq, ti = divmod(i, pa.L // 128)
                stg = self.stg[i % 2]
                st = f"stg{i % 2}"
                P.act(lambda e: e.copy(out=stg[:], in_=ps[:]), r=[pt], w=[st])
                P.dma("sp", lambda e: e.dma_start(out=self.nck[sq, l, ti * 128:(ti + 1) * 128, :], in_=stg[:]), r=[st], w=["nck"])
                hb = self.hb[i % 2]
                hbt = f"hb{i % 2}"
                P.dve(lambda e: e.tensor_copy(out=hb[:, 0:512], in_=ps[:]), r=[pt], w=[hbt])
                pb, pbt = self.next_b()
                for c in range(4):
                    P.pe(lambda e, c=c: e.transpose(out=pb[:, c, :], in_=hb[:, c * 128:(c + 1) * 128], identity=self.ident[:]),
                         r=[hbt, "ident"], w=[pbt])
                P.act(lambda e: e.copy(out=self.nkT[:, :, i * 128:(i + 1) * 128], in_=pb[:, 0:4, :]), r=[pbt], w=[f"nkT_t{i}"])
            self.proj_tok(pa, slot, wt, ck_)
        slot, wt = next(ws)

        def cv_(ps, pt, i):
            if pa.kind == "p":
                sq, ti = divmod(i, pa.L // 128)
                stg = self.stg[i % 2]
                st = f"stg{i % 2}"
                P.act(lambda e: e.copy(out=stg[:], in_=ps[:]), r=[pt], w=[st])
                P.dma("sp", lambda e: e.dma_start(out=self.ncv[sq, l, ti * 128:(ti + 1) * 128, :], in_=stg[:]), r=[st], w=["ncv"])
            P.dve(lambda e: e.tensor_copy(out=self.nv[:, i, :], in_=ps[:]), r=[pt], w=[f"nv{i}"])
        self.proj_tok(pa, slot, wt, cv_)
        for _ in ws:
            pass

    def finish_nout(self, ti, nout, nt):
        P = self.P
        pb, pbt = self.next_b()
        for c in range(4):
            P.pe(lambda e, c=c: e.transpose(out=pb[:, c, :], in_=nout[:, c * 128:(c + 1) * 128], identity=self.ident[:]),
                 r=[nt, "ident"], w=[pbt])
        P.act(lambda e: e.copy(out=self.noutT[:, :, ti * 128:(ti + 1) * 128], in_=pb[:, 0:4, :]), r=[pbt], w=[f"noutT{ti}"])

    def ctx_attention(self, pa, l):
        P = self.P
        SC = 64.0 ** -0.5
        TPS = pa.L // 128
        def unit(sq, qi):
            ti = sq * TPS + qi
            qtok = slice(ti * 128, (ti + 1) * 128)
            ktok = slice(sq * pa.L, (sq + 1) * pa.L)
            po, pot = self.next_f()
            nout = self.nout_tm[ti % 2]
            nt = f"nout{ti % 2}"
            def head(h):
                ch, hp = h // 2, h % 2
                prt = slice(hp * 64, hp * 64 + 64)
                ps, pt = self.next_f()
                P.pe(lambda e, ps=ps, ch=ch, prt=prt: e.matmul(ps[:, 0:pa.L], lhsT=self.nqT[prt, ch, qtok], rhs=self.nkT[prt, ch, ktok],
                                                               start=True, stop=True),
                     r=[f"nqT{ch}_{ti // 4}"] + [f"nkT_t{k}" for k in range(sq * TPS, (sq + 1) * TPS)], w=[pt])
                a = self.asm
                P.dve(lambda e, ps=ps: e.reduce_max(out=a[:, 0:1], in_=ps[:, 0:pa.L], axis=AX.X), r=[pt], w=["asm"])
                P.dve(lambda e: e.tensor_scalar(out=a[:, 1:2], in0=a[:, 0:1], scalar1=-SC, scalar2=None, op0=ALU.mult), r=["asm"], w=["asm"])
                Pb = self.Pb[h % 2]
                pbk = f"Pb{h % 2}"
                P.act(lambda e, ps=ps, Pb=Pb: e.activation(out=Pb[:, 0:pa.L], in_=ps[:, 0:pa.L], func=AF.Exp, bias=a[:, 1:2], scale=SC,
                                                           accum_out=a[:, 2:3]), r=[pt, "asm"], w=[pbk, "asm"])
                pb, pbt = self.next_b()
                for kt in range(TPS):
                    P.pe(lambda e, kt=kt, Pb=Pb, pb=pb: e.transpose(out=pb[:, kt, :], in_=Pb[:, kt * 128:(kt + 1) * 128], identity=self.ident[:]),
                         r=[pbk, "ident"], w=[pbt])
                PTt = self.PTt[h % 2]
                ptk = f"PTt{h % 2}"
                P.act(lambda e, pb=pb, PTt=PTt: e.copy(out=PTt[:, 0:TPS, :], in_=pb[:, 0:TPS, :]), r=[pbt], w=[ptk])
                for kt in range(TPS):
                    P.pe(lambda e, kt=kt, PTt=PTt, h=h, po=po: e.matmul(po[:, h * 64:(h + 1) * 64], lhsT=PTt[:, kt, :],
                                                                      rhs=self.nv[:, sq * TPS + kt, h * 64:(h + 1) * 64],
                                                                      start=(kt == 0), stop=(kt == TPS - 1)),
                         r=[ptk] + [f"nv{sq * TPS + kt}"], w=[pot])
                P.dve(lambda e: e.reciprocal(out=a[:, 3:4], in_=a[:, 2:3]), r=["asm"], w=["asm"])
                P.dve(lambda e, h=h, po=po, nout=nout: e.tensor_scalar(out=nout[:, h * 64:(h + 1) * 64], in0=po[:, h * 64:(h + 1) * 64],
                                                                      scalar1=a[:, 3:4], scalar2=None, op0=ALU.mult),
                      r=[pot, "asm"], w=[nt])

            for h in range(8):
                head(h)
            self.finish_nout(ti, nout, nt)

        for sq in range(pa.n_seq):
            for qi in range(TPS):
                unit(sq, qi)

    def na_attention(self, pa, l):
        P = self.P
        SC = 64.0 ** -0.5
        for hh in range(2):
            P.dma("pool", lambda e, hh=hh: e.dma_start(out=self.hb[hh][:, :].rearrange("p (t n) -> p t n", t=2),
                                                       in_=self.ck[l, hh * 256:(hh + 1) * 256, :].rearrange("(t p) n -> p t n", p=128)), w=[f"hb{hh}"])
        P.dma("pool", lambda e: e.dma_start(out=self.vctx[:], in_=self.cv[l].rearrange("(t p) n -> p t n", p=128)), w=["vctx"])
        P.dma("pool", lambda e: e.dma_start(out=self.Tpad[:], in_=self.rpbt[l]), w=["Tpad"])
        for t in range(4):
            pb, pbt = self.next_b()
            for c in range(4):
                P.pe(lambda e, c=c, t=t, pb=pb: e.transpose(out=pb[:, c, :], in_=self.hb[t // 2][:, (t % 2) * 512 + c * 128:(t % 2) * 512 + (c + 1) * 128],
                                                            identity=self.ident[:]), r=[f"hb{t // 2}", "ident"], w=[pbt])
            P.act(lambda e, t=t, pb=pb: e.copy(out=self.kctxT[:, :, t * 128:(t + 1) * 128], in_=pb[:, 0:4, :]), r=[pbt], w=["kctxT"])
        saved_psb = self.psb
        sets = [((self.psf[0], "psf0"), (self.psf[1], "psf1"), (self.psf[2], "psf2")),
                ((self.psf[3], "psf3"), (self.psf[4], "psf4"),
                 (saved_psb[2][:, :, :].rearrange("p a b -> p (a b)").bitcast(F32), "psb2"))]
        self.psb = saved_psb[0:2]
        self.nb = 0
        units = [(i, h) for i in range(8) for h in range(8)]
        a = self.asm

        def geom(i):
            ust = NA_UST[i]
            return ust, ust - 2 * i + 7, ust * 64, (ust * 64) // 128

        def stageA(u):
            i, h = units[u]
            ust, base, w0, wt0 = geom(i)
            (bA, tA), (bB, tB), (bC, tC) = sets[u % 2]
            qtok = slice(i * 128, (i + 1) * 128)
            rowm = self.rowm[i % 2]
            rmt = f"rowm{i % 2}"
            if h == 0:
                P.dma("pool", lambda e: e.dma_start(out=rowm[0:2, :], in_=self.C["rowm"][i]), w=[rmt])
            ch, hp = h // 2, h % 2
            prt = slice(hp * 64, hp * 64 + 64)
            qr = [f"nqT{ch}_{i // 4}"]
            kr = [f"nkT{ch}_{t}" for t in range(2)]
            P.pe(lambda e: e.matmul(bA[:, 0:512], lhsT=self.nqT[prt, ch, qtok], rhs=self.nkT[prt, ch, w0:w0 + 512], start=True, stop=False),
                 r=qr + kr, w=[tA])
            P.pe(lambda e: e.matmul(bA[:, 0:512], lhsT=self.ind[0:2, :], rhs=rowm[0:2, 0:512], start=False, stop=True), r=["c_ind", rmt], w=[tA])
            P.pe(lambda e: e.matmul(bB[:, 0:128], lhsT=self.nqT[prt, ch, qtok], rhs=self.nkT[prt, ch, w0 + 512:w0 + 640], start=True, stop=False),
                 r=qr + kr, w=[tB])
            P.pe(lambda e: e.matmul(bB[:, 0:128], lhsT=self.ind[0:2, :], rhs=rowm[0:2, 512:640], start=False, stop=True), r=["c_ind", rmt], w=[tB])
            P.pe(lambda e: e.matmul(bC[:, 0:512], lhsT=self.nqT[prt, ch, qtok], rhs=self.kctxT[prt, ch, :], start=True, stop=True),
                 r=qr + ["kctxT"], w=[tC])

        def stageBC(u):
            i, h = units[u]
            ust, base, w0, wt0 = geom(i)
            (bA, tA), (bB, tB), (bC, tC) = sets[u % 2]
            scl, sct = self.scl[u % 2], f"scl{u % 2}"
            Pb, pbk = self.Pb[u % 2], f"Pb{u % 2}"
            PTt, ptk = self.PTt[u % 2], f"PTt{u % 2}"
            nout, nt = self.nout_tm[i % 2], f"nout{i % 2}"
            as_ = a[:, (u % 2) * 8:(u % 2) * 8 + 8]
            ast = f"asm{u % 2}"
            e0 = base + 2
            P.dve(lambda e: e.scalar_tensor_tensor(out=scl[:, 0:512], in0=bA[:, 0:512], scalar=SC,
                                                   in1=self.Tpad[:, h, e0:e0 + 8, :].rearrange("p a b -> p (a b)"), op0=ALU.mult, op1=ALU.add),
                  r=[tA, "Tpad"], w=[sct])
            P.dve(lambda e: e.scalar_tensor_tensor(out=scl[:, 512:640], in0=bB[:, 0:128], scalar=SC,
                                                   in1=self.Tpad[:, h, e0 + 8:e0 + 10, :].rearrange("p a b -> p (a b)"), op0=ALU.mult, op1=ALU.add),
                  r=[tB, "Tpad", sct], w=[sct])
            P.dve(lambda e: e.reduce_max(out=as_[:, 0:1], in_=scl[:], axis=AX.X), r=[sct], w=[ast])
            P.dve(lambda e: e.reduce_max(out=as_[:, 1:2], in_=bC[:, 0:512], axis=AX.X), r=[tC, ast], w=[ast])
            P.dve(lambda e: e.scalar_tensor_tensor(out=as_[:, 2:3], in0=as_[:, 1:2], scalar=SC, in1=as_[:, 0:1], op0=ALU.mult, op1=ALU.max),
                  r=[ast], w=[ast])
            P.dve(lambda e: e.tensor_scalar(out=as_[:, 3:4], in0=as_[:, 2:3], scalar1=-1.0, scalar2=None, op0=ALU.mult), r=[ast], w=[ast])
            P.act(lambda e: e.activation(out=Pb[:, 0:640], in_=scl[:], func=AF.Exp, bias=as_[:, 3:4], scale=1.0, accum_out=as_[:, 4:5]),
                  r=[sct, ast], w=[pbk, ast])
            P.act(lambda e: e.activation(out=Pb[:, 640:1152], in_=bC[:, 0:512], func=AF.Exp, bias=as_[:, 3:4], scale=SC, accum_out=as_[:, 5:6]),
                  r=[tC, ast, pbk], w=[pbk, ast])
            pb, pbt = self.next_b()
            for kt in range(8):
                P.pe(lambda e, kt=kt: e.transpose(out=pb[:, kt, :], in_=Pb[:, kt * 128:(kt + 1) * 128], identity=self.ident[:]),
                     r=[pbk, "ident"], w=[pbt])
            P.act(lambda e: e.copy(out=PTt[:, 0:8, :], in_=pb[:]), r=[pbt], w=[ptk])
            pb9, pb9t = self.next_b()
            P.pe(lambda e: e.transpose(out=pb9[:, 0, :], in_=Pb[:, 1024:1152], identity=self.ident[:]), r=[pbk, "ident"], w=[pb9t])
            P.dve(lambda e: e.tensor_copy(out=PTt[:, 8, :], in_=pb9[:, 0, :]), r=[pb9t, ptk], w=[ptk])
            po = bB[:, 128:192]
            for kt in range(9):
                if kt < 5:
                    rhs, rt_ = self.nv[:, wt0 + kt, h * 64:(h + 1) * 64], f"nv{wt0 + kt}"
                else:
                    rhs, rt_ = self.vctx[:, kt - 5, h * 64:(h + 1) * 64], "vctx"
                P.pe(lambda e, kt=kt, rhs=rhs: e.matmul(po, lhsT=PTt[:, kt, :], rhs=rhs, start=(kt == 0), stop=(kt == 8)), r=[ptk, rt_], w=[tB])
            P.dve(lambda e: e.tensor_tensor(out=as_[:, 6:7], in0=as_[:, 4:5], in1=as_[:, 5:6], op=ALU.add), r=[ast], w=[ast])
            P.dve(lambda e: e.reciprocal(out=as_[:, 7:8], in_=as_[:, 6:7]), r=[ast], w=[ast])
            P.dve(lambda e: e.tensor_scalar(out=nout[:, h * 64:(h + 1) * 64], in0=po, scalar1=as_[:, 7:8], scalar2=None, op0=ALU.mult),
                  r=[tB, ast], w=[nt])
            if h == 7:
                self.finish_nout(i, nout, nt)

        stageA(0)
        for u in range(len(units)):
            if u + 1 < len(units):
                stageA(u + 1)
            stageBC(u)
        self.psb = saved_psb

    def layer_norm_all(self, pa):
        P = self.P
        NT = pa.NT
        for i in range(NT):
            x, xt = self.xres[i], f"x{i}"
            for k in range(2):
                P.dve(lambda e, k=k, x=x, i=i: e.bn_stats(out=self.lnst[:, i, k, 0:6], in_=x[:, k * 512:(k + 1) * 512]), r=[xt], w=[f"lnst{i}"])
            P.dve(lambda e, i=i: e.bn_aggr(out=self.lnag[:, i, 0:2], in_=self.lnst[:, i, :, 0:6]), r=[f"lnst{i}"], w=[f"lnag{i}"])
        var = self.lnag[:, 0:NT, 1]
        toks = [f"lnag{i}" for i in range(NT)]
        P.act(lambda e: e.activation(out=var, in_=var, func=AF.Ln, bias=self.epsc[:, 0:1], scale=1.0), r=toks + ["epsc"], w=toks)
        P.act(lambda e: e.activation(out=var, in_=var, func=AF.Exp, scale=-0.5), r=toks, w=toks)
        for i in range(NT):
            x, xt = self.xres[i], f"x{i}"
            P.dve(lambda e, x=x, i=i: e.scalar_tensor_tensor(out=x[:], in0=x[:], scalar=self.lnag[:, i, 0:1], in1=self.lng[:],
                                                             op0=ALU.subtract, op1=ALU.mult), r=[xt, f"lnag{i}", "lng"], w=[xt])
            P.dve(lambda e, x=x, i=i: e.scalar_tensor_tensor(out=x[:], in0=x[:], scalar=self.lnag[:, i, 1:2], in1=self.lnb[:],
                                                             op0=ALU.mult, op1=ALU.add), r=[xt, f"lnag{i}", "lnb"], w=[xt])

    def load_ln(self, gname, bname, l):
        P = self.P
        P.dma("sp", lambda e: e.dma_start(out=self.lng[:], in_=self.W[gname][l].partition_broadcast(128)), w=["lng"])
        P.dma("sp", lambda e: e.dma_start(out=self.lnb[:], in_=self.W[bname][l].partition_broadcast(128)), w=["lnb"])

    def residual_out(self, pa, l, slot_of, nk_total, lhs_of, gcol0):
        pass

    def merge(self, pa, l):
        P = self.P
        W = self.W
        av = self.av
        self.mT = av([128, 8, 1024], BF16)
        self.macc = [av([128, 512], F32) for _ in range(4 * pa.NH)]
        self.sig = [av([128, 512], F32) for _ in range(2)]
        self.lng = av([128, 1024], F32)
        self.lnb = av([128, 1024], F32)
        self.lnst = av([128, 8, 2, 8], F32)
        self.lnag = av([128, 8, 8], F32)
        self.load_ln("ln1_g", "ln1_b", l)
        outs = (self.routT, self.soutT, self.noutT)
        otoks = ([f"routT{c}" for c in range(pa.NT)], [f"soutT{c}_{t}" for c in range(4) for t in range(pa.NH)],
                 [f"noutT{c}" for c in range(pa.NT)])
        for jb in range(2):
            for b in range(3):
                specs = [(W["w_in"][l][:, 4096 + b * 1024 + jb * 512:4096 + b * 1024 + (jb + 1) * 512], 8, 512),
                         (W["w_branch"][l, b][:, jb * 512:(jb + 1) * 512], 4, 512)]
                (gs, gt), (bs, bt) = list(self.wstream(specs))

                def unit(dc, th, b=b, jb=jb, gs=gs, gt=gt, bs=bs, bt=bt):
                    cols = slice(th * 512, (th + 1) * 512)
                    pg, pgt = self.next_f()
                    for c in range(8):
                        P.pe(lambda e, c=c: e.matmul(pg[:], lhsT=gs[:, c, dc * 128:(dc + 1) * 128], rhs=self.hT[:, c, cols],
                                                     start=(c == 0), stop=(c == 7)), r=[gt] + self.hT_tokens(pa, th * 512, (th + 1) * 512), w=[pgt])
                    sg = self.sig[(dc + th) % 2]
                    sgt = f"sig{(dc + th) % 2}"
                    P.act(lambda e: e.activation(out=sg[:], in_=pg[:], func=AF.Sigmoid), r=[pgt], w=[sgt])
                    pp, ppt = self.next_f()
                    for c in range(4):
                        P.pe(lambda e, c=c: e.matmul(pp[:], lhsT=bs[:, c, dc * 128:(dc + 1) * 128], rhs=outs[b][:, c, cols],
                                                     start=(c == 0), stop=(c == 3)), r=[bt] + otoks[b], w=[ppt])
                    acc = self.macc[dc * pa.NH + th]
                    at = f"macc{dc * pa.NH + th}"
                    if b == 0:
                        P.dve(lambda e: e.tensor_tensor(out=acc[:], in0=sg[:], in1=pp[:], op=ALU.mult), r=[sgt, ppt], w=[at])
                    else:
                        P.dve(lambda e: e.tensor_tensor(out=sg[:], in0=sg[:], in1=pp[:], op=ALU.mult), r=[sgt, ppt], w=[sgt])
                        if b == 1:
                            P.dve(lambda e: e.tensor_tensor(out=acc[:], in0=acc[:], in1=sg[:], op=ALU.add), r=[sgt, at], w=[at])
                        else:
                            P.dve(lambda e: e.tensor_tensor(out=self.mT[:, jb * 4 + dc, cols], in0=acc[:], in1=sg[:], op=ALU.add),
                                  r=[sgt, at], w=[f"mT{jb * 4 + dc}_{th}"])
                for dc in range(4):
                    for th in range(pa.NH):
                        unit(dc, th)
        mtoks = [f"mT{c}_{t}" for c in range(8) for t in range(pa.NH)]
        specs = [(W["w_o"][l][:, ob * 512:(ob + 1) * 512], 8, 512) for ob in range(2)]
        for ob, (slot, wt) in enumerate(self.wstream(specs)):
            def unit2(i, ob=ob, slot=slot, wt=wt):
                ps, pt = self.next_f()
                for c in range(8):
                    P.pe(lambda e, c=c: e.matmul(ps[:], lhsT=self.mT[:, c, i * 128:(i + 1) * 128], rhs=slot[:, c, :], start=(c == 0), stop=(c == 7)),
                         r=[wt] + mtoks, w=[pt])
                t = self.t1024[0]
                cs = slice(ob * 512, (ob + 1) * 512)
                P.dve(lambda e: e.tensor_tensor(out=t[:, 0:512], in0=ps[:], in1=self.mod[:, 2048 + ob * 512:2048 + (ob + 1) * 512], op=ALU.mult),
                      r=[pt, "mod"], w=["t1024_0"])
                P.dve(lambda e: e.scalar_tensor_tensor(out=self.xres[i][:, cs], in0=self.xres[i][:, cs], scalar=ALPHA, in1=t[:, 0:512],
                                                       op0=ALU.mult, op1=ALU.add), r=["t1024_0", f"x{i}"], w=[f"x{i}"])
            for i in range(pa.NT):
                unit2(i)
        self.layer_norm_all(pa)

    def ffn(self, pa, l, last):
        P = self.P
        W = self.W
        av = self.av
        T, L, NSQ = pa.T, pa.L, pa.n_seq
        self.uT = av([128, 22, 1024], BF16)
        self.za = [av([128, NSQ, L + 2], F32) for _ in range(2)]
        self.zb = [av([128, NSQ, L + 2], F32) for _ in range(2)]
        self.acca2 = [av([128, NSQ, L], F32) for _ in range(2)]
        self.accb2 = [av([128, NSQ, L], F32) for _ in range(2)]
        self.cwr = av([128, 2, 128], F32)
        self.cbr = av([128, 128], F32)
        self.cwT = av([128, 3, 44], F32)
        self.cbT = av([128, 44], F32)
        self.lng = av([128, 1024], F32)
        self.lnb = av([128, 1024], F32)
        self.lnst = av([128, 8, 2, 8], F32)
        self.lnag = av([128, 8, 8], F32)
        self.load_ln("ln2_g", "ln2_b", l)
        cw = W["conv_w"][l].rearrange("k (c p) -> (k c) p", p=128)
        P.dma("sp", lambda e: e.dma_start(out=self.cwr[:, 0, :], in_=cw[0:128, :]), w=["cwr"])
        P.dma("sp", lambda e: e.dma_start(out=self.cwr[0:4, 1, :], in_=cw[128:132, :]), w=["cwr"])
        P.dma("sp", lambda e: e.dma_start(out=self.cbr[0:44, :], in_=W["conv_b"][l].rearrange("(c p) -> c p", p=128)), w=["cbr"])
        ps, pt = self.next_f()
        P.pe(lambda e: e.transpose(out=ps[:, 0:128], in_=self.cwr[:, 0, :], identity=self.ident_f[:]), r=["cwr", "ident_f"], w=[pt])
        P.pe(lambda e: e.transpose(out=ps[:, 128:132], in_=self.cwr[0:4, 1, :], identity=self.ident_f[0:4, 0:4]), r=["cwr", "ident_f"], w=[pt])
        P.pe(lambda e: e.transpose(out=ps[:, 256:300], in_=self.cbr[0:44, :], identity=self.ident_f[0:44, 0:44]), r=["cbr", "ident_f"], w=[pt])
        P.act(lambda e: e.copy(out=self.cwT[:, :, :].rearrange("p k c -> p (k c)"), in_=ps[:, 0:132]), r=[pt], w=["cwT"])
        P.act(lambda e: e.copy(out=self.cbT[:], in_=ps[:, 256:300]), r=[pt], w=["cbT"])
        for zz, nm in ((self.za, "za"), (self.zb, "zb")):
            for k in range(2):
                P.pool(lambda e, zz=zz, k=k: e.memset(zz[k][:], 0.0), w=[f"{nm}{k}"])
        if "cw" in self.debug and l == 0:
            self.dump_view("cwT", self.cwT[:, :, :].rearrange("p k c -> p (k c)"), 132, ["cwT"])
            self.dump_view("cbT", self.cbT[:], 44, ["cbT"])
        w_up = W["w_up"][l]
        nblk = 6
        specs = []
        for m in range(nblk):
            nc_ = 512 if m < 5 else 256
            specs.append((w_up[:, m * 512:m * 512 + nc_], 8, nc_))
            specs.append((w_up[:, 2816 + m * 512:2816 + m * 512 + nc_], 8, nc_))
        ws = self.wstream(specs)
        for m in range(nblk):
            sa, sat = next(ws)
            sbk, sbt = next(ws)
            nq = 4 if m < 5 else 2

            def chunk(q, m=m, sa=sa, sat=sat, sbk=sbk, sbt=sbt):
                k = m * 4 + q
                za, zb = self.za[k % 2], self.zb[k % 2]
                zat, zbt = f"za{k % 2}", f"zb{k % 2}"
                for (slot, wt, z, zt) in ((sa, sat, za, zat), (sbk, sbt, zb, zbt)):
                    for th in range(pa.NH):
                        ps, pt = self.next_f()
                        for c in range(8):
                            P.pe(lambda e, c=c, ps=ps, slot=slot, th=th: e.matmul(ps[:], lhsT=slot[:, c, q * 128:(q + 1) * 128],
                                                                                 rhs=self.hT[:, c, th * 512:(th + 1) * 512],
                                                                                 start=(c == 0), stop=(c == 7)),
                                 r=[wt] + self.hT_tokens(pa, th * 512, (th + 1) * 512), w=[pt])
                        if NSQ == 1:
                            dst = z[:, 0, 1 + th * 512:1 + (th + 1) * 512]
                            src = ps[:]
                        else:
                            dst = z[:, :, 1:L + 1]
                            src = ps[:].rearrange("p (s t) -> p s t", s=NSQ)
                        P.act(lambda e, dst=dst, src=src: e.copy(out=dst, in_=src), r=[pt], w=[zt])
                acca, accb = self.acca2[k % 2], self.accb2[k % 2]
                aat, abt = f"acca{k % 2}", f"accb{k % 2}"
                for (z, zt, acc, at, kk) in ((za, zat, acca, aat, k), (zb, zbt, accb, abt, 22 + k)):
                    P.act(lambda e, z=z, acc=acc, kk=kk: e.activation(out=acc[:], in_=z[:, :, 0:L], func=AF.Identity,
                                                                      scale=self.cwT[:, 0, kk:kk + 1], bias=self.cbT[:, kk:kk + 1]),
                          r=[zt, "cwT", "cbT"], w=[at])
                    for tap in (1, 2):
                        P.dve(lambda e, z=z, acc=acc, kk=kk, tap=tap: e.scalar_tensor_tensor(
                            out=acc[:], in0=z[:, :, tap:L + tap], scalar=self.cwT[:, tap, kk:kk + 1], in1=acc[:], op0=ALU.mult, op1=ALU.add),
                            r=[zt, "cwT", at], w=[at])
                if "zdbg" in self.debug and l == 0 and k == 0:
                    f2 = lambda v: v.rearrange("p a b -> p (a b)")
                    self.dump_view("za", f2(za[:, :, :]), NSQ * (L + 2), [zat])
                    self.dump_view("zb", f2(zb[:, :, :]), NSQ * (L + 2), [zbt])
                    self.dump_view("acca", f2(acca[:, :, :]), NSQ * L, [aat])
                    self.dump_view("accb", f2(accb[:, :, :]), NSQ * L, [abt])
                P.act(lambda e: e.activation(out=acca[:], in_=acca[:], func=AF.Gelu_apprx_tanh), r=[aat], w=[aat])
                P.dve(lambda e, k=k: e.tensor_tensor(out=self.uT[:, k, 0:T].rearrange("p (s t) -> p s t", s=NSQ), in0=acca[:], in1=accb[:],
                                                     op=ALU.mult), r=[aat, abt], w=[f"uT{k}"])
            for q in range(nq):
                chunk(q)
        for _ in ws:
            pass
        utoks = [f"uT{k}" for k in range(22)]
        if "uT" in self.debug and l == 0:
            self.dump_T(f"{pa.kind}_uT", self.uT, 4, pa.T, utoks)
            self.dump_T(f"{pa.kind}_uTb", self.uT[:, 18:22, :], 4, pa.T, utoks)
        for oh in range(2):
            parts = []
            for part, nk_ in ((0, 8), (1, 8), (2, 6)):
                parts.append(self.wload(W["w_down"][l][part * 1024:part * 1024 + nk_ * 128, oh * 512:(oh + 1) * 512], nk_, 512))

            def unit(i, oh=oh, parts=parts):
                ps, pt = self.next_f()
                for k in range(22):
                    slot, wt = parts[k // 8]
                    P.pe(lambda e, k=k, slot=slot: e.matmul(ps[:], lhsT=self.uT[:, k, i * 128:(i + 1) * 128], rhs=slot[:, k % 8, :],
                                                            start=(k == 0), stop=(k == 21)), r=[wt] + utoks, w=[pt])
                t = self.t1024[0]
                cs = slice(oh * 512, (oh + 1) * 512)
                P.dve(lambda e: e.tensor_tensor(out=t[:, 0:512], in0=ps[:], in1=self.mod[:, 2048 + oh * 512:2048 + (oh + 1) * 512], op=ALU.mult),
                      r=[pt, "mod"], w=["t1024_0"])
                P.dve(lambda e: e.scalar_tensor_tensor(out=self.xres[i][:, cs], in0=self.xres[i][:, cs], scalar=ALPHA, in1=t[:, 0:512],
                                                       op0=ALU.mult, op1=ALU.add), r=["t1024_0", f"x{i}"], w=[f"x{i}"])
            for i in range(pa.NT):
                unit(i)
        self.layer_norm_all(pa)
        for i in range(pa.NT):
            if last:
                P.dma("sp", lambda e, i=i: e.dma_start(out=self.yout[pa.kind][i * 128:(i + 1) * 128, :], in_=self.xres[i][:]),
                      r=[f"x{i}"], w=["yout"])

    def rstd(self, ap, tok):
        P = self.P
        P.act(lambda e: e.activation(out=ap, in_=ap, func=AF.Ln, bias=self.epsc[:, 0:1], scale=1.0), r=[tok, "epsc"], w=[tok])
        P.act(lambda e: e.activation(out=ap, in_=ap, func=AF.Exp, scale=-0.5), r=[tok], w=[tok])

    def dump_T(self, name, src, nchunk, T, rtoks):
        P = self.P
        out = self.dbg_out(name, [nchunk * 128, T])
        for c in range(nchunk):
            for th in range(T // 512):
                t = self.t1024[c % 2]
                P.dve(lambda e, c=c, th=th, t=t: e.tensor_copy(out=t[:, 0:512], in_=src[:, c, th * 512:(th + 1) * 512]),
                      r=rtoks, w=["t1024_0"])
                P.dma("sp", lambda e, c=c, th=th, t=t: e.dma_start(out=out[c * 128:(c + 1) * 128, th * 512:(th + 1) * 512],
                                                                   in_=t[:, 0:512]), r=["t1024_0"], w=["dbg"])

    def run_path(self, pa):
        P = self.P
        for i in range(pa.NT):
            P.dma("sp", lambda e, i=i: e.dma_start(out=self.xres[i][:], in_=self.xin[pa.kind][i * 128:(i + 1) * 128, :]),
                  w=[f"x{i}"])
        for l in range(DEPTH):
            self.ada_half(pa, l, 0)
            if "ada" in self.debug and l == 0:
                o = self.dbg_out(f"{pa.kind}_ada0", [128, 3072])
                P.dma("sp", lambda e, o=o: e.dma_start(out=o, in_=self.mod[:]), r=["mod"], w=["dbg"])
            if self.stop_after == "ada":
                return
            self.modulate_T(pa)
            if self.stop_after == "mod":
                if "hT" in self.debug and l == 0:
                    self.dump_T(f"{pa.kind}_hT", self.hT, 8, pa.T, [f"hT{i}" for i in range(pa.NT)])
                return
            if "hT" in self.debug and l == 0:
                self.dump_T(f"{pa.kind}_hT", self.hT, 8, pa.T, [f"hT{i}" for i in range(pa.NT)])
            self.ret_tables(l)
            if self.stop_after == "rettab":
                return
            self.prefetch([(self.W["w_in"][l][:, b * 512:(b + 1) * 512], 8, 512) for b in range(2)])
            self.phase()
            self.alloc_ret()
            self.retention(pa, l)
            if self.stop_after in ("retproj", "retA"):
                if "ret" in self.debug:
                    self.dump_T(f"{pa.kind}_rq", self.rqT, 4, pa.T, [f"rqT{h}_{t}" for h in range(4) for t in range(pa.NH)])
                return
            if "ret" in self.debug and l == 0:
                self.dump_T(f"{pa.kind}_rq", self.rqT, 4, pa.T, [f"rqT{h}_{t}" for h in range(4) for t in range(pa.NH)])
                self.dump_T(f"{pa.kind}_rout", self.routT, 4, pa.T, [f"routT{c}" for c in range(pa.NT)])
            if self.stop_after == "ret":
                return
            self.prefetch([(self.W["w_in"][l][:, 2048:2560], 8, 512)])
            self.phase()
            self.alloc_s5(pa)
            self.s5(pa, l)
            if self.stop_after in ("s5u", "s5a", "s5b", "s5c", "s5d", "s5e", "s5f", "s5g"):
                return
            if "s5" in self.debug and l == 0:
                self.dump_T(f"{pa.kind}_sout", self.soutT, 4, pa.T, [f"soutT{c}_{t}" for c in range(4) for t in range(pa.NH)])
            if self.stop_after == "s5":
                return
            self.prefetch([(self.W["w_in"][l][:, 2560 + b * 512:2560 + (b + 1) * 512], 8, 512) for b in range(2)])
            self.phase()
            self.alloc_attn(pa)
            self.attn_proj(pa, l)
            if pa.kind == "p":
                self.ctx_attention(pa, l)
            else:
                self.na_attention(pa, l)
            if "attn" in self.debug and l == 0:
                self.dump_T(f"{pa.kind}_nout", self.noutT, 4, pa.T, [f"noutT{c}" for c in range(pa.NT)])
            if self.stop_after == "attn":
                return
            self.prefetch([(self.W["w_in"][l][:, 4096:4608], 8, 512), (self.W["w_branch"][l, 0][:, 0:512], 4, 512)])
            self.phase()
            self.merge(pa, l)
            if "x1" in self.debug and l == 0:
                self.dump_x(pa, "x1")
            if self.stop_after == "merge":
                return
            self.ada_half(pa, l, 1)
            if "ada2" in self.debug and l == 0:
                o = self.dbg_out(f"{pa.kind}_ada1", [128, 3072])
                P.dma("sp", lambda e, o=o: e.dma_start(out=o, in_=self.mod[:]), r=["mod"], w=["dbg"])
            self.modulate_T(pa)
            if "ada2" in self.debug and l == 0:
                self.dump_T(f"{pa.kind}_h2T", self.hT, 8, pa.T, [f"hT{i}" for i in range(pa.NT)])
            self.prefetch([(self.W["w_up"][l][:, 0:512], 8, 512), (self.W["w_up"][l][:, 2816:2816 + 512], 8, 512)])
            self.phase(at=0)
            self.ffn(pa, l, last=(l == DEPTH - 1))
            if l + 1 < DEPTH and not (pa.kind == "s" and "noadashare" not in self.debug):
                self.prefetch([(self.W["w_ada"][l + 1, :, b * 512:(b + 1) * 512], 8, 512) for b in range(2)])
            self.phase(at=0)
            self.aoff = self.amark
            if "x2" in self.debug and l == 0:
                self.dump_x(pa, "x2")
            if self.stop_after == "ffn":
                return

    def build(self, paths=("p", "s")):
        self.load_consts()
        for k in paths:
            self.run_path(Path(k))
        self.P.emit()
        return self.nc


def make_in_maps(inputs):
    consts = _consts()
    maps = []
    f = lambda a: np.ascontiguousarray(a, dtype=np.float32)
    shared = {k: f(inputs[k]) for k in W_SHAPES}
    for k, v in consts.items():
        shared["c_" + k] = f(v)
    rpbt = np.stack([_rpb_table(f(inputs["na_rpb"][l])) for l in range(2)], 0)
    for c in range(N_CORES):
        b = c % 4
        m = dict(shared)
        m["xin_p"] = f(inputs["x_prompt"][2 * c:2 * c + 2].reshape(512, 1024))
        m["xin_s"] = f(inputs["x_sample"][b])
        m["cond_p"] = f(inputs["c_ctx"])
        m["cond_s"] = f(inputs["c"][b])
        m["sret"] = f(inputs["state_ret"][b])
        m["sssm"] = f(inputs["state_ssm"][b])
        m["ck"] = f(inputs["cache_na_k"][b].reshape(2, 512, 512))
        m["cv"] = f(inputs["cache_na_v"][b].reshape(2, 512, 512))
        m["rpbt"] = rpbt
        maps.append(m)
    return maps


def kernel(**inputs):
    b = Builder()
    nc = b.build()
    maps = make_in_maps(inputs)
    res = run_bass_kernel_spmd(nc, maps, core_ids=list(range(N_CORES)))
    R = res.results
    yp = np.concatenate([R[c]["yp"].reshape(2, 256, 1024) for c in range(8)], 0)
    ys = np.stack([R[c]["ys"] for c in range(4)], 0)
    nsr = np.concatenate([R[c]["nsr"] for c in range(8)], 0)
    nss = np.concatenate([R[c]["nss"] for c in range(8)], 0)
    nck = np.concatenate([R[c]["nck"] for c in range(8)], 0).reshape(16, 2, 256, 8, 64)
    ncv = np.concatenate([R[c]["ncv"] for c in range(8)], 0).reshape(16, 2, 256, 8, 64)
    return (yp.astype(np.float32), ys.astype(np.float32), nsr.astype(np.float32), nss.astype(np.float32),
            nck.astype(np.float32), ncv.astype(np.float32))
```
